# Optimizing a Trainium2 kernel written in Bass

```python
import jax, jax.numpy as jnp
from jax import lax
import numpy as np

D_MODEL = 1024
BATCH = 2
SEQ = 8192
DEPTH = 1
DEC_BATCH = 32
DEC_SEQ = 1
PAST_LEN = 16384
PAGE_SIZE = 128

N_HEADS = 8
HEAD_DIM = 64
ATTN_WIDTH = N_HEADS * HEAD_DIM
MOBA_BLOCK = 256
MOBA_TOP_K = 3
Q_CHUNK = 64
ATTN_SCALE = HEAD_DIM ** -0.5
POOL_WINDOWS = (2, 4, 8, 16)
N_POOL_GROUPS = len(POOL_WINDOWS)
POOL_WIDTH = D_MODEL // 2
POOL_GROUP = POOL_WIDTH // N_POOL_GROUPS
POOL_HIST = max(POOL_WINDOWS) - 1
D_FF = 4 * D_MODEL
PLE_DIM = 256
IN_WIDTH = 3 * ATTN_WIDTH + POOL_WIDTH + 2 * D_MODEL
SPLIT_AT = [ATTN_WIDTH, 2 * ATTN_WIDTH, 3 * ATTN_WIDTH, 3 * ATTN_WIDTH + POOL_WIDTH, 3 * ATTN_WIDTH + POOL_WIDTH + D_MODEL]
EPS = 1e-6
NEG_INF = -1e30

kernel_name = 'moba_pool_hybrid_step'


def rms_norm(x, g):
    xf = x.astype(jnp.float32)
    y = xf * lax.rsqrt(jnp.mean(xf * xf, axis=-1, keepdims=True) + EPS) * g.astype(jnp.float32)
    return y.astype(x.dtype)


def mixer_inputs(x, ln, w_in, q_norm, k_norm):
    B, S, _ = x.shape
    z = rms_norm(x, ln) @ w_in
    q, k, v, u, ga, gb = jnp.split(z, SPLIT_AT, axis=-1)
    heads = lambda t: t.reshape(B, S, N_HEADS, HEAD_DIM)
    return rms_norm(heads(q), q_norm), rms_norm(heads(k), k_norm), heads(v), u, ga, gb


def moba_attention(q, k, v, q_pos):
    B, S, H, hd = q.shape
    L = k.shape[1]
    nb = -(-L // MOBA_BLOCK)
    pad = nb * MOBA_BLOCK - L
    blocks = lambda t: jnp.pad(t, ((0, 0), (0, pad), (0, 0), (0, 0))).reshape(B, nb, MOBA_BLOCK, H, hd).transpose(0, 3, 1, 2, 4)
    kb, vb = blocks(k), blocks(v)
    k_mean = jnp.mean(kb.astype(jnp.float32), axis=3)
    top = min(MOBA_TOP_K, nb)
    qc = Q_CHUNK if S % Q_CHUNK == 0 else S
    n_chunks = S // qc
    q_chunks = q.reshape(B, n_chunks, qc, H, hd).transpose(1, 0, 3, 2, 4)
    pos_chunks = q_pos.reshape(n_chunks, qc)
    b_idx = jnp.arange(B)[:, None, None, None]
    h_idx = jnp.arange(H)[None, :, None, None]
    blk_range = jnp.arange(nb)
    in_block = jnp.arange(MOBA_BLOCK)

    def chunk(args):
        qq, pp = args
        own = pp // MOBA_BLOCK
        gate = jnp.einsum('bhqd,bhnd->bhqn', qq.astype(jnp.float32), k_mean)
        gate = jnp.where(blk_range[None, :] < own[:, None], gate, NEG_INF)
        _, idx = lax.top_k(gate, top)
        own_b = jnp.broadcast_to(own[None, None, :, None], (B, H, qc, 1))
        sel = jnp.concatenate([idx, own_b], axis=-1)
        slot_ok = jnp.concatenate([idx < own[:, None], jnp.ones((B, H, qc, 1), bool)], axis=-1)
        kg = kb[b_idx, h_idx, sel]
        vg = vb[b_idx, h_idx, sel]
        key_pos = sel[..., None] * MOBA_BLOCK + in_block
        mask = slot_ok[..., None] & (key_pos <= pp[:, None, None])
        s = jnp.einsum('bhqd,bhqnkd->bhqnk', qq, kg, preferred_element_type=jnp.float32) * ATTN_SCALE
        s = jnp.where(mask, s, NEG_INF).reshape(B, H, qc, (top + 1) * MOBA_BLOCK)
        w = jax.nn.softmax(s, axis=-1).astype(vg.dtype).reshape(B, H, qc, top + 1, MOBA_BLOCK)
        return jnp.einsum('bhqnk,bhqnkd->bhqd', w, vg)

    out = lax.map(chunk, (q_chunks, pos_chunks))
    return out.transpose(1, 0, 3, 2, 4).reshape(B, S, H * hd)


def pool_branch(u, hist, pos0, pool_w, pool_scale):
    B, S, C = u.shape
    e = jnp.concatenate([hist, u], axis=1)
    ef = e.astype(jnp.float32)
    cs = jnp.concatenate([jnp.zeros((B, 1, C), jnp.float32), jnp.cumsum(ef, axis=1)], axis=1)
    pos = pos0 + jnp.arange(S)
    means = []
    for g, w in enumerate(POOL_WINDOWS):
        c0, c1 = g * POOL_GROUP, (g + 1) * POOL_GROUP
        win = cs[:, POOL_HIST + 1:, c0:c1] - cs[:, POOL_HIST + 1 - w:POOL_HIST + 1 - w + S, c0:c1]
        cnt = jnp.minimum(w, pos + 1).astype(jnp.float32)[None, :, None]
        means.append(win / cnt)
    d = (jnp.concatenate(means, axis=-1) - ef[:, POOL_HIST:]).astype(u.dtype)
    d = d.reshape(B, S, N_POOL_GROUPS, POOL_GROUP)
    y = jnp.einsum('bsgc,gcd->bsgd', d, pool_w).reshape(B, S, C) * pool_scale
    return y, e[:, -POOL_HIST:]


def merge_and_channel_mix(x, a, b, ga, gb, p, w_attn_out, w_pool_out, w_out, ln_mlp, w_up, w_down, ln_ple, w_ple_gate, w_ple_proj):
    m = jax.nn.sigmoid(ga) * (a @ w_attn_out) + jax.nn.sigmoid(gb) * (b @ w_pool_out)
    x = x + m @ w_out
    hid = jax.nn.relu(rms_norm(x, ln_mlp) @ w_up)
    x = x + (hid * hid) @ w_down
    gate = jax.nn.sigmoid(rms_norm(x, ln_ple) @ w_ple_gate)
    return x + gate * (p @ w_ple_proj)


def setup_inputs(seed: int = 0) -> dict:
    key = jax.random.key(seed)
    ks = jax.random.split(key, 24)
    f32 = jnp.float32
    n_pages = PAST_LEN // PAGE_SIZE
    n_used = DEC_BATCH * n_pages
    n_phys = n_used + max(1, n_used // 4)
    nrm = lambda k, shape, scale=1.0: jax.random.normal(k, shape, f32) * scale
    gain = lambda k, shape: 1.0 + 0.05 * jax.random.normal(k, shape, f32)
    page_table = jax.random.permutation(ks[5], n_phys)[:n_used].reshape(DEC_BATCH, n_pages).astype(jnp.int32)
    return {
        'x_prompt': nrm(ks[0], (BATCH, SEQ, D_MODEL)),
        'x_sample': nrm(ks[1], (DEC_BATCH, DEC_SEQ, D_MODEL)),
        'cache_k': nrm(ks[2], (DEPTH, n_phys, PAGE_SIZE, N_HEADS, HEAD_DIM)),
        'cache_v': nrm(ks[3], (DEPTH, n_phys, PAGE_SIZE, N_HEADS, HEAD_DIM)),
        'state_pool': nrm(ks[4], (DEPTH, DEC_BATCH, POOL_HIST, POOL_WIDTH)),
        'page_table': page_table,
        'p_prompt': nrm(ks[6], (DEPTH, BATCH, SEQ, PLE_DIM)),
        'p_sample': nrm(ks[7], (DEPTH, DEC_BATCH, DEC_SEQ, PLE_DIM)),
        'ln_mix': gain(ks[8], (DEPTH, D_MODEL)),
        'w_in': nrm(ks[9], (DEPTH, D_MODEL, IN_WIDTH), D_MODEL ** -0.5),
        'q_norm': gain(ks[10], (DEPTH, HEAD_DIM)),
        'k_norm': gain(ks[11], (DEPTH, HEAD_DIM)),
        'pool_w': nrm(ks[12], (DEPTH, N_POOL_GROUPS, POOL_GROUP, POOL_GROUP), POOL_GROUP ** -0.5),
        'pool_scale': gain(ks[13], (DEPTH, POOL_WIDTH)),
        'w_attn_out': nrm(ks[14], (DEPTH, ATTN_WIDTH, D_MODEL), ATTN_WIDTH ** -0.5),
        'w_pool_out': nrm(ks[15], (DEPTH, POOL_WIDTH, D_MODEL), POOL_WIDTH ** -0.5),
        'w_out': nrm(ks[16], (DEPTH, D_MODEL, D_MODEL), D_MODEL ** -0.5),
        'ln_mlp': gain(ks[17], (DEPTH, D_MODEL)),
        'w_up': nrm(ks[18], (DEPTH, D_MODEL, D_FF), D_MODEL ** -0.5),
        'w_down': nrm(ks[19], (DEPTH, D_FF, D_MODEL), D_FF ** -0.5),
        'ln_ple': gain(ks[20], (DEPTH, D_MODEL)),
        'w_ple_gate': nrm(ks[21], (DEPTH, D_MODEL, D_MODEL), D_MODEL ** -0.5),
        'w_ple_proj': nrm(ks[22], (DEPTH, PLE_DIM, D_MODEL), PLE_DIM ** -0.5),
    }


def reference(x_prompt, x_sample, cache_k, cache_v, state_pool, page_table, p_prompt, p_sample, ln_mix, w_in, q_norm, k_norm, pool_w, pool_scale, w_attn_out, w_pool_out, w_out, ln_mlp, w_up, w_down, ln_ple, w_ple_gate, w_ple_proj):
    n_seq, n_pages = page_table.shape
    past_len = n_pages * PAGE_SIZE
    bp, sp, _ = x_prompt.shape
    ss = x_sample.shape[1]
    xp, xs = x_prompt, x_sample
    kp_l, vp_l, hp_l, ks_l, vs_l, hs_l = [], [], [], [], [], []
    for l in range(DEPTH):
        tail = (w_attn_out[l], w_pool_out[l], w_out[l], ln_mlp[l], w_up[l], w_down[l], ln_ple[l], w_ple_gate[l], w_ple_proj[l])
        q, k, v, u, ga, gb = mixer_inputs(xp, ln_mix[l], w_in[l], q_norm[l], k_norm[l])
        a = moba_attention(q, k, v, jnp.arange(sp))
        b, hist_p = pool_branch(u, jnp.zeros((bp, POOL_HIST, POOL_WIDTH), u.dtype), 0, pool_w[l], pool_scale[l])
        xp = merge_and_channel_mix(xp, a, b, ga, gb, p_prompt[l], *tail)
        kp_l.append(k); vp_l.append(v); hp_l.append(hist_p)
        q, k, v, u, ga, gb = mixer_inputs(xs, ln_mix[l], w_in[l], q_norm[l], k_norm[l])
        k_past = cache_k[l, page_table].reshape(n_seq, past_len, N_HEADS, HEAD_DIM)
        v_past = cache_v[l, page_table].reshape(n_seq, past_len, N_HEADS, HEAD_DIM)
        a = moba_attention(q, jnp.concatenate([k_past, k], axis=1), jnp.concatenate([v_past, v], axis=1), past_len + jnp.arange(ss))
        b, hist_s = pool_branch(u, state_pool[l], past_len, pool_w[l], pool_scale[l])
        xs = merge_and_channel_mix(xs, a, b, ga, gb, p_sample[l], *tail)
        ks_l.append(k); vs_l.append(v); hs_l.append(hist_s)
    return (xp, xs, jnp.stack(kp_l), jnp.stack(vp_l), jnp.stack(hp_l), jnp.stack(ks_l), jnp.stack(vs_l), jnp.stack(hs_l))
```

```python
import numpy as np
from contextlib import ExitStack
import concourse.bass as bass
import concourse.mybir as mybir
from concourse.bass_utils import run_bass_kernel_spmd

F32 = mybir.dt.float32
BF16 = mybir.dt.bfloat16
I32 = mybir.dt.int32
ALU = mybir.AluOpType
AF = mybir.ActivationFunctionType
AX = mybir.AxisListType

D = 1024
KC = 8
H = 8
HD = 64
UT = 512
NSLOT = 16
SEQ = 8192
EPS = 1e-6
NEG = -30000.0
N_PHYS = 5120
CFG = {"n_phys": 5120, "stop": None, "nslots": 16, "pieces": True}
ORDER = ["p0", "A", "samp", "uS", "u0", "u1", "u2", "u3"]


def _on(name):
    st = CFG.get("stop")
    return True if st is None else ORDER.index(name) <= ORDER.index(st)
NSEQ = 4

ENGS = ["sync", "scalar", "vector", "gpsimd", "tensor"]
N_DMA_SLOTS = {"sync": 16, "gpsimd": 8}


class Buf:
    __slots__ = ("last_write", "readers", "excl")

    def __init__(self):
        self.last_write = None
        self.readers = {}
        self.excl = False


class Sched:
    def __init__(self, nc, stack):
        self.nc = nc
        self.q = {e: [] for e in ENGS}
        self.sem = {}
        for e in ["scalar", "vector", "gpsimd", "tensor"]:
            self.sem[e] = stack.enter_context(nc.semaphore("p_" + e))
        self.cnt = {e: 0 for e in self.sem}
        self.dsem = {}
        self.dcnt = {}
        self.dnext = {}
        for e, n in N_DMA_SLOTS.items():
            for i in range(n):
                self.dsem[(e, i)] = stack.enter_context(nc.semaphore(f"d_{e}{i}"))
                self.dcnt[(e, i)] = 0
            self.dnext[e] = 0
        self.waited = {e: {} for e in ENGS}

    def _semobj(self, key):
        return self.sem[key] if key in self.sem else self.dsem[key]

    def op(self, eng, fn, reads=(), writes=(), dma=False):
        deps = {}
        ex = [b for b in reads if b.excl]
        if ex:
            reads = [b for b in reads if not b.excl]
            writes = list(writes) + [b for b in ex if b not in writes]

        def add(h):
            if h is None:
                return
            k, v = h
            if deps.get(k, 0) < v:
                deps[k] = v

        for b in reads:
            add(b.last_write)
        for b in writes:
            add(b.last_write)
            for k, v in b.readers.items():
                add((k, v))
        if dma:
            slot = self.dnext[eng]
            self.dnext[eng] = (slot + 1) % N_DMA_SLOTS[eng]
            key = (eng, slot)
            if self.dcnt[key] > 0:
                add((key, self.dcnt[key]))
            self.dcnt[key] += 16
            h = (key, self.dcnt[key])
            inc = 16
        else:
            key = eng
            self.cnt[eng] += 1
            h = (key, self.cnt[eng])
            inc = 1
        waits = []
        w = self.waited[eng]
        for k, v in deps.items():
            if k == "tensor" and eng == "tensor":
                continue
            if w.get(k, 0) >= v:
                continue
            w[k] = v
            waits.append((self._semobj(k), v))
        semo = self._semobj(key)

        def emit(e, waits=waits, fn=fn, semo=semo, inc=inc):
            for s, v in waits:
                e.wait_ge(s, v)
            ins = fn(e)
            ins.then_inc(semo, inc)

        self.q[eng].append(emit)
        for b in writes:
            b.last_write = h
            b.readers = {}
        for b in reads:
            if b.readers.get(h[0], 0) < h[1]:
                b.readers[h[0]] = h[1]
        return h

    def barrier(self):
        targets = [(k, v) for k, v in self.cnt.items() if v > 0]
        targets += [(k, v) for k, v in self.dcnt.items() if v > 0]
        for eng in ENGS:
            w = self.waited[eng]
            waits = []
            for k, v in targets:
                if w.get(k, 0) >= v:
                    continue
                w[k] = v
                waits.append((self._semobj(k), v))

            def emit(e, waits=waits):
                for s, v in waits:
                    e.wait_ge(s, v)

            self.q[eng].append(emit)

    def run_block(self):
        nc = self.nc
        q = self.q
        with nc.Block() as block:
            @block.sync
            def _(e):
                for c in q["sync"]:
                    c(e)

            @block.scalar
            def _(e):
                for c in q["scalar"]:
                    c(e)

            @block.vector
            def _(e):
                for c in q["vector"]:
                    c(e)

            @block.gpsimd
            def _(e):
                for c in q["gpsimd"]:
                    c(e)

            @block.tensor
            def _(e):
                for c in q["tensor"]:
                    c(e)
        self.q = {e: [] for e in ENGS}


class TB:
    def __init__(self, t, nb=1):
        self.t = t
        self.bs = [Buf() for _ in range(nb)]

    @property
    def b(self):
        return self.bs[0]


def DMA(out, in_):
    return lambda e: e.dma_start(out=out, in_=in_)


def IDMA(out, in_, idx):
    return lambda e: e.indirect_dma_start(out=out, out_offset=None, in_=in_,
                                          in_offset=bass.IndirectOffsetOnAxis(ap=idx, axis=0))


def MMG(lst):
    def f(e):
        r = None
        for (ps, lhsT, rhs, st, sp) in lst:
            r = e.matmul(ps, lhsT=lhsT, rhs=rhs, start=st, stop=sp)
        return r
    return f


def TRG(lst):
    def f(e):
        r = None
        for (out, in_, ident) in lst:
            r = e.transpose(out=out, in_=in_, identity=ident)
        return r
    return f


def ACTF(out, in_, func, scale=None, accum_out=None):
    kw = {}
    if scale is not None:
        kw["scale"] = scale
    if accum_out is not None:
        kw["accum_out"] = accum_out
    return lambda e: e.activation(out=out, in_=in_, func=func, **kw)


def TT(out, in0, in1, op):
    return lambda e: e.tensor_tensor(out=out, in0=in0, in1=in1, op=op)


def TS(out, in0, s1, s2, op0, op1=None):
    if op1 is None:
        return lambda e: e.tensor_scalar(out=out, in0=in0, scalar1=s1, scalar2=None, op0=op0)
    return lambda e: e.tensor_scalar(out=out, in0=in0, scalar1=s1, scalar2=s2, op0=op0, op1=op1)


def STT(out, in0, scalar, in1, op0, op1):
    return lambda e: e.scalar_tensor_tensor(out=out, in0=in0, scalar=scalar, in1=in1, op0=op0, op1=op1)


def RED(out, in_, op=ALU.add):
    return lambda e: e.tensor_reduce(out=out, in_=in_, axis=AX.X, op=op)


def CP(out, in_):
    return lambda e: e.tensor_copy(out=out, in_=in_)


def RCP(out, in_):
    return lambda e: e.reciprocal(out=out, in_=in_)


def MAX8(out, in_):
    return lambda e: e.max(out=out, in_=in_)


def MEMSET(ap, v):
    return lambda e: e.memset(ap, v)


def build_nc():
    nc = bass.Bass("TRN2", target_bir_lowering=False)
    uid = [0]

    def din(name, shape, dt=F32):
        return nc.dram_tensor(name, list(shape), dt, kind="ExternalInput").ap()

    def dout(name, shape, dt=F32):
        return nc.dram_tensor(name, list(shape), dt, kind="ExternalOutput").ap()

    def dscr(name, shape, dt):
        return nc.dram_tensor(name, list(shape), dt).ap()

    xb = din("xb", [SEQ, D])
    xhalo = din("xhalo", [64, D])
    p_own = din("p_own", [4 * UT, 256])
    xs_d = din("xs", [NSEQ, D])
    ps_d = din("ps", [NSEQ, 256])
    ptT_d = din("ptT", [128, NSEQ], I32)
    pt_d = din("pt", [NSEQ, 128], I32)
    state_d = din("state", [NSEQ, 15, 512])
    NP_ = CFG["n_phys"]
    ck_d = din("cache_k", [NP_ * 1024, HD])
    cv_d = din("cache_v", [NP_ * 1024, HD])
    ln_mix_d = din("ln_mix", [D])
    w_in_d = din("w_in", [D, 4096])
    q_norm_d = din("q_norm", [HD])
    k_norm_d = din("k_norm", [HD])
    pool_w_d = din("pool_w", [4, 128, 128])
    pool_scale_d = din("pool_scale", [512])
    w_ao_d = din("w_attn_out", [512, D])
    w_po_d = din("w_pool_out", [512, D])
    w_out_d = din("w_out", [D, D])
    ln_mlp_d = din("ln_mlp", [D])
    w_up_d = din("w_up", [D, 4096])
    w_down_d = din("w_down", [4096, D])
    ln_ple_d = din("ln_ple", [D])
    w_pg_d = din("w_ple_gate", [D, D])
    w_pp_d = din("w_ple_proj", [256, D])
    cm_d = din("cm", [128, 4 * UT])
    ind_d = din("ind", [NSLOT * 32, 4 * UT])
    gmask_d = din("gmask", [128, 256])
    pastind_d = din("pastind", [128, 256])
    ownind_d = din("ownind", [128, 256])
    corr_d = din("corr", [128, 64])
    pair_d = din("pairm", [128, 64])
    tokoff_d = din("tokoff", [128, 48])
    delta_d = din("delta", [8, 48])
    pmmask_d = din("pmmask", [4, 32])
    selw_d = din("selw", [15, 4])

    y_own = dout("y_own", [4 * UT, D])
    k_own = dout("k_own", [4 * UT, 512])
    v_own = dout("v_own", [4 * UT, 512])
    pool_o = dout("pool_o", [15, 512])
    ys_o = dout("ys_o", [NSEQ, D])
    ks_o = dout("ks_o", [NSEQ, 512])
    vs_o = dout("vs_o", [NSEQ, 512])
    pools_o = dout("pools_o", [NSEQ, 15, 512])

    wbf = {
        "in": dscr("wbf_in", [D, 4096], BF16),
        "ao": dscr("wbf_ao", [512, D], BF16),
        "po": dscr("wbf_po", [512, D], BF16),
        "out": dscr("wbf_out", [D, D], BF16),
        "up": dscr("wbf_up", [D, 4096], BF16),
        "down": dscr("wbf_down", [4096, D], BF16),
        "pg": dscr("wbf_pg", [D, D], BF16),
        "pp": dscr("wbf_pp", [256, D], BF16),
    }
    wsrc = {"in": w_in_d, "ao": w_ao_d, "po": w_po_d, "out": w_out_d, "up": w_up_d,
            "down": w_down_d, "pg": w_pg_d, "pp": w_pp_d}
    ind_bf = dscr("ind_bf", [NSLOT * 32, 4 * UT], BF16)
    kT_s = dscr("kT_s", [H * HD, SEQ], BF16)
    v_s = dscr("v_s", [SEQ, H * 128], BF16)

    with ExitStack() as top:
        S = Sched(nc, top)

        def mk(stack):
            def sb(shape, dt=F32, nb=1, name="t"):
                uid[0] += 1
                return TB(stack.enter_context(nc.sbuf_tensor(f"{name}_{uid[0]}", list(shape), dt)), nb)

            def ps(shape, dt=F32, name="ps", nb=1):
                uid[0] += 1
                t = TB(stack.enter_context(nc.psum_tensor(f"{name}_{uid[0]}", list(shape), dt)), nb)
                for b_ in t.bs:
                    b_.excl = True
                return t

            def psb(name="ptr"):
                uid[0] += 1
                h = stack.enter_context(nc.psum_tensor(f"{name}_{uid[0]}", [128, 512], F32))
                f = TB(h)
                f.b.excl = True
                b = TB(h.bitcast(BF16))
                b.bs = f.bs
                return f, b
            ps.bank = psb
            return sb, ps

        sb, ps = mk(top)
        top.enter_context(nc.allow_non_contiguous_dma(reason="small strided parameter loads"))

        identf = sb([128, 128], F32, name="identf")
        identb = sb([128, 128], BF16, name="identb")
        bd = sb([128, 128], F32, name="bd")
        ones_f = sb([128, 128], F32, name="ones")
        gq = sb([128, 1], F32, name="gq")
        gk = sb([128, 1], F32, name="gk")
        gmix = sb([128, KC], F32, name="gmix")
        gmlp = sb([128, KC], F32, name="gmlp")
        gple = sb([128, KC], F32, name="gple")
        pscale = sb([128, 4], F32, name="pscale")
        poolw = sb([128, 4, 128], BF16, name="poolw")
        kmT = sb([128, 4, 32], F32, name="kmT")
        kmz = sb([128, H, 32], F32, name="kmz")
        gmask = sb([128, 256], F32, name="gmask")
        pastind = sb([128, 256], F32, name="pastind")
        ownind = sb([128, 256], F32, name="ownind")
        corr = sb([128, 64], F32, name="corr")
        cm = sb([128, 4 * UT], BF16, name="cm")

        S.op("gpsimd", MEMSET(identf.t[:], 0.0), writes=[identf.b])
        S.op("gpsimd", lambda e: e.affine_select(out=identf.t[:], in_=identf.t[:], pattern=[[-1, 128]],
                                                 compare_op=ALU.not_equal, fill=1.0, base=0,
                                                 channel_multiplier=1), reads=[identf.b], writes=[identf.b])
        S.op("vector", CP(identb.t[:], identf.t[:]), reads=[identf.b], writes=[identb.b])
        S.op("vector", MEMSET(bd.t[:], 0.0), writes=[bd.b])
        S.op("vector", MEMSET(bd.t[0:64, 0:64], 1.0), writes=[bd.b])
        S.op("vector", MEMSET(bd.t[64:128, 64:128], 1.0), writes=[bd.b])
        S.op("vector", MEMSET(ones_f.t[:], 1.0), writes=[ones_f.b])
        S.op("vector", MEMSET(kmz.t[:], 0.0), writes=[kmz.b])
        S.op("vector", MEMSET(kmT.t[:], 0.0), writes=[kmT.b])
        for (dst, src) in ((gq, q_norm_d), (gk, k_norm_d)):
            for hh in range(2):
                S.op("sync", DMA(dst.t[hh * 64:(hh + 1) * 64, :], src.rearrange("(d o) -> d o", o=1)),
                     writes=[dst.b], dma=True)
        for (dst, src) in ((gmix, ln_mix_d), (gmlp, ln_mlp_d), (gple, ln_ple_d)):
            S.op("sync", DMA(dst.t[:], src.rearrange("(k p) -> p k", p=128)), writes=[dst.b], dma=True)
        S.op("sync", DMA(pscale.t[:], pool_scale_d.rearrange("(g p) -> p g", p=128)), writes=[pscale.b], dma=True)
        for (dst, src) in ((gmask, gmask_d), (pastind, pastind_d), (ownind, ownind_d), (corr, corr_d)):
            S.op("sync", DMA(dst.t[:], src), writes=[dst.b], dma=True)
        S.op("gpsimd", DMA(poolw.t[:], pool_w_d.rearrange("g c d -> c g d")), writes=[poolw.b], dma=True)
        S.op("gpsimd", DMA(cm.t[:], cm_d), writes=[cm.b], dma=True)

        xres = sb([128, 4, D], F32, nb=4, name="xres")
        xsb = sb([128, 4, D], BF16, nb=4, name="xsb")
        aT_s = sb([128, 4, NSEQ], BF16, name="aT_s")
        xnT = sb([128, KC, UT], BF16, nb=KC, name="xnT")
        xnTs = sb([128, KC, NSEQ], BF16, nb=KC, name="xnTs")
        junk = sb([128, D], BF16, name="junk")
        ssq4 = sb([128, 4], F32, name="ssq4")
        rstd4 = sb([128, 4], F32, name="rstd4")

        b_wbf = {k: Buf() for k in wbf}
        b_indbf = Buf()
        b_kT = [[Buf() for _ in range(4)] for _ in range(NSLOT)]
        b_v = [[Buf() for _ in range(4)] for _ in range(NSLOT)]

        def ln_transpose(xt, P, ntt, gcols, dstT, ptr):
            T = P * ntt
            S.op("vector", MEMSET(ssq4.t[:P, :], 0.0), writes=[ssq4.b])
            for tt in range(ntt):
                S.op("scalar", ACTF(junk.t[:P, :], xt.t[:P, tt, :], AF.Square, accum_out=ssq4.t[:P, tt:tt + 1]),
                     reads=[xt.bs[tt]], writes=[junk.b, ssq4.b])
            S.op("vector", TS(rstd4.t[:P, :ntt], ssq4.t[:P, :ntt], 1.0 / D, EPS, ALU.mult, ALU.add),
                 reads=[ssq4.b], writes=[rstd4.b])
            S.op("scalar", ACTF(rstd4.t[:P, :ntt], rstd4.t[:P, :ntt], AF.Sqrt), reads=[rstd4.b], writes=[rstd4.b])
            S.op("vector", RCP(rstd4.t[:P, :ntt], rstd4.t[:P, :ntt]), reads=[rstd4.b], writes=[rstd4.b])
            for tt in range(ntt):
                S.op("vector", TS(xsb.t[:P, tt, :], xt.t[:P, tt, :], rstd4.t[:P, tt:tt + 1], None, ALU.mult),
                     reads=[xt.bs[tt], rstd4.b], writes=[xsb.bs[tt]])
            for kc in range(KC):
                pt_k = ptr[kc % 2]
                S.op("tensor", TRG([(pt_k.t[:, tt * P:(tt + 1) * P], xsb.t[:P, tt, kc * 128:(kc + 1) * 128],
                                     identb.t[:P, :P]) for tt in range(ntt)]),
                     reads=xsb.bs[:ntt] + [identb.b], writes=[pt_k.b])
                if kc % 2 == 0:
                    S.op("scalar", ACTF(dstT.t[:, kc, :T], pt_k.t[:, :T], AF.Copy, scale=gcols.t[:, kc:kc + 1]),
                         reads=[pt_k.b, gcols.b], writes=[dstT.bs[kc]])
                else:
                    S.op("vector", TS(dstT.t[:, kc, :T], pt_k.t[:, :T], gcols.t[:, kc:kc + 1], None, ALU.mult),
                         reads=[pt_k.b, gcols.b], writes=[dstT.bs[kc]])

        def head_norm(pk, T, gcol, dst_ap, dst_b, sq, rk, pss):
            S.op("scalar", ACTF(sq.t[:, :T], pk.t[:, :T], AF.Square), reads=[pk.b], writes=[sq.b])
            S.op("tensor", MMG([(pss.t[:, :T], bd.t[:], sq.t[:, :T], True, True)]), reads=[bd.b, sq.b], writes=[pss.b])
            S.op("vector", TS(rk.t[:, :T], pss.t[:, :T], 1.0 / HD, EPS, ALU.mult, ALU.add), reads=[pss.b], writes=[rk.b])
            S.op("scalar", ACTF(rk.t[:, :T], rk.t[:, :T], AF.Sqrt), reads=[rk.b], writes=[rk.b])
            S.op("vector", RCP(rk.t[:, :T], rk.t[:, :T]), reads=[rk.b], writes=[rk.b])
            S.op("vector", STT(dst_ap, pk.t[:, :T], gcol.t[:, 0:1], rk.t[:, :T], ALU.mult, ALU.mult),
                 reads=[pk.b, gcol.b, rk.b], writes=[dst_b])

        with ExitStack() as pa:
            sbA, psA = mk(pa)
            wq_s = sbA([128, KC, 512], BF16, name="wq_s")
            sqA = sbA([128, UT], F32, name="sqA")
            rkA = sbA([128, UT], F32, name="rkA")
            ksum = sbA([128, NSEQ, 512], F32, nb=NSEQ, name="ksum")
            ks_tok = sbA([NSEQ, 512], F32, name="ks_tok")
            vs_tok = sbA([NSEQ, 512], F32, name="vs_tok")
            _pb = [psA.bank(), psA.bank()]
            ptr = [_pb[0][1], _pb[1][1]]
            pk = [psA([128, UT], F32, name="pk") for _ in range(2)]
            pss = psA([128, UT], F32, name="pss")
            pv = [psA([128, 512], F32, name="pv") for _ in range(2)]
            pkT = psA([128, 512], F32, name="pkT")

            with ExitStack() as pa1:
                sbB, _ = mk(pa1)
                wkv = sbB([128, KC, 1024], BF16, nb=KC, name="wkv")
                for kc in range(KC):
                    S.op("gpsimd", DMA(wkv.t[:, kc, :], w_in_d[kc * 128:(kc + 1) * 128, 512:1536]),
                         writes=[wkv.bs[kc]], dma=True)
                S.op("gpsimd", DMA(ind_bf, ind_d), writes=[b_indbf], dma=True)
                for name in ["in", "ao", "po", "out", "up", "down", "pg", "pp"]:
                    src = wsrc[name]
                    ncols = src.shape[1]
                    step = min(ncols, 2048)
                    for c0 in range(0, ncols, step):
                        S.op("gpsimd", DMA(wbf[name][:, c0:c0 + step], src[:, c0:c0 + step]),
                             writes=[b_wbf[name]], dma=True)

                xresB = sbB([128, 4, D], F32, nb=4, name="xresB")
                xbufs = [xres, xresB]
                knT = sbB([128, 4, UT], F32, nb=4, name="knT")
                kbf = [sbB([128, UT], BF16, name="kbf") for _ in range(2)]
                vbf = [sbB([128, H, 128], BF16, name="vbf") for _ in range(2)]
                vf = [sbB([128, 512], F32, name="vf") for _ in range(2)]
                kout = [sbB([128, 512], F32, name="kout") for _ in range(2)]
                kp = [sbB([128, 2048], F32, name="kp") for _ in range(2)]
                kred = [sbB([128, 512], F32, name="kred") for _ in range(2)]
                ptT = sbB([128, NSEQ], I32, name="ptT")
                ptf = sbB([128, NSEQ], F32, name="ptf")
                pidx = sbB([128, NSEQ * 32], I32, name="pidx")
                pidxf = sbB([128, NSEQ * 32], F32, name="pidxf")
                xs_t = sbB([128, 1, D], F32, name="xs_t")
                knTs = sbB([128, 4, NSEQ], F32, nb=4, name="knTs")
                for v in vbf:
                    S.op("vector", MEMSET(v.t[:, :, 64:128], 1.0), writes=[v.b])

                S.op("sync", DMA(ptT.t[:], ptT_d), writes=[ptT.b], dma=True)
                S.op("vector", CP(ptf.t[:], ptT.t[:]), reads=[ptT.b], writes=[ptf.b])
                for s in range(NSEQ):
                    for c in range(32):
                        col = s * 32 + c
                        S.op("gpsimd", TS(pidxf.t[:, col:col + 1], ptf.t[:, s:s + 1], 32.0, float(c), ALU.mult, ALU.add),
                             reads=[ptf.b], writes=[pidxf.b])
                S.op("gpsimd", CP(pidx.t[:], pidxf.t[:]), reads=[pidxf.b], writes=[pidx.b])
                for s in range(NSEQ):
                    S.op("gpsimd", MEMSET(ksum.t[:, s, :], 0.0), writes=[ksum.bs[s]])
                ck_pieces = ck_d.rearrange("(a b) d -> a (b d)", b=32)
                pieces = [(s, c) for s in range(NSEQ) for c in range(32)]
                piece_i = [0]

                def emit_piece():
                    i = piece_i[0]
                    if i >= len(pieces):
                        return
                    piece_i[0] += 1
                    s, c = pieces[i]
                    buf = kp[i % 2]
                    col = s * 32 + c
                    S.op("gpsimd", IDMA(buf.t[:], ck_pieces, pidx.t[:, col:col + 1]), reads=[pidx.b], writes=[buf.b], dma=True)
                    kr = kred[i % 2]
                    eng = "vector" if i % 2 == 0 else "gpsimd"
                    if eng == "vector":
                        S.op("vector", RED(kr.t[:], buf.t[:].rearrange("p (t f) -> p f t", f=512)), reads=[buf.b], writes=[kr.b])
                    else:
                        S.op("gpsimd", TT(buf.t[:, 0:1024], buf.t[:, 0:1024], buf.t[:, 1024:2048], ALU.add), reads=[buf.b], writes=[buf.b])
                        S.op("gpsimd", TT(kr.t[:], buf.t[:, 0:512], buf.t[:, 512:1024], ALU.add), reads=[buf.b], writes=[kr.b])
                    S.op(eng, TT(ksum.t[:, s, :], ksum.t[:, s, :], kr.t[:], ALU.add), reads=[kr.b, ksum.bs[s]], writes=[ksum.bs[s]])

                def load_x(slot, dst):
                    for tt in range(4):
                        S.op("sync", DMA(dst.t[:, tt, :], xb[slot * UT + tt * 128: slot * UT + (tt + 1) * 128, :]),
                             writes=[dst.bs[tt]], dma=True)

                def kv_stage(xT, P, ntt, slot):
                    T = P * ntt
                    sample = slot is None
                    kdst = knTs if sample else knT
                    for m in range(4):
                        pkm = pk[m % 2]
                        S.op("tensor", MMG([(pkm.t[:, :T], wkv.t[:, kc, m * 128:(m + 1) * 128], xT.t[:, kc, :T], kc == 0, kc == KC - 1)
                                            for kc in range(KC)]),
                             reads=list(wkv.bs) + list(xT.bs), writes=[pkm.b])
                        head_norm(pkm, T, gk, kdst.t[:, m, :T], kdst.bs[m], sqA, rkA, pss)
                        if not sample:
                            kb = kbf[m % 2]
                            S.op("gpsimd", CP(kb.t[:], knT.t[:, m, :]), reads=[knT.bs[m]], writes=[kb.b])
                            S.op("sync", DMA(kT_s[m * 128:(m + 1) * 128, slot * UT:(slot + 1) * UT], kb.t[:]),
                                 reads=[kb.b], writes=[b_kT[slot][m]], dma=True)
                            S.op("vector", RED(kmT.t[:, m, 2 * slot:2 * slot + 2], knT.t[:, m, :].rearrange("p (b k) -> p b k", k=256)),
                                 reads=[knT.bs[m]], writes=[kmT.b])
                    if sample or slot < 4:
                        for tt in range(ntt):
                            S.op("tensor", TRG([(pkT.t[:P, m * 128:(m + 1) * 128], kdst.t[:, m, tt * P:(tt + 1) * P], identf.t[:])
                                                for m in range(4)]),
                                 reads=list(kdst.bs) + [identf.b], writes=[pkT.b])
                            if sample:
                                S.op("vector", CP(ks_tok.t[:], pkT.t[:P, :]), reads=[pkT.b], writes=[ks_tok.b])
                                S.op("sync", DMA(ks_o, ks_tok.t[:]), reads=[ks_tok.b], dma=True)
                            else:
                                ko = kout[tt % 2]
                                S.op("scalar", ACTF(ko.t[:], pkT.t[:, :], AF.Copy), reads=[pkT.b], writes=[ko.b])
                                S.op("sync", DMA(k_own[slot * UT + tt * 128: slot * UT + (tt + 1) * 128, :], ko.t[:]),
                                     reads=[ko.b], dma=True)
                    for tt in range(ntt):
                        pvt = pv[tt % 2]
                        S.op("tensor", MMG([(pvt.t[:P, :], xT.t[:, kc, tt * P:(tt + 1) * P], wkv.t[:, kc, 512:1024], kc == 0, kc == KC - 1)
                                            for kc in range(KC)]),
                             reads=list(wkv.bs) + list(xT.bs), writes=[pvt.b])
                        if sample:
                            S.op("vector", CP(vs_tok.t[:], pvt.t[:P, :]), reads=[pvt.b], writes=[vs_tok.b])
                            S.op("sync", DMA(vs_o, vs_tok.t[:]), reads=[vs_tok.b], dma=True)
                        else:
                            vb = vbf[tt % 2]
                            S.op("scalar", ACTF(vb.t[:, :, 0:64], pvt.t[:, :].rearrange("p (h d) -> p h d", d=HD), AF.Copy),
                                 reads=[pvt.b], writes=[vb.b])
                            S.op("sync", DMA(v_s[slot * UT + tt * 128: slot * UT + (tt + 1) * 128, :], vb.t[:].rearrange("p h c -> p (h c)")),
                                 reads=[vb.b], writes=[b_v[slot][tt]], dma=True)
                            if slot < 4:
                                vv = vf[tt % 2]
                                S.op("vector", CP(vv.t[:], pvt.t[:, :]), reads=[pvt.b], writes=[vv.b])
                                S.op("sync", DMA(v_own[slot * UT + tt * 128: slot * UT + (tt + 1) * 128, :], vv.t[:]),
                                     reads=[vv.b], dma=True)

                NSL = CFG["nslots"] if _on("A") else 0
                if NSL:
                    load_x(0, xbufs[0])
                for slot in range(NSL):
                    if slot + 1 < NSL:
                        load_x(slot + 1, xbufs[(slot + 1) % 2])
                    if CFG["pieces"]:
                        for _ in range(8):
                            emit_piece()
                    ln_transpose(xbufs[slot % 2], 128, 4, gmix, xnT, ptr)
                    kv_stage(xnT, 128, 4, slot)
                while _on("A") and CFG["pieces"] and piece_i[0] < len(pieces):
                    emit_piece()
                if _on("A"):
                    S.op("sync", DMA(xs_t.t[:NSEQ, 0, :], xs_d), writes=[xs_t.b], dma=True)
                    ln_transpose(xs_t, NSEQ, 1, gmix, xnTs, ptr)
                    kv_stage(xnTs, NSEQ, 1, None)
                for m in range(4):
                    S.op("vector", CP(kmz.t[0:64, 2 * m, :], kmT.t[0:64, m, :]), reads=[kmT.b], writes=[kmz.b])
                    S.op("vector", CP(kmz.t[64:128, 2 * m + 1, :], kmT.t[64:128, m, :]), reads=[kmT.b], writes=[kmz.b])
                S.barrier()
                S.run_block()

            with ExitStack() as sa:
                sbS, psS = mk(sa)
                qnTs = sbS([128, 4, NSEQ], F32, name="qnTs")
                q_tok = sbS([NSEQ, 512], F32, name="q_tok")
                qrep = sbS([128, 512], F32, name="qrep")
                esel = sbS([NSEQ, NSEQ, 128], F32, name="esel")
                prod = sbS([128, 512], F32, name="prod")
                pgs = sbS([128, H], F32, name="pgs")
                pairm = sbS([128, 64], F32, name="pairm")
                gates = sbS([H, NSEQ, 64], F32, name="gates")
                mx8 = sbS([H, NSEQ, 8], F32, name="mx8")
                pt8i = sbS([H, NSEQ * 128], I32, name="pt8i")
                pt8 = sbS([H, NSEQ * 128], F32, name="pt8")
                oh = sbS([H, 64], F32, name="oh")
                ohp = sbS([H, 64], F32, name="ohp")
                Pm = sbS([H, NSEQ, 6], F32, name="Pm")
                Dm = sbS([H, 48], F32, name="Dm")
                delta = sbS([H, 48], F32, name="delta")
                tokoff = sbS([128, 48], F32, name="tokoff")
                gidxf = sbS([128, NSEQ, 48], F32, name="gidxf")
                gidx = sbS([128, NSEQ, 48], I32, name="gidx")
                KG = [sbS([128, 48, HD], F32, nb=48, name="KG") for _ in range(2)]
                VG = [sbS([128, 48, HD], F32, nb=48, name="VG") for _ in range(2)]
                sprod = sbS([128, 48, HD], F32, name="sprod")
                sc = sbS([128, 48], F32, name="sc")
                pexp = sbS([128, 48], F32, name="pexp")
                den8 = sbS([1, H], F32, name="den8")
                own_p = sbS([NSEQ, 512], F32, name="own_p")
                own_s = sbS([NSEQ, H], F32, name="own_s")
                own_e = sbS([NSEQ, H], F32, name="own_e")
                pmmask = sbS([NSEQ, 32], F32, name="pmmask")
                PM = sbS([NSEQ, NSEQ, H], F32, name="PM")
                a_row = sbS([1, 512], F32, name="a_row")
                rden = sbS([1, H], F32, name="rden")

                pq = pk[0]
                pqT = pkT
                pgt = pv[0]
                pidxp = pv[1]
                pacc = pk[1]
                pden = pss
                paT = pkT

                if _on("samp"):
                    S.op("sync", DMA(wq_s.t[:], wbf["in"].rearrange("(k p) n -> p k n", p=128)[:, :, 0:512]),
                         reads=[b_wbf["in"]], writes=[wq_s.b], dma=True)
                    for m in range(4):
                        S.op("tensor", MMG([(pq.t[:, :NSEQ], wq_s.t[:, kc, m * 128:(m + 1) * 128], xnTs.t[:, kc, :], kc == 0, kc == KC - 1)
                                            for kc in range(KC)]), reads=[wq_s.b] + list(xnTs.bs), writes=[pq.b])
                        head_norm(pq, NSEQ, gq, qnTs.t[:, m, :], qnTs.b, sqA, rkA, pss)
                    S.op("tensor", TRG([(pqT.t[:NSEQ, m * 128:(m + 1) * 128], qnTs.t[:, m, :], identf.t[:]) for m in range(4)]),
                         reads=[qnTs.b, identf.b], writes=[pqT.b])
                    S.op("vector", CP(q_tok.t[:], pqT.t[:NSEQ, :]), reads=[pqT.b], writes=[q_tok.b])
                    S.op("sync", DMA(pairm.t[:], pair_d), writes=[pairm.b], dma=True)
                    S.op("sync", DMA(delta.t[:], delta_d), writes=[delta.b], dma=True)
                    S.op("sync", DMA(tokoff.t[:], tokoff_d), writes=[tokoff.b], dma=True)
                    S.op("sync", DMA(pmmask.t[:], pmmask_d), writes=[pmmask.b], dma=True)
                    S.op("sync", DMA(pt8i.t[:], pt_d.rearrange("s p -> (s p)").partition_broadcast(H)), writes=[pt8i.b], dma=True)
                    S.op("vector", CP(pt8.t[:], pt8i.t[:]), reads=[pt8i.b], writes=[pt8.b])
                    for s in range(NSEQ):
                        S.op("vector", TS(esel.t[:, s, :], ones_f.t[:NSEQ, :], pmmask.t[:, s * H:s * H + 1], None, ALU.mult),
                             reads=[ones_f.b, pmmask.b], writes=[esel.b])
                    for s in range(NSEQ):
                        S.op("tensor", MMG([(pgt.t[:, :], esel.t[:, s, :], q_tok.t[:, :], True, True)]),
                             reads=[esel.b, q_tok.b], writes=[pgt.b])
                        S.op("vector", TT(prod.t[:], ksum.t[:, s, :], pgt.t[:, :], ALU.mult), reads=[ksum.bs[s], pgt.b], writes=[prod.b])
                        S.op("vector", RED(pgs.t[:], prod.t[:].rearrange("p (h d) -> p h d", d=HD)), reads=[prod.b], writes=[pgs.b])
                        S.op("tensor", MMG([(pidxp.t[:H, :64], pgs.t[:], pairm.t[:], True, True)]), reads=[pgs.b, pairm.b], writes=[pidxp.b])
                        S.op("vector", CP(gates.t[:, s, :], pidxp.t[:H, :64]), reads=[pidxp.b], writes=[gates.b])
                        S.op("vector", MAX8(mx8.t[:, s, :], gates.t[:, s, :]), reads=[gates.b], writes=[mx8.b])
                        for i in range(3):
                            S.op("vector", TS(oh.t[:], gates.t[:, s, :], mx8.t[:, s, i:i + 1], None, ALU.is_equal),
                                 reads=[gates.b, mx8.b], writes=[oh.b])
                            for e_ in range(2):
                                ptv = pt8.t[:, s * 128:(s + 1) * 128].rearrange("p (j e) -> p e j", e=2)[:, e_, :]
                                S.op("vector", TT(ohp.t[:], oh.t[:], ptv, ALU.mult), reads=[oh.b, pt8.b], writes=[ohp.b])
                                S.op("vector", RED(Pm.t[:, s, 2 * i + e_:2 * i + e_ + 1], ohp.t[:]), reads=[ohp.b], writes=[Pm.b])
                        S.op("vector", TT(Dm.t[:].rearrange("p (h c) -> p h c", c=6), delta.t[:].rearrange("p (h c) -> p h c", c=6),
                                          Pm.t[:, s, :].unsqueeze(1).to_broadcast([H, H, 6]), ALU.mult),
                             reads=[delta.b, Pm.b], writes=[Dm.b])
                        S.op("tensor", MMG([(pidxp.t[:, 64:112], ones_f.t[:H, :], Dm.t[:], True, True)]), reads=[ones_f.b, Dm.b], writes=[pidxp.b])
                        S.op("vector", STT(gidxf.t[:, s, :], pidxp.t[:, 64:112], 1024.0, tokoff.t[:], ALU.mult, ALU.add),
                             reads=[pidxp.b, tokoff.b], writes=[gidxf.b])
                    S.op("vector", CP(gidx.t[:], gidxf.t[:]), reads=[gidxf.b], writes=[gidx.b])
                    S.op("vector", TT(own_p.t[:], q_tok.t[:], ks_tok.t[:], ALU.mult), reads=[q_tok.b, ks_tok.b], writes=[own_p.b])
                    S.op("vector", RED(own_s.t[:], own_p.t[:].rearrange("p (h d) -> p h d", d=HD)), reads=[own_p.b], writes=[own_s.b])
                    S.op("scalar", ACTF(own_e.t[:], own_s.t[:], AF.Exp, scale=HD ** -0.5), reads=[own_s.b], writes=[own_e.b])
                    S.op("vector", TT(PM.t[:], pmmask.t[:].rearrange("p (s h) -> p s h", h=H),
                                      own_e.t[:].unsqueeze(1).to_broadcast([NSEQ, NSEQ, H]), ALU.mult),
                         reads=[pmmask.b, own_e.b], writes=[PM.b])
                    for s in range(NSEQ):
                        kg = KG[s % 2]
                        vg = VG[s % 2]
                        for c in range(48):
                            S.op("gpsimd", IDMA(kg.t[:, c, :], ck_d, gidx.t[:, s, c:c + 1]), reads=[gidx.b], writes=[kg.bs[c]], dma=True)
                        for c in range(48):
                            S.op("gpsimd", IDMA(vg.t[:, c, :], cv_d, gidx.t[:, s, c:c + 1]), reads=[gidx.b], writes=[vg.bs[c]], dma=True)
                        S.op("tensor", MMG([(pgt.t[:, :], esel.t[:, s, :], q_tok.t[:, :], True, True)]),
                             reads=[esel.b, q_tok.b], writes=[pgt.b])
                        S.op("vector", CP(qrep.t[:], pgt.t[:, :]), reads=[pgt.b], writes=[qrep.b])
                        S.op("vector", TT(sprod.t[:].rearrange("p (h c) d -> p h c d", c=6), kg.t[:].rearrange("p (h c) d -> p h c d", c=6),
                                          qrep.t[:].rearrange("p (h d) -> p h d", d=HD).unsqueeze(2).to_broadcast([128, H, 6, HD]), ALU.mult),
                             reads=list(kg.bs) + [qrep.b], writes=[sprod.b])
                        S.op("vector", RED(sc.t[:], sprod.t[:]), reads=[sprod.b], writes=[sc.b])
                        S.op("scalar", ACTF(pexp.t[:], sc.t[:], AF.Exp, scale=HD ** -0.5), reads=[sc.b], writes=[pexp.b])
                        S.op("tensor", MMG([(pden.t[:1, :48], ones_f.t[:, 0:1], pexp.t[:], True, True)]), reads=[ones_f.b, pexp.b], writes=[pden.b])
                        S.op("vector", RED(den8.t[:], pden.t[:1, :48].rearrange("p (h c) -> p h c", c=6)), reads=[pden.b], writes=[den8.b])
                        S.op("tensor", MMG([(pden.t[:1, 64:64 + H], ones_f.t[:NSEQ, 0:1], PM.t[:, s, :], True, True)]),
                             reads=[ones_f.b, PM.b], writes=[pden.b])
                        S.op("vector", TT(den8.t[:], den8.t[:], pden.t[:1, 64:64 + H], ALU.add), reads=[pden.b, den8.b], writes=[den8.b])
                        S.op("vector", RCP(rden.t[:], den8.t[:]), reads=[den8.b], writes=[rden.b])
                        lst = []
                        for h in range(H):
                            for c6 in range(6):
                                c = h * 6 + c6
                                lst.append((pacc.t[:1, h * HD:(h + 1) * HD], pexp.t[:, c:c + 1], vg.t[:, c, :], c6 == 0, False))
                            lst.append((pacc.t[:1, h * HD:(h + 1) * HD], PM.t[:, s, h:h + 1], vs_tok.t[:, h * HD:(h + 1) * HD], False, True))
                        S.op("tensor", MMG(lst), reads=[pexp.b, PM.b, vs_tok.b] + list(vg.bs), writes=[pacc.b])
                        S.op("vector", TT(a_row.t[:].rearrange("p (h d) -> p h d", d=HD), pacc.t[:1, :].rearrange("p (h d) -> p h d", d=HD),
                                          rden.t[:].unsqueeze(2).to_broadcast([1, H, HD]), ALU.mult),
                             reads=[pacc.b, rden.b], writes=[a_row.b])
                        S.op("tensor", TRG([(paT.t[:, m:m + 1], a_row.t[:1, m * 128:(m + 1) * 128], identf.t[:1, :1]) for m in range(4)]),
                             reads=[a_row.b, identf.b], writes=[paT.b])
                        S.op("vector", CP(aT_s.t[:, :, s], paT.t[:, 0:4]), reads=[paT.b], writes=[aT_s.b])
                S.barrier()
                S.run_block()

        NRING = 6
        ring = [sb([128, 4096], BF16, name="ring") for _ in range(NRING)]
        ring_i = [0]
        tmpf = [sb([128, UT], F32, name="tmpf") for _ in range(4)]

        class Panels:
            def __init__(self, specs, ahead=3):
                self.specs = specs
                self.loaded = {}
                self.next = 0
                self.ahead = ahead

            def _load(self, i):
                name, k0, nk, c0, ncols = self.specs[i]
                r = ring[ring_i[0] % NRING]
                ring_i[0] += 1
                view = r.t[:, 0:nk * ncols].rearrange("p (k c) -> p k c", c=ncols)
                src = wbf[name].rearrange("(k p) n -> p k n", p=128)[:, k0:k0 + nk, c0:c0 + ncols]
                S.op("sync", DMA(view, src), reads=[b_wbf[name]], writes=[r.b], dma=True)
                self.loaded[i] = (view, r.b)

            def get(self, i, oldest=None):
                if oldest is None:
                    oldest = i
                while self.next <= min(i + self.ahead, len(self.specs) - 1) and self.next < oldest + NRING:
                    self._load(self.next)
                    self.next += 1
                assert i in self.loaded and i >= self.next - NRING, (i, self.next)
                return self.loaded[i]

        def unit_pass(J):
            sample = J is None
            P = NSEQ if sample else 128
            ntt = 1 if sample else 4
            T = P * ntt
            xT = xnTs if sample else xnT

            with ExitStack() as s1:
                sb1, ps1 = mk(s1)
                uT = sb1([128, 4, 16 + UT], F32, nb=4, name="uT")
                pl = [sb1([128, 16 + UT], F32, name="pl") for _ in range(2)]
                dT = sb1([128, 4, UT], BF16, nb=4, name="dT")
                bT = sb1([128, 4, UT], BF16, nb=4, name="bT")
                mT = sb1([128, KC, UT], BF16, nb=KC, name="mT")
                _pb = [ps1.bank(), ps1.bank()]
                ptr = [_pb[0][1], _pb[1][1]]
                pA = [ps1([128, UT], F32, name="pA") for _ in range(2)]
                pG = [ps1([128, UT], F32, name="pG") for _ in range(2)]
                pS = [ps1([128, UT], F32, name="pS") for _ in range(2)]
                pacc = _pb[0][0]
                if sample:
                    aT = aT_s
                    aT_bs = [aT_s.b] * 4
                    hist = sb1([15, NSEQ, 512], F32, name="hist")
                    selw = sb1([15, 4], F32, name="selw")
                    u_tok = sb1([NSEQ, 512], F32, name="u_tok")
                else:
                    aTt = sb1([128, 4, UT], BF16, nb=4, name="aT")
                    aT = aTt
                    aT_bs = aTt.bs
                    qnT = sb1([128, 4, UT], F32, nb=4, name="qnT")
                    qaug = sb1([96, H, UT], BF16, nb=H, name="qaug")
                    g1 = sb1([128, 256], F32, name="g1")
                    mx8 = sb1([128, 8], F32, name="mx8")
                    selt = sb1([128, 32], F32, name="selt")
                    negm = sb1([128, 256], F32, name="negm")
                    kb = [sb1([96, 4, UT], BF16, nb=2, name="kb") for _ in range(3)]
                    vb = [sb1([128, 4, 512], BF16, name="vb") for _ in range(3)]
                    pT = [sb1([128, UT], BF16, name="pT") for _ in range(3)]
                    rden = sb1([64, UT], F32, name="rden")
                    xh = sb1([16, 1, D], F32, name="xh")
                    xnTh = sb1([128, KC, 16], BF16, nb=KC, name="xnTh")
                    uo = sb1([15, 512], F32, name="uo")

                if not sample:
                    for tt in range(4):
                        S.op("sync", DMA(xres.t[:, tt, :], xb[J * UT + tt * 128: J * UT + (tt + 1) * 128, :]),
                             writes=[xres.bs[tt]], dma=True)
                    ln_transpose(xres, 128, 4, gmix, xnT, ptr)
                    S.op("sync", DMA(xh.t[:, 0, :], xhalo[J * 16:(J + 1) * 16, :]), writes=[xh.b], dma=True)
                    ln_transpose(xh, 16, 1, gmix, xnTh, ptr)
                else:
                    S.op("sync", DMA(xres.t[:NSEQ, 0, :], xs_d), writes=[xres.bs[0]], dma=True)
                    S.op("sync", DMA(hist.t[:], state_d.rearrange("s r c -> r s c")), writes=[hist.b], dma=True)
                    S.op("sync", DMA(selw.t[:], selw_d), writes=[selw.b], dma=True)

                specs = []
                if not sample:
                    specs.append(("in", 0, KC, 0, 512))
                specs.append(("in", 0, KC, 1536, 512))
                specs += [("ao", 0, 4, 0, 1024), ("po", 0, 4, 0, 1024)]
                specs += [("in", 0, KC, 2048, 512), ("in", 0, KC, 3072, 512), ("in", 0, KC, 2560, 512), ("in", 0, KC, 3584, 512)]
                specs += [("out", 0, KC, 0, 512), ("out", 0, KC, 512, 512)]
                pn = Panels(specs)
                pi = 0

                if not sample:
                    wv, wb = pn.get(pi); pi += 1
                    for m in range(4):
                        pkm = pA[m % 2]
                        S.op("tensor", MMG([(pkm.t[:, :T], wv[:, kc, m * 128:(m + 1) * 128], xT.t[:, kc, :T], kc == 0, kc == KC - 1)
                                            for kc in range(KC)]), reads=[wb] + list(xT.bs), writes=[pkm.b])
                        head_norm(pkm, T, gq, qnT.t[:, m, :], qnT.bs[m], tmpf[0], tmpf[1], pG[0])
                        S.op("scalar", ACTF(qaug.t[0:64, 2 * m, :], qnT.t[0:64, m, :], AF.Copy, scale=HD ** -0.5),
                             reads=[qnT.bs[m]], writes=[qaug.bs[2 * m]])
                        S.op("vector", TS(qaug.t[0:64, 2 * m + 1, :], qnT.t[64:128, m, :], HD ** -0.5, None, ALU.mult),
                             reads=[qnT.bs[m]], writes=[qaug.bs[2 * m + 1]])
                    for tt in range(4):
                        bq = tt // 2
                        gcol = (J * 2 + bq) * 32
                        S.op("tensor", MMG([(pG[1].t[:, h * 32:(h + 1) * 32], qnT.t[:, h // 2, tt * 128:(tt + 1) * 128], kmz.t[:, h, :], True, True)
                                            for h in range(H)]), reads=list(qnT.bs) + [kmz.b], writes=[pG[1].b])
                        S.op("vector", TT(g1.t[:].rearrange("p (h j) -> p h j", j=32), pG[1].t[:, 0:256].rearrange("p (h j) -> p h j", j=32),
                                          gmask.t[:, gcol:gcol + 32].unsqueeze(1).to_broadcast([128, H, 32]), ALU.add),
                             reads=[pG[1].b, gmask.b], writes=[g1.b])
                        for h in range(H):
                            S.op("vector", MAX8(mx8.t[:], g1.t[:, h * 32:(h + 1) * 32]), reads=[g1.b], writes=[mx8.b])
                            S.op("vector", STT(selt.t[:], g1.t[:, h * 32:(h + 1) * 32], mx8.t[:, 2:3], pastind.t[:, gcol:gcol + 32],
                                               ALU.is_ge, ALU.mult), reads=[g1.b, mx8.b, pastind.b], writes=[selt.b])
                            S.op("vector", TT(selt.t[:], selt.t[:], ownind.t[:, gcol:gcol + 32], ALU.max),
                                 reads=[selt.b, ownind.b], writes=[selt.b])
                            S.op("vector", TS(negm.t[:, h * 32:(h + 1) * 32], selt.t[:], -1.0, -NEG, ALU.add, ALU.mult),
                                 reads=[selt.b], writes=[negm.b])
                        for hp in range(4):
                            S.op("tensor", TRG([(pG[0].t[:64, 0:128], negm.t[:, hp * 64:(hp + 1) * 64], identf.t[:])]),
                                 reads=[negm.b, identf.b], writes=[pG[0].b])
                            S.op("vector", CP(qaug.t[64:96, 2 * hp, tt * 128:(tt + 1) * 128], pG[0].t[0:32, 0:128]),
                                 reads=[pG[0].b], writes=[qaug.bs[2 * hp]])
                            S.op("scalar", ACTF(qaug.t[64:96, 2 * hp + 1, tt * 128:(tt + 1) * 128], pG[0].t[32:64, 0:128], AF.Copy),
                                 reads=[pG[0].b], writes=[qaug.bs[2 * hp + 1]])

                wv, wb = pn.get(pi); pi += 1
                for g in range(4):
                    pu = pA[g % 2]
                    S.op("tensor", MMG([(pu.t[:, :T], wv[:, kc, g * 128:(g + 1) * 128], xT.t[:, kc, :T], kc == 0, kc == KC - 1)
                                        for kc in range(KC)]), reads=[wb] + list(xT.bs), writes=[pu.b])
                    S.op("scalar", ACTF(uT.t[:, g, 16:16 + T], pu.t[:, :T], AF.Copy), reads=[pu.b], writes=[uT.bs[g]])
                    if not sample:
                        S.op("tensor", MMG([(pG[0].t[:, :16], wv[:, kc, g * 128:(g + 1) * 128], xnTh.t[:, kc, :], kc == 0, kc == KC - 1)
                                            for kc in range(KC)]), reads=[wb] + list(xnTh.bs), writes=[pG[0].b])
                        S.op("vector", CP(uT.t[:, g, 0:16], pG[0].t[:, :16]), reads=[pG[0].b], writes=[uT.bs[g]])
                if not sample:
                    W = 16 + UT
                    for g in range(4):
                        w = (2, 4, 8, 16)[g]
                        cur = uT.t[:, g, :]
                        cur_b = uT.bs[g]
                        sh = 1
                        k = 0
                        while sh < w:
                            dst = pl[k % 2]
                            S.op("vector", TT(dst.t[:, sh:W], cur[:, sh:W], cur[:, 0:W - sh], ALU.add), reads=[cur_b], writes=[dst.b])
                            S.op("vector", CP(dst.t[:, 0:sh], cur[:, 0:sh]), reads=[cur_b], writes=[dst.b])
                            cur = dst.t[:, :]
                            cur_b = dst.b
                            sh *= 2
                            k += 1
                        if J == 0:
                            S.op("vector", TT(cur[:, 16:32], cur[:, 16:32], corr.t[:, g * 16:(g + 1) * 16], ALU.mult),
                                 reads=[cur_b, corr.b], writes=[cur_b])
                        S.op("vector", STT(dT.t[:, g, :], cur[:, 16:W], 1.0 / w, uT.t[:, g, 16:W], ALU.mult, ALU.subtract),
                             reads=[cur_b, uT.bs[g]], writes=[dT.bs[g]])
                    if J == 3:
                        S.op("tensor", TRG([(pG[1].t[:15, g * 128:(g + 1) * 128], uT.t[:, g, UT + 1:UT + 16], identf.t[:]) for g in range(4)]),
                             reads=list(uT.bs) + [identf.b], writes=[pG[1].b])
                        S.op("vector", CP(uo.t[:], pG[1].t[:15, :]), reads=[pG[1].b], writes=[uo.b])
                        S.op("sync", DMA(pool_o, uo.t[:]), reads=[uo.b], dma=True)
                else:
                    S.op("tensor", MMG([(pG[1].t[:, g * NSEQ + s:g * NSEQ + s + 1], hist.t[:, s, g * 128:(g + 1) * 128], selw.t[:, g:g + 1], True, True)
                                        for g in range(4) for s in range(NSEQ)]), reads=[hist.b, selw.b], writes=[pG[1].b])
                    for g in range(4):
                        w = (2, 4, 8, 16)[g]
                        S.op("vector", STT(dT.t[:, g, :NSEQ], uT.t[:, g, 16:16 + NSEQ], 1.0 / w - 1.0, pG[1].t[:, g * NSEQ:(g + 1) * NSEQ],
                                           ALU.mult, ALU.add), reads=[uT.bs[g], pG[1].b], writes=[dT.bs[g]])
                    S.op("sync", DMA(pools_o[:, 0:14, :], state_d[:, 1:15, :]), dma=True)
                    S.op("tensor", TRG([(pG[0].t[:NSEQ, g * 128:(g + 1) * 128], uT.t[:, g, 16:16 + NSEQ], identf.t[:]) for g in range(4)]),
                         reads=list(uT.bs) + [identf.b], writes=[pG[0].b])
                    S.op("vector", CP(u_tok.t[:], pG[0].t[:NSEQ, :]), reads=[pG[0].b], writes=[u_tok.b])
                    S.op("sync", DMA(pools_o[:, 14, :], u_tok.t[:]), reads=[u_tok.b], dma=True)
                for g in range(4):
                    pb = pA[g % 2]
                    S.op("tensor", MMG([(pb.t[:, :T], poolw.t[:, g, :], dT.t[:, g, :T], True, True)]), reads=[poolw.b, dT.bs[g]], writes=[pb.b])
                    S.op("scalar", ACTF(bT.t[:, g, :T], pb.t[:, :T], AF.Copy, scale=pscale.t[:, g:g + 1]),
                         reads=[pb.b, pscale.b], writes=[bT.bs[g]])

                if not sample:
                    slots = list(range(J + 1)) + list(range(4, 4 + 3 * (J + 1)))
                    accs = [pA[0], pA[1], pG[0], pG[1]]
                    scs = [pS[0], pS[1], pacc]
                    nsl = len(slots)
                    ldi = [0]

                    def load_kv(hg, sl):
                        kbuf = kb[ldi[0] % 3]
                        vbuf = vb[ldi[0] % 3]
                        ldi[0] += 1
                        S.op("sync", DMA(kbuf.t[0:64, :, :], kT_s.rearrange("(h d) k -> d h k", d=HD)[:, hg * 4:hg * 4 + 4, sl * UT:(sl + 1) * UT]),
                             reads=b_kT[sl], writes=[kbuf.bs[0]], dma=True)
                        S.op("sync", DMA(kbuf.t[64:96, :, :], ind_bf[sl * 32:(sl + 1) * 32, :].rearrange("r (a k) -> r a k", a=4)),
                             reads=[b_indbf], writes=[kbuf.bs[1]], dma=True)
                        S.op("sync", DMA(vbuf.t[:, :, :], v_s[sl * UT:(sl + 1) * UT, hg * 512:(hg + 1) * 512].rearrange("(t p) c -> p t c", p=128)),
                             reads=b_v[sl], writes=[vbuf.b], dma=True)
                        return kbuf, vbuf

                    for hg in range(2):
                        steps = []
                        bufs = {}
                        order = [(li, sl) for li, sl in enumerate(slots)]
                        bufs[0] = load_kv(hg, order[0][1])
                        for li, sl in order:
                            for tau in range(4):
                                for hh in range(4):
                                    steps.append((li, sl, tau, hh))
                        n = len(steps)

                        def emit_score(idx):
                            li, sl, tau, hh = steps[idx]
                            if tau == 0 and hh == 0 and li + 1 < nsl:
                                bufs[li + 1] = load_kv(hg, order[li + 1][1])
                            kbuf, vbuf = bufs[li]
                            h = hg * 4 + hh
                            diag = (sl == J)
                            psc = scs[idx % 3]
                            lst = [(psc.t[:, :], kbuf.t[0:96, hh, tau * 128:(tau + 1) * 128], qaug.t[0:96, h, :], True, not diag)]
                            rd = [kbuf.bs[0], kbuf.bs[1], qaug.bs[h]]
                            if diag:
                                lst.append((psc.t[:, :], identb.t[:], cm.t[:, tau * UT:(tau + 1) * UT], False, True))
                                rd += [identb.b, cm.b]
                            S.op("tensor", MMG(lst), reads=rd, writes=[psc.b])
                            pt_ = pT[idx % 3]
                            S.op("scalar", ACTF(pt_.t[:], psc.t[:, :], AF.Exp), reads=[psc.b], writes=[pt_.b])

                        def emit_pv(idx):
                            li, sl, tau, hh = steps[idx]
                            kbuf, vbuf = bufs[li]
                            pt_ = pT[idx % 3]
                            S.op("tensor", MMG([(accs[hh].t[:, :], vbuf.t[:, tau, hh * 128:(hh + 1) * 128], pt_.t[:],
                                                 li == 0 and tau == 0, li == nsl - 1 and tau == 3)]),
                                 reads=[vbuf.b, pt_.b], writes=[accs[hh].b])

                        LOOK = 2
                        for idx in range(n + LOOK):
                            if idx < n:
                                emit_score(idx)
                            if idx >= LOOK:
                                emit_pv(idx - LOOK)
                        for hh in range(4):
                            h = hg * 4 + hh
                            m, e_ = h // 2, h % 2
                            S.op("vector", RCP(rden.t[:], accs[hh].t[64:128, :]), reads=[accs[hh].b], writes=[rden.b])
                            S.op("vector", TT(aT.t[64 * e_:64 * e_ + 64, m, :], accs[hh].t[0:64, :], rden.t[:], ALU.mult),
                                 reads=[accs[hh].b, rden.b], writes=[aT_bs[m]])

                pi_ao = pi
                wao, wao_b = pn.get(pi_ao, oldest=pi_ao)
                wpo, wpo_b = pn.get(pi_ao + 1, oldest=pi_ao)
                for m in range(KC):
                    half = m // 4
                    gav, gab = pn.get(pi_ao + 2 + 2 * half, oldest=pi_ao)
                    gbv, gbb = pn.get(pi_ao + 3 + 2 * half, oldest=pi_ao)
                    pa_, pg_ = pA[0], pG[0]
                    S.op("tensor", MMG([(pa_.t[:, :T], wao[:, kc, m * 128:(m + 1) * 128], aT.t[:, kc, :T], kc == 0, kc == 3) for kc in range(4)]),
                         reads=[wao_b] + list(aT_bs), writes=[pa_.b])
                    S.op("tensor", MMG([(pg_.t[:, :T], gav[:, kc, (m % 4) * 128:(m % 4 + 1) * 128], xT.t[:, kc, :T], kc == 0, kc == KC - 1)
                                        for kc in range(KC)]), reads=[gab] + list(xT.bs), writes=[pg_.b])
                    S.op("scalar", ACTF(tmpf[0].t[:, :T], pg_.t[:, :T], AF.Sigmoid), reads=[pg_.b], writes=[tmpf[0].b])
                    S.op("vector", TT(tmpf[1].t[:, :T], pa_.t[:, :T], tmpf[0].t[:, :T], ALU.mult), reads=[pa_.b, tmpf[0].b], writes=[tmpf[1].b])
                    pb_, ph_ = pA[1], pG[1]
                    S.op("tensor", MMG([(pb_.t[:, :T], wpo[:, kc, m * 128:(m + 1) * 128], bT.t[:, kc, :T], kc == 0, kc == 3) for kc in range(4)]),
                         reads=[wpo_b] + list(bT.bs), writes=[pb_.b])
                    S.op("tensor", MMG([(ph_.t[:, :T], gbv[:, kc, (m % 4) * 128:(m % 4 + 1) * 128], xT.t[:, kc, :T], kc == 0, kc == KC - 1)
                                        for kc in range(KC)]), reads=[gbb] + list(xT.bs), writes=[ph_.b])
                    S.op("scalar", ACTF(tmpf[2].t[:, :T], ph_.t[:, :T], AF.Sigmoid), reads=[ph_.b], writes=[tmpf[2].b])
                    S.op("vector", TT(tmpf[3].t[:, :T], pb_.t[:, :T], tmpf[2].t[:, :T], ALU.mult), reads=[pb_.b, tmpf[2].b], writes=[tmpf[3].b])
                    S.op("vector", TT(mT.t[:, m, :T], tmpf[1].t[:, :T], tmpf[3].t[:, :T], ALU.add),
                         reads=[tmpf[1].b, tmpf[3].b], writes=[mT.bs[m]])
                pi = pi_ao + 6
                for n in range(2):
                    wv, wb = pn.get(pi); pi += 1
                    for tt in range(ntt):
                        po = pS[tt % 2]
                        S.op("tensor", MMG([(po.t[:P, :], mT.t[:, kc, tt * P:(tt + 1) * P], wv[:, kc, :], kc == 0, kc == KC - 1) for kc in range(KC)]),
                             reads=[wb] + list(mT.bs), writes=[po.b])
                        S.op("vector", TT(xres.t[:P, tt, n * 512:(n + 1) * 512], po.t[:P, :], xres.t[:P, tt, n * 512:(n + 1) * 512], ALU.add),
                             reads=[po.b, xres.bs[tt]], writes=[xres.bs[tt]])
                S.barrier()
                S.run_block()

            with ExitStack() as s2:
                sb2, ps2 = mk(s2)
                hT = sb2([128, 32, UT], BF16, nb=32, name="hT")
                pt_tok = sb2([128, 4, 256], F32, name="pt_tok")
                pt_bf = sb2([128, 4, 256], BF16, name="pt_bf")
                ppT = sb2([128, 2, UT], BF16, nb=2, name="ppT")
                yt = [sb2([128, 512], F32, name="yt") for _ in range(2)]
                _pb = [ps2.bank(), ps2.bank()]
                ptr = [_pb[0][1], _pb[1][1]]
                pH = [ps2([128, UT], F32, name="pH") for _ in range(2)]
                pD = [ps2([128, 512], F32, name="pD") for _ in range(4)]

                psrc = ps_d if sample else p_own
                for tt in range(ntt):
                    r0 = 0 if sample else J * UT + tt * 128
                    S.op("sync", DMA(pt_tok.t[:P, tt, :], psrc[r0:r0 + P, :]), writes=[pt_tok.b], dma=True)
                ln_transpose(xres, P, ntt, gmlp, xT, ptr)
                specs = [("up", 0, KC, c * 512, 512) for c in range(8)]
                specs += [("down", q * 8, 8, n * 512, 512) for n in range(2) for q in range(4)]
                specs += [("pg", 0, KC, 0, 512), ("pg", 0, KC, 512, 512), ("pp", 0, 2, 0, 1024)]
                pn = Panels(specs)
                pi = 0
                for c in range(8):
                    wv, wb = pn.get(pi); pi += 1
                    for mm in range(4):
                        ph = pH[mm % 2]
                        S.op("tensor", MMG([(ph.t[:, :T], wv[:, kc, mm * 128:(mm + 1) * 128], xT.t[:, kc, :T], kc == 0, kc == KC - 1)
                                            for kc in range(KC)]), reads=[wb] + list(xT.bs), writes=[ph.b])
                        tf = tmpf[mm % 2]
                        S.op("scalar", ACTF(tf.t[:, :T], ph.t[:, :T], AF.Relu), reads=[ph.b], writes=[tf.b])
                        S.op("vector", TT(hT.t[:, c * 4 + mm, :T], tf.t[:, :T], tf.t[:, :T], ALU.mult), reads=[tf.b], writes=[hT.bs[c * 4 + mm]])
                for n in range(2):
                    for q in range(4):
                        wv, wb = pn.get(pi); pi += 1
                        for tt in range(ntt):
                            S.op("tensor", MMG([(pD[tt].t[:P, :], hT.t[:, q * 8 + kc, tt * P:(tt + 1) * P], wv[:, kc, :],
                                                 q == 0 and kc == 0, q == 3 and kc == 7) for kc in range(8)]),
                                 reads=[wb] + hT.bs[q * 8:(q + 1) * 8], writes=[pD[tt].b])
                    for tt in range(ntt):
                        S.op("vector", TT(xres.t[:P, tt, n * 512:(n + 1) * 512], pD[tt].t[:P, :], xres.t[:P, tt, n * 512:(n + 1) * 512], ALU.add),
                             reads=[pD[tt].b, xres.bs[tt]], writes=[xres.bs[tt]])
                ln_transpose(xres, P, ntt, gple, xT, ptr)
                for tt in range(ntt):
                    S.op("vector", CP(pt_bf.t[:P, tt, :], pt_tok.t[:P, tt, :]), reads=[pt_tok.b], writes=[pt_bf.b])
                for kc in range(2):
                    S.op("tensor", TRG([(ptr[kc].t[:, tt * P:(tt + 1) * P], pt_bf.t[:P, tt, kc * 128:(kc + 1) * 128], identb.t[:P, :P])
                                        for tt in range(ntt)]), reads=[pt_bf.b, identb.b], writes=[ptr[kc].b])
                    S.op("vector", CP(ppT.t[:, kc, :T], ptr[kc].t[:, :T]), reads=[ptr[kc].b], writes=[ppT.bs[kc]])
                wg = [pn.get(pi, oldest=pi), pn.get(pi + 1, oldest=pi)]
                wpp, wpp_b = pn.get(pi + 2, oldest=pi)
                pi += 3
                k = 0
                for tt in range(ntt):
                    for n in range(2):
                        pg_, pp_ = pD[0 + (k % 2) * 2], pD[1 + (k % 2) * 2]
                        S.op("tensor", MMG([(pg_.t[:P, :], xT.t[:, kc, tt * P:(tt + 1) * P], wg[n][0][:, kc, :], kc == 0, kc == KC - 1)
                                            for kc in range(KC)]), reads=[wg[n][1]] + list(xT.bs), writes=[pg_.b])
                        S.op("tensor", MMG([(pp_.t[:P, :], ppT.t[:, kc, tt * P:(tt + 1) * P], wpp[:, kc, n * 512:(n + 1) * 512], kc == 0, kc == 1)
                                            for kc in range(2)]), reads=[wpp_b] + list(ppT.bs), writes=[pp_.b])
                        tf = tmpf[k % 2]
                        S.op("scalar", ACTF(tf.t[:P, :], pg_.t[:P, :], AF.Sigmoid), reads=[pg_.b], writes=[tf.b])
                        y = yt[k % 2]
                        S.op("vector", TT(y.t[:P, :], pp_.t[:P, :], tf.t[:P, :], ALU.mult), reads=[pp_.b, tf.b], writes=[y.b])
                        S.op("vector", TT(y.t[:P, :], y.t[:P, :], xres.t[:P, tt, n * 512:(n + 1) * 512], ALU.add),
                             reads=[y.b, xres.bs[tt]], writes=[y.b])
                        if sample:
                            dst = ys_o[:, n * 512:(n + 1) * 512]
                        else:
                            dst = y_own[J * UT + tt * 128: J * UT + (tt + 1) * 128, n * 512:(n + 1) * 512]
                        S.op("sync", DMA(dst, y.t[:P, :]), reads=[y.b], dma=True)
                        k += 1
                S.barrier()
                S.run_block()

        if _on("uS"):
            unit_pass(None)
        for J in range(4):
            if _on("u%d" % J):
                unit_pass(J)
        S.barrier()
        S.run_block()
    return nc


_NC_CACHE = {}


def _core_consts(r):
    own = [4 * J + r for J in range(4)]
    nonown = [u for u in range(16) if u % 4 != r]
    slot_units = own + nonown
    gmask = np.zeros((4, 2, 32), np.float32)
    pastind = np.zeros((4, 2, 32), np.float32)
    ownind = np.zeros((4, 2, 32), np.float32)
    for J in range(4):
        for bq in range(2):
            ob_q = 2 * own[J] + bq
            for sl in range(16):
                for be in range(2):
                    ob = 2 * slot_units[sl] + be
                    rho = 2 * sl + be
                    if ob < ob_q:
                        pastind[J, bq, rho] = 1.0
                    else:
                        gmask[J, bq, rho] = -1e30
                    if ob == ob_q:
                        ownind[J, bq, rho] = 1.0
    corr = np.ones((4, 16), np.float32)
    if r == 0:
        for g, w in enumerate((2, 4, 8, 16)):
            for t in range(16):
                corr[g, t] = w / min(w, t + 1)
    rep = lambda a: np.ascontiguousarray(np.broadcast_to(a.reshape(1, -1), (128, a.size))).astype(np.float32)
    return slot_units, rep(gmask), rep(pastind), rep(ownind), rep(corr)


def _static_consts():
    k = np.arange(128)[:, None, None]
    tau = np.arange(4)[None, :, None]
    q = np.arange(UT)[None, None, :]
    cm = np.where(128 * tau + k <= q, 0.0, NEG).astype(np.float32).reshape(128, 4 * UT)
    ind = np.zeros((NSLOT, 32, UT), np.float32)
    for sl in range(NSLOT):
        ind[sl, 2 * sl, 0:256] = 1.0
        ind[sl, 2 * sl + 1, 256:512] = 1.0
    ind = np.ascontiguousarray(np.broadcast_to(ind.reshape(NSLOT * 32, 1, UT), (NSLOT * 32, 4, UT))).reshape(NSLOT * 32, 4 * UT)
    pairm = np.zeros((128, 64), np.float32)
    pairm[np.arange(128), np.arange(128) // 2] = 1.0
    tokoff = (np.arange(128)[:, None] * 8 + (np.arange(48)[None, :] // 6)).astype(np.float32)
    delta = np.zeros((8, 48), np.float32)
    for h in range(8):
        delta[h, h * 6:(h + 1) * 6] = 1.0
    pmmask = np.zeros((4, 32), np.float32)
    for s in range(4):
        pmmask[s, s * 8:(s + 1) * 8] = 1.0
    selw = np.zeros((15, 4), np.float32)
    for g, w in enumerate((2, 4, 8, 16)):
        selw[16 - w:, g] = 1.0 / w
    return dict(cm=cm, ind=ind, pairm=pairm, tokoff=tokoff, delta=delta, pmmask=pmmask, selw=selw)


def kernel(x_prompt, x_sample, cache_k, cache_v, state_pool, page_table, p_prompt, p_sample, ln_mix, w_in,
           q_norm, k_norm, pool_w, pool_scale, w_attn_out, w_pool_out, w_out, ln_mlp, w_up, w_down, ln_ple,
           w_ple_gate, w_ple_proj):
    f = lambda a: np.ascontiguousarray(np.asarray(a, dtype=np.float32))
    x_prompt = f(x_prompt); x_sample = f(x_sample); p_prompt = f(p_prompt); p_sample = f(p_sample)
    ck = f(cache_k).reshape(-1, HD)
    cv = f(cache_v).reshape(-1, HD)
    page_table = np.ascontiguousarray(np.asarray(page_table, dtype=np.int32))
    state_pool = f(state_pool)
    if "nc" not in _NC_CACHE:
        _NC_CACHE["nc"] = build_nc()
    nc = _NC_CACHE["nc"]
    st = _static_consts()
    shared = dict(
        cache_k=ck, cache_v=cv, ln_mix=f(ln_mix)[0], w_in=f(w_in)[0], q_norm=f(q_norm)[0], k_norm=f(k_norm)[0],
        pool_w=f(pool_w)[0], pool_scale=f(pool_scale)[0], w_attn_out=f(w_attn_out)[0], w_pool_out=f(w_pool_out)[0],
        w_out=f(w_out)[0], ln_mlp=f(ln_mlp)[0], w_up=f(w_up)[0], w_down=f(w_down)[0], ln_ple=f(ln_ple)[0],
        w_ple_gate=f(w_ple_gate)[0], w_ple_proj=f(w_ple_proj)[0], **st)
    in_maps = []
    layouts = {}
    cores = list(CFG.get("cores", range(8)))
    for c in cores:
        b, r = c // 4, c % 4
        slot_units, gmask, pastind, ownind, corr = _core_consts(r)
        xb = np.concatenate([x_prompt[b, u * UT:(u + 1) * UT] for u in slot_units], axis=0)
        xhalo = np.zeros((64, D), np.float32)
        for J in range(4):
            u = slot_units[J]
            if u > 0:
                xhalo[J * 16:(J + 1) * 16] = x_prompt[b, u * UT - 16:u * UT]
        p_own = np.concatenate([p_prompt[0, b, slot_units[J] * UT:(slot_units[J] + 1) * UT] for J in range(4)], axis=0)
        pt = page_table[4 * c:4 * c + 4]
        m = dict(shared)
        m.update(xb=xb, xhalo=xhalo, p_own=np.ascontiguousarray(p_own), xs=np.ascontiguousarray(x_sample[4 * c:4 * c + 4, 0]),
                 ps=np.ascontiguousarray(p_sample[0, 4 * c:4 * c + 4, 0]), ptT=np.ascontiguousarray(pt.T), pt=np.ascontiguousarray(pt),
                 state=np.ascontiguousarray(state_pool[0, 4 * c:4 * c + 4]), gmask=gmask, pastind=pastind, ownind=ownind, corr=corr)
        in_maps.append(m)
        layouts[c] = slot_units
    res = run_bass_kernel_spmd(nc, in_maps, core_ids=list(range(len(cores))))
    outs = res.results
    y_prompt = np.zeros((2, SEQ, D), np.float32)
    k_prompt = np.zeros((1, 2, SEQ, H, HD), np.float32)
    v_prompt = np.zeros((1, 2, SEQ, H, HD), np.float32)
    pool_prompt = np.zeros((1, 2, 15, 512), np.float32)
    y_sample = np.zeros((32, 1, D), np.float32)
    k_sample = np.zeros((1, 32, 1, H, HD), np.float32)
    v_sample = np.zeros((1, 32, 1, H, HD), np.float32)
    pool_sample = np.zeros((1, 32, 15, 512), np.float32)
    for ci, c in enumerate(cores):
        b, r = c // 4, c % 4
        o = outs[ci]
        for J in range(4):
            u = layouts[c][J]
            y_prompt[b, u * UT:(u + 1) * UT] = o["y_own"][J * UT:(J + 1) * UT]
            k_prompt[0, b, u * UT:(u + 1) * UT] = o["k_own"][J * UT:(J + 1) * UT].reshape(UT, H, HD)
            v_prompt[0, b, u * UT:(u + 1) * UT] = o["v_own"][J * UT:(J + 1) * UT].reshape(UT, H, HD)
        if r == 3:
            pool_prompt[0, b] = o["pool_o"]
        y_sample[4 * c:4 * c + 4, 0] = o["ys_o"]
        k_sample[0, 4 * c:4 * c + 4, 0] = o["ks_o"].reshape(4, H, HD)
        v_sample[0, 4 * c:4 * c + 4, 0] = o["vs_o"].reshape(4, H, HD)
        pool_sample[0, 4 * c:4 * c + 4] = o["pools_o"]
    return (y_prompt, y_sample, k_prompt, v_prompt, pool_prompt, k_sample, v_sample, pool_sample)
```

```python
import numpy as np
from contextlib import ExitStack
import concourse.bass as bass
import concourse.mybir as mybir
from concourse.bass_utils import run_bass_kernel_spmd

F32 = mybir.dt.float32
BF16 = mybir.dt.bfloat16
I32 = mybir.dt.int32
ALU = mybir.AluOpType
AF = mybir.ActivationFunctionType
AX = mybir.AxisListType

D = 1024
KC = 8
H = 8
HD = 64
UT = 512
NSLOT = 16
SEQ = 8192
EPS = 1e-6
NEG = -30000.0
N_PHYS = 5120
CFG = {"n_phys": 5120, "stop": None, "nslots": 16, "pieces": True}
ORDER = ["p0", "A", "samp", "uS", "u0", "u1", "u2", "u3"]


def _on(name):
    st = CFG.get("stop")
    return True if st is None else ORDER.index(name) <= ORDER.index(st)
NSEQ = 4

ENGS = ["sync", "scalar", "vector", "gpsimd", "tensor"]
N_DMA_SLOTS = {"sync": 16, "gpsimd": 8}


class Buf:
    __slots__ = ("last_write", "readers", "excl")

    def __init__(self):
        self.last_write = None
        self.readers = {}
        self.excl = False


class Sched:
    def __init__(self, nc, stack):
        self.nc = nc
        self.q = {e: [] for e in ENGS}
        self.sem = {}
        for e in ["scalar", "vector", "gpsimd", "tensor"]:
            self.sem[e] = stack.enter_context(nc.semaphore("p_" + e))
        self.cnt = {e: 0 for e in self.sem}
        self.dsem = {}
        self.dcnt = {}
        self.dnext = {}
        for e, n in N_DMA_SLOTS.items():
            for i in range(n):
                self.dsem[(e, i)] = stack.enter_context(nc.semaphore(f"d_{e}{i}"))
                self.dcnt[(e, i)] = 0
            self.dnext[e] = 0
        self.waited = {e: {} for e in ENGS}

    def _semobj(self, key):
        return self.sem[key] if key in self.sem else self.dsem[key]

    def op(self, eng, fn, reads=(), writes=(), dma=False):
        deps = {}
        ex = [b for b in reads if b.excl]
        if ex:
            reads = [b for b in reads if not b.excl]
            writes = list(writes) + [b for b in ex if b not in writes]

        def add(h):
            if h is None:
                return
            k, v = h
            if deps.get(k, 0) < v:
                deps[k] = v

        for b in reads:
            add(b.last_write)
        for b in writes:
            add(b.last_write)
            for k, v in b.readers.items():
                add((k, v))
        if dma:
            slot = self.dnext[eng]
            self.dnext[eng] = (slot + 1) % N_DMA_SLOTS[eng]
            key = (eng, slot)
            if self.dcnt[key] > 0:
                add((key, self.dcnt[key]))
            self.dcnt[key] += 16
            h = (key, self.dcnt[key])
            inc = 16
        else:
            key = eng
            self.cnt[eng] += 1
            h = (key, self.cnt[eng])
            inc = 1
        waits = []
        w = self.waited[eng]
        for k, v in deps.items():
            if k == "tensor" and eng == "tensor":
                continue
            if w.get(k, 0) >= v:
                continue
            w[k] = v
            waits.append((self._semobj(k), v))
        semo = self._semobj(key)

        def emit(e, waits=waits, fn=fn, semo=semo, inc=inc):
            for s, v in waits:
                e.wait_ge(s, v)
            ins = fn(e)
            ins.then_inc(semo, inc)

        self.q[eng].append(emit)
        for b in writes:
            b.last_write = h
            b.readers = {}
        for b in reads:
            if b.readers.get(h[0], 0) < h[1]:
                b.readers[h[0]] = h[1]
        return h

    def barrier(self):
        targets = [(k, v) for k, v in self.cnt.items() if v > 0]
        targets += [(k, v) for k, v in self.dcnt.items() if v > 0]
        for eng in ENGS:
            w = self.waited[eng]
            waits = []
            for k, v in targets:
                if w.get(k, 0) >= v:
                    continue
                w[k] = v
                waits.append((self._semobj(k), v))

            def emit(e, waits=waits):
                for s, v in waits:
                    e.wait_ge(s, v)

            self.q[eng].append(emit)

    def run_block(self):
        nc = self.nc
        q = self.q
        with nc.Block() as block:
            @block.sync
            def _(e):
                for c in q["sync"]:
                    c(e)

            @block.scalar
            def _(e):
                for c in q["scalar"]:
                    c(e)

            @block.vector
            def _(e):
                for c in q["vector"]:
                    c(e)

            @block.gpsimd
            def _(e):
                for c in q["gpsimd"]:
                    c(e)

            @block.tensor
            def _(e):
                for c in q["tensor"]:
                    c(e)
        self.q = {e: [] for e in ENGS}


class TB:
    def __init__(self, t, nb=1):
        self.t = t
        self.bs = [Buf() for _ in range(nb)]

    @property
    def b(self):
        return self.bs[0]


def DMA(out, in_):
    return lambda e: e.dma_start(out=out, in_=in_)


def IDMA(out, in_, idx):
    return lambda e: e.indirect_dma_start(out=out, out_offset=None, in_=in_,
                                          in_offset=bass.IndirectOffsetOnAxis(ap=idx, axis=0))


def MMG(lst):
    def f(e):
        r = None
        for (ps, lhsT, rhs, st, sp) in lst:
            r = e.matmul(ps, lhsT=lhsT, rhs=rhs, start=st, stop=sp)
        return r
    return f


def TRG(lst):
    def f(e):
        r = None
        for (out, in_, ident) in lst:
            r = e.transpose(out=out, in_=in_, identity=ident)
        return r
    return f


def ACTF(out, in_, func, scale=None, accum_out=None):
    kw = {}
    if scale is not None:
        kw["scale"] = scale
    if accum_out is not None:
        kw["accum_out"] = accum_out
    return lambda e: e.activation(out=out, in_=in_, func=func, **kw)


def TT(out, in0, in1, op):
    return lambda e: e.tensor_tensor(out=out, in0=in0, in1=in1, op=op)


def TS(out, in0, s1, s2, op0, op1=None):
    if op1 is None:
        return lambda e: e.tensor_scalar(out=out, in0=in0, scalar1=s1, scalar2=None, op0=op0)
    return lambda e: e.tensor_scalar(out=out, in0=in0, scalar1=s1, scalar2=s2, op0=op0, op1=op1)


def STT(out, in0, scalar, in1, op0, op1):
    return lambda e: e.scalar_tensor_tensor(out=out, in0=in0, scalar=scalar, in1=in1, op0=op0, op1=op1)


def RED(out, in_, op=ALU.add):
    return lambda e: e.tensor_reduce(out=out, in_=in_, axis=AX.X, op=op)


def CP(out, in_):
    return lambda e: e.tensor_copy(out=out, in_=in_)


def RCP(out, in_):
    return lambda e: e.reciprocal(out=out, in_=in_)


def MAX8(out, in_):
    return lambda e: e.max(out=out, in_=in_)


def MEMSET(ap, v):
    return lambda e: e.memset(ap, v)


def build_nc():
    nc = bass.Bass("TRN2", target_bir_lowering=False)
    uid = [0]

    def din(name, shape, dt=F32):
        return nc.dram_tensor(name, list(shape), dt, kind="ExternalInput").ap()

    def dout(name, shape, dt=F32):
        return nc.dram_tensor(name, list(shape), dt, kind="ExternalOutput").ap()

    def dscr(name, shape, dt):
        return nc.dram_tensor(name, list(shape), dt).ap()

    xb = din("xb", [SEQ, D])
    xhalo = din("xhalo", [64, D])
    p_own = din("p_own", [4 * UT, 256])
    xs_d = din("xs", [NSEQ, D])
    ps_d = din("ps", [NSEQ, 256])
    ptT_d = din("ptT", [128, NSEQ], I32)
    pt_d = din("pt", [NSEQ, 128], I32)
    state_d = din("state", [NSEQ, 15, 512])
    NP_ = CFG["n_phys"]
    ck_d = din("cache_k", [NP_ * 1024, HD])
    cv_d = din("cache_v", [NP_ * 1024, HD])
    ln_mix_d = din("ln_mix", [D])
    w_in_d = din("w_in", [D, 4096])
    q_norm_d = din("q_norm", [HD])
    k_norm_d = din("k_norm", [HD])
    pool_w_d = din("pool_w", [4, 128, 128])
    pool_scale_d = din("pool_scale", [512])
    w_ao_d = din("w_attn_out", [512, D])
    w_po_d = din("w_pool_out", [512, D])
    w_out_d = din("w_out", [D, D])
    ln_mlp_d = din("ln_mlp", [D])
    w_up_d = din("w_up", [D, 4096])
    w_down_d = din("w_down", [4096, D])
    ln_ple_d = din("ln_ple", [D])
    w_pg_d = din("w_ple_gate", [D, D])
    w_pp_d = din("w_ple_proj", [256, D])
    cm_d = din("cm", [128, 4 * UT])
    ind_d = din("ind", [NSLOT * 32, 4 * UT])
    gmask_d = din("gmask", [128, 256])
    pastind_d = din("pastind", [128, 256])
    ownind_d = din("ownind", [128, 256])
    corr_d = din("corr", [128, 64])
    pair_d = din("pairm", [128, 64])
    tokoff_d = din("tokoff", [128, 48])
    delta_d = din("delta", [8, 48])
    pmmask_d = din("pmmask", [4, 32])
    selw_d = din("selw", [15, 4])

    y_own = dout("y_own", [4 * UT, D])
    k_own = dout("k_own", [4 * UT, 512])
    v_own = dout("v_own", [4 * UT, 512])
    pool_o = dout("pool_o", [15, 512])
    ys_o = dout("ys_o", [NSEQ, D])
    ks_o = dout("ks_o", [NSEQ, 512])
    vs_o = dout("vs_o", [NSEQ, 512])
    pools_o = dout("pools_o", [NSEQ, 15, 512])

    wbf = {
        "in": dscr("wbf_in", [D, 4096], BF16),
        "ao": dscr("wbf_ao", [512, D], BF16),
        "po": dscr("wbf_po", [512, D], BF16),
        "out": dscr("wbf_out", [D, D], BF16),
        "up": dscr("wbf_up", [D, 4096], BF16),
        "down": dscr("wbf_down", [4096, D], BF16),
        "pg": dscr("wbf_pg", [D, D], BF16),
        "pp": dscr("wbf_pp", [256, D], BF16),
    }
    wsrc = {"in": w_in_d, "ao": w_ao_d, "po": w_po_d, "out": w_out_d, "up": w_up_d,
            "down": w_down_d, "pg": w_pg_d, "pp": w_pp_d}
    ind_bf = dscr("ind_bf", [NSLOT * 32, 4 * UT], BF16)
    kT_s = dscr("kT_s", [H * HD, SEQ], BF16)
    v_s = dscr("v_s", [SEQ, H * 128], BF16)

    with ExitStack() as top:
        S = Sched(nc, top)

        def mk(stack):
            def sb(shape, dt=F32, nb=1, name="t"):
                uid[0] += 1
                return TB(stack.enter_context(nc.sbuf_tensor(f"{name}_{uid[0]}", list(shape), dt)), nb)

            def ps(shape, dt=F32, name="ps", nb=1):
                uid[0] += 1
                t = TB(stack.enter_context(nc.psum_tensor(f"{name}_{uid[0]}", list(shape), dt)), nb)
                for b_ in t.bs:
                    b_.excl = True
                return t

            def psb(name="ptr"):
                uid[0] += 1
                h = stack.enter_context(nc.psum_tensor(f"{name}_{uid[0]}", [128, 512], F32))
                f = TB(h)
                f.b.excl = True
                b = TB(h.bitcast(BF16))
                b.bs = f.bs
                return f, b
            ps.bank = psb
            return sb, ps

        sb, ps = mk(top)
        top.enter_context(nc.allow_non_contiguous_dma(reason="small strided parameter loads"))

        identf = sb([128, 128], F32, name="identf")
        identb = sb([128, 128], BF16, name="identb")
        bd = sb([128, 128], F32, name="bd")
        ones_f = sb([128, 128], F32, name="ones")
        gq = sb([128, 1], F32, name="gq")
        gk = sb([128, 1], F32, name="gk")
        gmix = sb([128, KC], F32, name="gmix")
        gmlp = sb([128, KC], F32, name="gmlp")
        gple = sb([128, KC], F32, name="gple")
        pscale = sb([128, 4], F32, name="pscale")
        poolw = sb([128, 4, 128], BF16, name="poolw")
        kmT = sb([128, 4, 32], F32, name="kmT")
        kmz = sb([128, H, 32], F32, name="kmz")
        gmask = sb([128, 256], F32, name="gmask")
        pastind = sb([128, 256], F32, name="pastind")
        ownind = sb([128, 256], F32, name="ownind")
        corr = sb([128, 64], F32, name="corr")
        cm = sb([128, 4 * UT], BF16, name="cm")

        S.op("gpsimd", MEMSET(identf.t[:], 0.0), writes=[identf.b])
        S.op("gpsimd", lambda e: e.affine_select(out=identf.t[:], in_=identf.t[:], pattern=[[-1, 128]],
                                                 compare_op=ALU.not_equal, fill=1.0, base=0,
                                                 channel_multiplier=1), reads=[identf.b], writes=[identf.b])
        S.op("vector", CP(identb.t[:], identf.t[:]), reads=[identf.b], writes=[identb.b])
        S.op("vector", MEMSET(bd.t[:], 0.0), writes=[bd.b])
        S.op("vector", MEMSET(bd.t[0:64, 0:64], 1.0), writes=[bd.b])
        S.op("vector", MEMSET(bd.t[64:128, 64:128], 1.0), writes=[bd.b])
        S.op("vector", MEMSET(ones_f.t[:], 1.0), writes=[ones_f.b])
        S.op("vector", MEMSET(kmz.t[:], 0.0), writes=[kmz.b])
        S.op("vector", MEMSET(kmT.t[:], 0.0), writes=[kmT.b])
        gstage = sb([KC, 5, 128], F32, name="gstage")
        S.op("vector", MEMSET(gstage.t[:], 0.0), writes=[gstage.b])
        for j, src in enumerate((ln_mix_d, ln_mlp_d, ln_ple_d)):
            S.op("sync", DMA(gstage.t[:, j, :], src.rearrange("(k p) -> k p", p=128)), reads=[gstage.b], writes=[gstage.b], dma=True)
        S.op("sync", DMA(gstage.t[0:4, 3, :], pool_scale_d.rearrange("(g p) -> g p", p=128)), reads=[gstage.b], writes=[gstage.b], dma=True)
        for hh in range(2):
            S.op("sync", DMA(gstage.t[0:1, 4, hh * 64:(hh + 1) * 64], q_norm_d.rearrange("(o d) -> o d", o=1)), reads=[gstage.b], writes=[gstage.b], dma=True)
            S.op("sync", DMA(gstage.t[1:2, 4, hh * 64:(hh + 1) * 64], k_norm_d.rearrange("(o d) -> o d", o=1)), reads=[gstage.b], writes=[gstage.b], dma=True)
        for (dst, src) in ((gmask, gmask_d), (pastind, pastind_d), (ownind, ownind_d), (corr, corr_d)):
            S.op("sync", DMA(dst.t[:], src), writes=[dst.b], dma=True)
        S.op("gpsimd", DMA(poolw.t[:], pool_w_d.rearrange("g c d -> c g d")), writes=[poolw.b], dma=True)
        S.op("gpsimd", DMA(cm.t[:], cm_d), writes=[cm.b], dma=True)

        xres = sb([128, 4, D], F32, nb=4, name="xres")
        xsb = sb([128, 4, D], BF16, nb=4, name="xsb")
        aT_s = sb([128, 4, NSEQ], BF16, name="aT_s")
        xnT = sb([128, KC, UT], BF16, nb=KC, name="xnT")
        xnTs = sb([128, KC, NSEQ], BF16, nb=KC, name="xnTs")
        junk = sb([128, D], BF16, name="junk")
        ssq4 = sb([128, 4], F32, name="ssq4")
        rstd4 = sb([128, 4], F32, name="rstd4")

        b_wbf = {k: Buf() for k in wbf}
        b_indbf = Buf()
        b_kT = [[Buf() for _ in range(4)] for _ in range(NSLOT)]
        b_v = [[Buf() for _ in range(4)] for _ in range(NSLOT)]

        def ln_transpose(xt, P, ntt, gcols, dstT, ptr):
            T = P * ntt
            S.op("vector", MEMSET(ssq4.t[:P, :], 0.0), writes=[ssq4.b])
            for tt in range(ntt):
                S.op("scalar", ACTF(junk.t[:P, :], xt.t[:P, tt, :], AF.Square, accum_out=ssq4.t[:P, tt:tt + 1]),
                     reads=[xt.bs[tt]], writes=[junk.b, ssq4.b])
            S.op("vector", TS(rstd4.t[:P, :ntt], ssq4.t[:P, :ntt], 1.0 / D, EPS, ALU.mult, ALU.add),
                 reads=[ssq4.b], writes=[rstd4.b])
            S.op("scalar", ACTF(rstd4.t[:P, :ntt], rstd4.t[:P, :ntt], AF.Sqrt), reads=[rstd4.b], writes=[rstd4.b])
            S.op("vector", RCP(rstd4.t[:P, :ntt], rstd4.t[:P, :ntt]), reads=[rstd4.b], writes=[rstd4.b])
            for tt in range(ntt):
                S.op("vector", TS(xsb.t[:P, tt, :], xt.t[:P, tt, :], rstd4.t[:P, tt:tt + 1], None, ALU.mult),
                     reads=[xt.bs[tt], rstd4.b], writes=[xsb.bs[tt]])
            for kc in range(KC):
                pt_k = ptr[kc % 2]
                S.op("tensor", TRG([(pt_k.t[:, tt * P:(tt + 1) * P], xsb.t[:P, tt, kc * 128:(kc + 1) * 128],
                                     identb.t[:P, :P]) for tt in range(ntt)]),
                     reads=xsb.bs[:ntt] + [identb.b], writes=[pt_k.b])
                if kc % 2 == 0:
                    S.op("scalar", ACTF(dstT.t[:, kc, :T], pt_k.t[:, :T], AF.Copy, scale=gcols.t[:, kc:kc + 1]),
                         reads=[pt_k.b, gcols.b], writes=[dstT.bs[kc]])
                else:
                    S.op("vector", TS(dstT.t[:, kc, :T], pt_k.t[:, :T], gcols.t[:, kc:kc + 1], None, ALU.mult),
                         reads=[pt_k.b, gcols.b], writes=[dstT.bs[kc]])

        def head_norm(pk, T, gcol, dst_ap, dst_b, sq, rk, pss):
            S.op("scalar", ACTF(sq.t[:, :T], pk.t[:, :T], AF.Square), reads=[pk.b], writes=[sq.b])
            S.op("tensor", MMG([(pss.t[:, :T], bd.t[:], sq.t[:, :T], True, True)]), reads=[bd.b, sq.b], writes=[pss.b])
            S.op("vector", TS(rk.t[:, :T], pss.t[:, :T], 1.0 / HD, EPS, ALU.mult, ALU.add), reads=[pss.b], writes=[rk.b])
            S.op("scalar", ACTF(rk.t[:, :T], rk.t[:, :T], AF.Sqrt), reads=[rk.b], writes=[rk.b])
            S.op("vector", RCP(rk.t[:, :T], rk.t[:, :T]), reads=[rk.b], writes=[rk.b])
            S.op("vector", STT(dst_ap, pk.t[:, :T], gcol.t[:, 0:1], rk.t[:, :T], ALU.mult, ALU.mult),
                 reads=[pk.b, gcol.b, rk.b], writes=[dst_b])

        with ExitStack() as pa:
            sbA, psA = mk(pa)
            wq_s = sbA([128, KC, 512], BF16, name="wq_s")
            sqA = sbA([128, UT], F32, name="sqA")
            rkA = sbA([128, UT], F32, name="rkA")
            ksum = sbA([128, NSEQ, 512], F32, nb=NSEQ, name="ksum")
            ks_tok = sbA([NSEQ, 512], F32, name="ks_tok")
            vs_tok = sbA([NSEQ, 512], F32, name="vs_tok")
            _pb = [psA.bank(), psA.bank()]
            ptr = [_pb[0][1], _pb[1][1]]
            pk = [psA([128, UT], F32, name="pk") for _ in range(2)]
            pss = psA([128, UT], F32, name="pss")
            pv = [psA([128, 512], F32, name="pv") for _ in range(2)]
            pkT = psA([128, 512], F32, name="pkT")
            S.op("tensor", TRG([(pkT.t[:, j * KC:(j + 1) * KC], gstage.t[:, j, :], identf.t[:KC, :KC]) for j in range(5)]),
                 reads=[gstage.b, identf.b], writes=[pkT.b])
            S.op("vector", CP(gmix.t[:], pkT.t[:, 0:KC]), reads=[pkT.b], writes=[gmix.b])
            S.op("vector", CP(gmlp.t[:], pkT.t[:, KC:2 * KC]), reads=[pkT.b], writes=[gmlp.b])
            S.op("vector", CP(gple.t[:], pkT.t[:, 2 * KC:3 * KC]), reads=[pkT.b], writes=[gple.b])
            S.op("vector", CP(pscale.t[:], pkT.t[:, 3 * KC:3 * KC + 4]), reads=[pkT.b], writes=[pscale.b])
            S.op("vector", CP(gq.t[:], pkT.t[:, 4 * KC:4 * KC + 1]), reads=[pkT.b], writes=[gq.b])
            S.op("vector", CP(gk.t[:], pkT.t[:, 4 * KC + 1:4 * KC + 2]), reads=[pkT.b], writes=[gk.b])

            with ExitStack() as pa1:
                sbB, _ = mk(pa1)
                wkv = sbB([128, KC, 1024], BF16, nb=KC, name="wkv")
                for kc in range(KC):
                    S.op("gpsimd", DMA(wkv.t[:, kc, :], w_in_d[kc * 128:(kc + 1) * 128, 512:1536]),
                         writes=[wkv.bs[kc]], dma=True)
                S.op("gpsimd", DMA(ind_bf, ind_d), writes=[b_indbf], dma=True)
                for name in ["in", "ao", "po", "out", "up", "down", "pg", "pp"]:
                    src = wsrc[name]
                    ncols = src.shape[1]
                    step = min(ncols, 2048)
                    for c0 in range(0, ncols, step):
                        S.op("gpsimd", DMA(wbf[name][:, c0:c0 + step], src[:, c0:c0 + step]),
                             writes=[b_wbf[name]], dma=True)

                xresB = sbB([128, 4, D], F32, nb=4, name="xresB")
                xbufs = [xres, xresB]
                knT = sbB([128, 4, UT], F32, nb=4, name="knT")
                kbf = [sbB([128, UT], BF16, name="kbf") for _ in range(2)]
                vbf = [sbB([128, H, 128], BF16, name="vbf") for _ in range(2)]
                vf = [sbB([128, 512], F32, name="vf") for _ in range(2)]
                kout = [sbB([128, 512], F32, name="kout") for _ in range(2)]
                NKP = 4
                kp = [sbB([128, 2048], F32, name="kp") for _ in range(NKP)]
                kred = [sbB([128, 512], F32, name="kred") for _ in range(2)]
                ptT = sbB([128, NSEQ], I32, name="ptT")
                ptf = sbB([128, NSEQ], F32, name="ptf")
                pidx = sbB([128, NSEQ * 32], I32, name="pidx")
                pidxf = sbB([128, NSEQ * 32], F32, name="pidxf")
                xs_t = sbB([128, 1, D], F32, name="xs_t")
                knTs = sbB([128, 4, NSEQ], F32, nb=4, name="knTs")
                for v in vbf:
                    S.op("vector", MEMSET(v.t[:, :, 64:128], 1.0), writes=[v.b])

                S.op("sync", DMA(ptT.t[:], ptT_d), writes=[ptT.b], dma=True)
                S.op("vector", CP(ptf.t[:], ptT.t[:]), reads=[ptT.b], writes=[ptf.b])
                for s in range(NSEQ):
                    for c in range(32):
                        col = s * 32 + c
                        S.op("gpsimd", TS(pidxf.t[:, col:col + 1], ptf.t[:, s:s + 1], 32.0, float(c), ALU.mult, ALU.add),
                             reads=[ptf.b], writes=[pidxf.b])
                S.op("gpsimd", CP(pidx.t[:], pidxf.t[:]), reads=[pidxf.b], writes=[pidx.b])
                for s in range(NSEQ):
                    S.op("gpsimd", MEMSET(ksum.t[:, s, :], 0.0), writes=[ksum.bs[s]])
                ck_pieces = ck_d.rearrange("(a b) d -> a (b d)", b=32)
                pieces = [(s, c) for s in range(NSEQ) for c in range(32)]
                piece_i = [0]

                g_issued = [0]
                r_done = [0]
                g_limit = [0]

                def piece_gather():
                    i = g_issued[0]
                    g_issued[0] += 1
                    s, c = pieces[i]
                    buf = kp[i % NKP]
                    col = s * 32 + c
                    S.op("gpsimd", IDMA(buf.t[:], ck_pieces, pidx.t[:, col:col + 1]), reads=[pidx.b], writes=[buf.b], dma=True)

                def piece_reduce():
                    i = r_done[0]
                    r_done[0] += 1
                    s, c = pieces[i]
                    buf = kp[i % NKP]
                    kr = kred[i % 2]
                    eng = "vector" if i % 2 == 0 else "gpsimd"
                    if eng == "vector":
                        S.op("vector", RED(kr.t[:], buf.t[:].rearrange("p (t f) -> p f t", f=512)), reads=[buf.b], writes=[kr.b])
                    else:
                        S.op("gpsimd", TT(buf.t[:, 0:1024], buf.t[:, 0:1024], buf.t[:, 1024:2048], ALU.add), reads=[buf.b], writes=[buf.b])
                        S.op("gpsimd", TT(kr.t[:], buf.t[:, 0:512], buf.t[:, 512:1024], ALU.add), reads=[buf.b], writes=[kr.b])
                    S.op("gpsimd", TT(ksum.t[:, s, :], ksum.t[:, s, :], kr.t[:], ALU.add), reads=[kr.b, ksum.bs[s]], writes=[ksum.bs[s]])

                def hook(flush=False):
                    if not (_on("A") and CFG["pieces"]):
                        return
                    while True:
                        did = False
                        if r_done[0] < g_issued[0] and (flush or g_issued[0] - r_done[0] >= NKP - 1 or g_issued[0] >= g_limit[0]):
                            piece_reduce()
                            did = True
                        if g_issued[0] < min(g_limit[0], len(pieces)) and g_issued[0] - r_done[0] < NKP:
                            piece_gather()
                            did = True
                        if not flush or not did:
                            break

                def load_x(slot, dst):
                    for tt in range(4):
                        S.op("sync", DMA(dst.t[:, tt, :], xb[slot * UT + tt * 128: slot * UT + (tt + 1) * 128, :]),
                             writes=[dst.bs[tt]], dma=True)

                def kv_stage(xT, P, ntt, slot):
                    T = P * ntt
                    sample = slot is None
                    kdst = knTs if sample else knT
                    for m in range(4):
                        pkm = pk[m % 2]
                        S.op("tensor", MMG([(pkm.t[:, :T], wkv.t[:, kc, m * 128:(m + 1) * 128], xT.t[:, kc, :T], kc == 0, kc == KC - 1)
                                            for kc in range(KC)]),
                             reads=list(wkv.bs) + list(xT.bs), writes=[pkm.b])
                        head_norm(pkm, T, gk, kdst.t[:, m, :T], kdst.bs[m], sqA, rkA, pss)
                        if not sample:
                            hook()
                            kb = kbf[m % 2]
                            S.op("gpsimd", CP(kb.t[:], knT.t[:, m, :]), reads=[knT.bs[m]], writes=[kb.b])
                            S.op("sync", DMA(kT_s[m * 128:(m + 1) * 128, slot * UT:(slot + 1) * UT], kb.t[:]),
                                 reads=[kb.b], writes=[b_kT[slot][m]], dma=True)
                            S.op("vector", RED(kmT.t[:, m, 2 * slot:2 * slot + 2], knT.t[:, m, :].rearrange("p (b k) -> p b k", k=256)),
                                 reads=[knT.bs[m]], writes=[kmT.b])
                    if sample or slot < 4:
                        for tt in range(ntt):
                            S.op("tensor", TRG([(pkT.t[:P, m * 128:(m + 1) * 128], kdst.t[:, m, tt * P:(tt + 1) * P], identf.t[:])
                                                for m in range(4)]),
                                 reads=list(kdst.bs) + [identf.b], writes=[pkT.b])
                            if sample:
                                S.op("vector", CP(ks_tok.t[:], pkT.t[:P, :]), reads=[pkT.b], writes=[ks_tok.b])
                                S.op("sync", DMA(ks_o, ks_tok.t[:]), reads=[ks_tok.b], dma=True)
                            else:
                                ko = kout[tt % 2]
                                S.op("scalar", ACTF(ko.t[:], pkT.t[:, :], AF.Copy), reads=[pkT.b], writes=[ko.b])
                                S.op("sync", DMA(k_own[slot * UT + tt * 128: slot * UT + (tt + 1) * 128, :], ko.t[:]),
                                     reads=[ko.b], dma=True)
                    for tt in range(ntt):
                        if not sample:
                            hook()
                        pvt = pv[tt % 2]
                        S.op("tensor", MMG([(pvt.t[:P, :], xT.t[:, kc, tt * P:(tt + 1) * P], wkv.t[:, kc, 512:1024], kc == 0, kc == KC - 1)
                                            for kc in range(KC)]),
                             reads=list(wkv.bs) + list(xT.bs), writes=[pvt.b])
                        if sample:
                            S.op("vector", CP(vs_tok.t[:], pvt.t[:P, :]), reads=[pvt.b], writes=[vs_tok.b])
                            S.op("sync", DMA(vs_o, vs_tok.t[:]), reads=[vs_tok.b], dma=True)
                        else:
                            vb = vbf[tt % 2]
                            S.op("scalar", ACTF(vb.t[:, :, 0:64], pvt.t[:, :].rearrange("p (h d) -> p h d", d=HD), AF.Copy),
                                 reads=[pvt.b], writes=[vb.b])
                            S.op("sync", DMA(v_s[slot * UT + tt * 128: slot * UT + (tt + 1) * 128, :], vb.t[:].rearrange("p h c -> p (h c)")),
                                 reads=[vb.b], writes=[b_v[slot][tt]], dma=True)
                            if slot < 4:
                                vv = vf[tt % 2]
                                S.op("vector", CP(vv.t[:], pvt.t[:, :]), reads=[pvt.b], writes=[vv.b])
                                S.op("sync", DMA(v_own[slot * UT + tt * 128: slot * UT + (tt + 1) * 128, :], vv.t[:]),
                                     reads=[vv.b], dma=True)

                NSL = CFG["nslots"] if _on("A") else 0
                if NSL:
                    load_x(0, xbufs[0])
                for slot in range(NSL):
                    if slot + 1 < NSL:
                        load_x(slot + 1, xbufs[(slot + 1) % 2])
                    g_limit[0] = 8 * (slot + 1) + 2
                    hook()
                    ln_transpose(xbufs[slot % 2], 128, 4, gmix, xnT, ptr)
                    hook()
                    kv_stage(xnT, 128, 4, slot)
                g_limit[0] = len(pieces)
                hook(flush=True)
                if _on("A"):
                    S.op("sync", DMA(xs_t.t[:NSEQ, 0, :], xs_d), writes=[xs_t.b], dma=True)
                    ln_transpose(xs_t, NSEQ, 1, gmix, xnTs, ptr)
                    kv_stage(xnTs, NSEQ, 1, None)
                for m in range(4):
                    S.op("vector", CP(kmz.t[0:64, 2 * m, :], kmT.t[0:64, m, :]), reads=[kmT.b], writes=[kmz.b])
                    S.op("vector", CP(kmz.t[64:128, 2 * m + 1, :], kmT.t[64:128, m, :]), reads=[kmT.b], writes=[kmz.b])
                S.barrier()
                S.run_block()

            with ExitStack() as sa:
                sbS, psS = mk(sa)
                qnTs = sbS([128, 4, NSEQ], F32, name="qnTs")
                q_tok = sbS([NSEQ, 512], F32, name="q_tok")
                qrep = sbS([128, 512], F32, name="qrep")
                esel = sbS([NSEQ, NSEQ, 128], F32, name="esel")
                prod = sbS([128, 512], F32, name="prod")
                pgs = sbS([128, H], F32, name="pgs")
                pairm = sbS([128, 64], F32, name="pairm")
                gates = sbS([H, NSEQ, 64], F32, name="gates")
                mx8 = sbS([H, NSEQ, 8], F32, name="mx8")
                pt8i = sbS([H, NSEQ * 128], I32, name="pt8i")
                pt8 = sbS([H, NSEQ * 128], F32, name="pt8")
                oh = sbS([H, 64], F32, name="oh")
                ohp = sbS([H, 64], F32, name="ohp")
                Pm = sbS([H, NSEQ, 6], F32, name="Pm")
                Dm = sbS([H, 48], F32, name="Dm")
                delta = sbS([H, 48], F32, name="delta")
                tokoff = sbS([128, 48], F32, name="tokoff")
                gidxf = sbS([128, NSEQ, 48], F32, name="gidxf")
                gidx = sbS([128, NSEQ, 48], I32, name="gidx")
                KG = [sbS([128, 48, HD], F32, nb=48, name="KG") for _ in range(2)]
                VG = [sbS([128, 48, HD], F32, nb=48, name="VG") for _ in range(2)]
                sprod = sbS([128, 48, HD], F32, name="sprod")
                sc = sbS([128, 48], F32, name="sc")
                pexp = sbS([128, 48], F32, name="pexp")
                den8 = sbS([1, H], F32, name="den8")
                own_p = sbS([NSEQ, 512], F32, name="own_p")
                own_s = sbS([NSEQ, H], F32, name="own_s")
                own_e = sbS([NSEQ, H], F32, name="own_e")
                pmmask = sbS([NSEQ, 32], F32, name="pmmask")
                PM = sbS([NSEQ, NSEQ, H], F32, name="PM")
                a_row = sbS([1, 512], F32, name="a_row")
                rden = sbS([1, H], F32, name="rden")

                pq = pk[0]
                pqT = pkT
                pgt = pv[0]
                pidxp = pv[1]
                pacc = pk[1]
                pden = pss
                paT = pkT

                if _on("samp"):
                    S.op("sync", DMA(wq_s.t[:], wbf["in"].rearrange("(k p) n -> p k n", p=128)[:, :, 0:512]),
                         reads=[b_wbf["in"]], writes=[wq_s.b], dma=True)
                    for m in range(4):
                        S.op("tensor", MMG([(pq.t[:, :NSEQ], wq_s.t[:, kc, m * 128:(m + 1) * 128], xnTs.t[:, kc, :], kc == 0, kc == KC - 1)
                                            for kc in range(KC)]), reads=[wq_s.b] + list(xnTs.bs), writes=[pq.b])
                        head_norm(pq, NSEQ, gq, qnTs.t[:, m, :], qnTs.b, sqA, rkA, pss)
                    S.op("tensor", TRG([(pqT.t[:NSEQ, m * 128:(m + 1) * 128], qnTs.t[:, m, :], identf.t[:]) for m in range(4)]),
                         reads=[qnTs.b, identf.b], writes=[pqT.b])
                    S.op("vector", CP(q_tok.t[:], pqT.t[:NSEQ, :]), reads=[pqT.b], writes=[q_tok.b])
                    S.op("sync", DMA(pairm.t[:], pair_d), writes=[pairm.b], dma=True)
                    S.op("sync", DMA(delta.t[:], delta_d), writes=[delta.b], dma=True)
                    S.op("sync", DMA(tokoff.t[:], tokoff_d), writes=[tokoff.b], dma=True)
                    S.op("sync", DMA(pmmask.t[:], pmmask_d), writes=[pmmask.b], dma=True)
                    S.op("sync", DMA(pt8i.t[:], pt_d.rearrange("s p -> (s p)").partition_broadcast(H)), writes=[pt8i.b], dma=True)
                    S.op("vector", CP(pt8.t[:], pt8i.t[:]), reads=[pt8i.b], writes=[pt8.b])
                    for s in range(NSEQ):
                        S.op("vector", TS(esel.t[:, s, :], ones_f.t[:NSEQ, :], pmmask.t[:, s * H:s * H + 1], None, ALU.mult),
                             reads=[ones_f.b, pmmask.b], writes=[esel.b])
                    for s in range(NSEQ):
                        S.op("tensor", MMG([(pgt.t[:, :], esel.t[:, s, :], q_tok.t[:, :], True, True)]),
                             reads=[esel.b, q_tok.b], writes=[pgt.b])
                        S.op("vector", TT(prod.t[:], ksum.t[:, s, :], pgt.t[:, :], ALU.mult), reads=[ksum.bs[s], pgt.b], writes=[prod.b])
                        S.op("vector", RED(pgs.t[:], prod.t[:].rearrange("p (h d) -> p h d", d=HD)), reads=[prod.b], writes=[pgs.b])
                        S.op("tensor", MMG([(pidxp.t[:H, :64], pgs.t[:], pairm.t[:], True, True)]), reads=[pgs.b, pairm.b], writes=[pidxp.b])
                        S.op("vector", CP(gates.t[:, s, :], pidxp.t[:H, :64]), reads=[pidxp.b], writes=[gates.b])
                        S.op("vector", MAX8(mx8.t[:, s, :], gates.t[:, s, :]), reads=[gates.b], writes=[mx8.b])
                        for i in range(3):
                            S.op("vector", TS(oh.t[:], gates.t[:, s, :], mx8.t[:, s, i:i + 1], None, ALU.is_equal),
                                 reads=[gates.b, mx8.b], writes=[oh.b])
                            for e_ in range(2):
                                ptv = pt8.t[:, s * 128:(s + 1) * 128].rearrange("p (j e) -> p e j", e=2)[:, e_, :]
                                S.op("vector", TT(ohp.t[:], oh.t[:], ptv, ALU.mult), reads=[oh.b, pt8.b], writes=[ohp.b])
                                S.op("vector", RED(Pm.t[:, s, 2 * i + e_:2 * i + e_ + 1], ohp.t[:]), reads=[ohp.b], writes=[Pm.b])
                        S.op("vector", TT(Dm.t[:].rearrange("p (h c) -> p h c", c=6), delta.t[:].rearrange("p (h c) -> p h c", c=6),
                                          Pm.t[:, s, :].unsqueeze(1).to_broadcast([H, H, 6]), ALU.mult),
                             reads=[delta.b, Pm.b], writes=[Dm.b])
                        S.op("tensor", MMG([(pidxp.t[:, 64:112], ones_f.t[:H, :], Dm.t[:], True, True)]), reads=[ones_f.b, Dm.b], writes=[pidxp.b])
                        S.op("vector", STT(gidxf.t[:, s, :], pidxp.t[:, 64:112], 1024.0, tokoff.t[:], ALU.mult, ALU.add),
                             reads=[pidxp.b, tokoff.b], writes=[gidxf.b])
                    S.op("vector", CP(gidx.t[:], gidxf.t[:]), reads=[gidxf.b], writes=[gidx.b])
                    S.op("vector", TT(own_p.t[:], q_tok.t[:], ks_tok.t[:], ALU.mult), reads=[q_tok.b, ks_tok.b], writes=[own_p.b])
                    S.op("vector", RED(own_s.t[:], own_p.t[:].rearrange("p (h d) -> p h d", d=HD)), reads=[own_p.b], writes=[own_s.b])
                    S.op("scalar", ACTF(own_e.t[:], own_s.t[:], AF.Exp, scale=HD ** -0.5), reads=[own_s.b], writes=[own_e.b])
                    S.op("vector", TT(PM.t[:], pmmask.t[:].rearrange("p (s h) -> p s h", h=H),
                                      own_e.t[:].unsqueeze(1).to_broadcast([NSEQ, NSEQ, H]), ALU.mult),
                         reads=[pmmask.b, own_e.b], writes=[PM.b])
                    for s in range(NSEQ):
                        kg = KG[s % 2]
                        vg = VG[s % 2]
                        for c in range(48):
                            S.op("gpsimd", IDMA(kg.t[:, c, :], ck_d, gidx.t[:, s, c:c + 1]), reads=[gidx.b], writes=[kg.bs[c]], dma=True)
                        for c in range(48):
                            S.op("gpsimd", IDMA(vg.t[:, c, :], cv_d, gidx.t[:, s, c:c + 1]), reads=[gidx.b], writes=[vg.bs[c]], dma=True)
                        S.op("tensor", MMG([(pgt.t[:, :], esel.t[:, s, :], q_tok.t[:, :], True, True)]),
                             reads=[esel.b, q_tok.b], writes=[pgt.b])
                        S.op("vector", CP(qrep.t[:], pgt.t[:, :]), reads=[pgt.b], writes=[qrep.b])
                        S.op("vector", TT(sprod.t[:].rearrange("p (h c) d -> p h c d", c=6), kg.t[:].rearrange("p (h c) d -> p h c d", c=6),
                                          qrep.t[:].rearrange("p (h d) -> p h d", d=HD).unsqueeze(2).to_broadcast([128, H, 6, HD]), ALU.mult),
                             reads=list(kg.bs) + [qrep.b], writes=[sprod.b])
                        S.op("vector", RED(sc.t[:], sprod.t[:]), reads=[sprod.b], writes=[sc.b])
                        S.op("scalar", ACTF(pexp.t[:], sc.t[:], AF.Exp, scale=HD ** -0.5), reads=[sc.b], writes=[pexp.b])
                        S.op("tensor", MMG([(pden.t[:1, :48], ones_f.t[:, 0:1], pexp.t[:], True, True)]), reads=[ones_f.b, pexp.b], writes=[pden.b])
                        S.op("vector", RED(den8.t[:], pden.t[:1, :48].rearrange("p (h c) -> p h c", c=6)), reads=[pden.b], writes=[den8.b])
                        S.op("tensor", MMG([(pden.t[:1, 64:64 + H], ones_f.t[:NSEQ, 0:1], PM.t[:, s, :], True, True)]),
                             reads=[ones_f.b, PM.b], writes=[pden.b])
                        S.op("vector", TT(den8.t[:], den8.t[:], pden.t[:1, 64:64 + H], ALU.add), reads=[pden.b, den8.b], writes=[den8.b])
                        S.op("vector", RCP(rden.t[:], den8.t[:]), reads=[den8.b], writes=[rden.b])
                        lst = []
                        for h in range(H):
                            for c6 in range(6):
                                c = h * 6 + c6
                                lst.append((pacc.t[:1, h * HD:(h + 1) * HD], pexp.t[:, c:c + 1], vg.t[:, c, :], c6 == 0, False))
                            lst.append((pacc.t[:1, h * HD:(h + 1) * HD], PM.t[:, s, h:h + 1], vs_tok.t[:, h * HD:(h + 1) * HD], False, True))
                        S.op("tensor", MMG(lst), reads=[pexp.b, PM.b, vs_tok.b] + list(vg.bs), writes=[pacc.b])
                        S.op("vector", TT(a_row.t[:].rearrange("p (h d) -> p h d", d=HD), pacc.t[:1, :].rearrange("p (h d) -> p h d", d=HD),
                                          rden.t[:].unsqueeze(2).to_broadcast([1, H, HD]), ALU.mult),
                             reads=[pacc.b, rden.b], writes=[a_row.b])
                        S.op("tensor", TRG([(paT.t[:, m:m + 1], a_row.t[:1, m * 128:(m + 1) * 128], identf.t[:1, :1]) for m in range(4)]),
                             reads=[a_row.b, identf.b], writes=[paT.b])
                        S.op("vector", CP(aT_s.t[:, :, s], paT.t[:, 0:4]), reads=[paT.b], writes=[aT_s.b])
                S.barrier()
                S.run_block()

        NRING = 6
        ring = [sb([128, 4096], BF16, name="ring") for _ in range(NRING)]
        ring_i = [0]
        tmpf = [sb([128, UT], F32, name="tmpf") for _ in range(4)]

        class Panels:
            def __init__(self, specs, ahead=3):
                self.specs = specs
                self.loaded = {}
                self.next = 0
                self.ahead = ahead

            def _load(self, i):
                name, k0, nk, c0, ncols = self.specs[i]
                r = ring[ring_i[0] % NRING]
                ring_i[0] += 1
                view = r.t[:, 0:nk * ncols].rearrange("p (k c) -> p k c", c=ncols)
                src = wbf[name].rearrange("(k p) n -> p k n", p=128)[:, k0:k0 + nk, c0:c0 + ncols]
                S.op("sync", DMA(view, src), reads=[b_wbf[name]], writes=[r.b], dma=True)
                self.loaded[i] = (view, r.b)

            def get(self, i, oldest=None):
                if oldest is None:
                    oldest = i
                while self.next <= min(i + self.ahead, len(self.specs) - 1) and self.next < oldest + NRING:
                    self._load(self.next)
                    self.next += 1
                assert i in self.loaded and i >= self.next - NRING, (i, self.next)
                return self.loaded[i]

        def unit_pass(J):
            sample = J is None
            P = NSEQ if sample else 128
            ntt = 1 if sample else 4
            T = P * ntt
            xT = xnTs if sample else xnT

            with ExitStack() as s1:
                sb1, ps1 = mk(s1)
                uT = sb1([128, 4, 16 + UT], F32, nb=4, name="uT")
                pl = [sb1([128, 16 + UT], F32, name="pl") for _ in range(2)]
                dT = sb1([128, 4, UT], BF16, nb=4, name="dT")
                bT = sb1([128, 4, UT], BF16, nb=4, name="bT")
                mT = sb1([128, KC, UT], BF16, nb=KC, name="mT")
                _pb = [ps1.bank(), ps1.bank()]
                ptr = [_pb[0][1], _pb[1][1]]
                pA = [ps1([128, UT], F32, name="pA") for _ in range(2)]
                pG = [ps1([128, UT], F32, name="pG") for _ in range(2)]
                pS = [ps1([128, UT], F32, name="pS") for _ in range(2)]
                pacc = _pb[0][0]
                if sample:
                    aT = aT_s
                    aT_bs = [aT_s.b] * 4
                    hist = sb1([15, NSEQ, 512], F32, name="hist")
                    selw = sb1([15, 4], F32, name="selw")
                    u_tok = sb1([NSEQ, 512], F32, name="u_tok")
                else:
                    aTt = sb1([128, 4, UT], BF16, nb=4, name="aT")
                    aT = aTt
                    aT_bs = aTt.bs
                    qnT = sb1([128, 4, UT], F32, nb=4, name="qnT")
                    qaug = sb1([96, H, UT], BF16, nb=H, name="qaug")
                    g1 = sb1([128, 256], F32, name="g1")
                    mx8 = sb1([128, 8], F32, name="mx8")
                    selt = sb1([128, 32], F32, name="selt")
                    negm = sb1([128, 256], F32, name="negm")
                    kb = [sb1([96, 4, UT], BF16, nb=2, name="kb") for _ in range(3)]
                    vb = [sb1([128, 4, 512], BF16, name="vb") for _ in range(3)]
                    pT = [sb1([128, UT], BF16, name="pT") for _ in range(3)]
                    rden = sb1([64, UT], F32, name="rden")
                    xh = sb1([16, 1, D], F32, name="xh")
                    xnTh = sb1([128, KC, 16], BF16, nb=KC, name="xnTh")
                    uo = sb1([15, 512], F32, name="uo")

                if not sample:
                    for tt in range(4):
                        S.op("sync", DMA(xres.t[:, tt, :], xb[J * UT + tt * 128: J * UT + (tt + 1) * 128, :]),
                             writes=[xres.bs[tt]], dma=True)
                    ln_transpose(xres, 128, 4, gmix, xnT, ptr)
                    S.op("sync", DMA(xh.t[:, 0, :], xhalo[J * 16:(J + 1) * 16, :]), writes=[xh.b], dma=True)
                    ln_transpose(xh, 16, 1, gmix, xnTh, ptr)
                else:
                    S.op("sync", DMA(xres.t[:NSEQ, 0, :], xs_d), writes=[xres.bs[0]], dma=True)
                    S.op("sync", DMA(hist.t[:], state_d.rearrange("s r c -> r s c")), writes=[hist.b], dma=True)
                    S.op("sync", DMA(selw.t[:], selw_d), writes=[selw.b], dma=True)

                specs = []
                if not sample:
                    specs.append(("in", 0, KC, 0, 512))
                specs.append(("in", 0, KC, 1536, 512))
                specs += [("ao", 0, 4, 0, 1024), ("po", 0, 4, 0, 1024)]
                specs += [("in", 0, KC, 2048, 512), ("in", 0, KC, 3072, 512), ("in", 0, KC, 2560, 512), ("in", 0, KC, 3584, 512)]
                specs += [("out", 0, KC, 0, 512), ("out", 0, KC, 512, 512)]
                pn = Panels(specs)
                pi = 0

                if not sample:
                    wv, wb = pn.get(pi); pi += 1
                    for m in range(4):
                        pkm = pA[m % 2]
                        S.op("tensor", MMG([(pkm.t[:, :T], wv[:, kc, m * 128:(m + 1) * 128], xT.t[:, kc, :T], kc == 0, kc == KC - 1)
                                            for kc in range(KC)]), reads=[wb] + list(xT.bs), writes=[pkm.b])
                        head_norm(pkm, T, gq, qnT.t[:, m, :], qnT.bs[m], tmpf[0], tmpf[1], pG[0])
                        S.op("scalar", ACTF(qaug.t[0:64, 2 * m, :], qnT.t[0:64, m, :], AF.Copy, scale=HD ** -0.5),
                             reads=[qnT.bs[m]], writes=[qaug.bs[2 * m]])
                        S.op("vector", TS(qaug.t[0:64, 2 * m + 1, :], qnT.t[64:128, m, :], HD ** -0.5, None, ALU.mult),
                             reads=[qnT.bs[m]], writes=[qaug.bs[2 * m + 1]])
                    for tt in range(4):
                        bq = tt // 2
                        gcol = (J * 2 + bq) * 32
                        S.op("tensor", MMG([(pG[1].t[:, h * 32:(h + 1) * 32], qnT.t[:, h // 2, tt * 128:(tt + 1) * 128], kmz.t[:, h, :], True, True)
                                            for h in range(H)]), reads=list(qnT.bs) + [kmz.b], writes=[pG[1].b])
                        S.op("vector", TT(g1.t[:].rearrange("p (h j) -> p h j", j=32), pG[1].t[:, 0:256].rearrange("p (h j) -> p h j", j=32),
                                          gmask.t[:, gcol:gcol + 32].unsqueeze(1).to_broadcast([128, H, 32]), ALU.add),
                             reads=[pG[1].b, gmask.b], writes=[g1.b])
                        for h in range(H):
                            S.op("vector", MAX8(mx8.t[:], g1.t[:, h * 32:(h + 1) * 32]), reads=[g1.b], writes=[mx8.b])
                            S.op("vector", STT(selt.t[:], g1.t[:, h * 32:(h + 1) * 32], mx8.t[:, 2:3], pastind.t[:, gcol:gcol + 32],
                                               ALU.is_ge, ALU.mult), reads=[g1.b, mx8.b, pastind.b], writes=[selt.b])
                            S.op("vector", TT(selt.t[:], selt.t[:], ownind.t[:, gcol:gcol + 32], ALU.max),
                                 reads=[selt.b, ownind.b], writes=[selt.b])
                            S.op("vector", TS(negm.t[:, h * 32:(h + 1) * 32], selt.t[:], -1.0, -NEG, ALU.add, ALU.mult),
                                 reads=[selt.b], writes=[negm.b])
                        for hp in range(4):
                            S.op("tensor", TRG([(pG[0].t[:64, 0:128], negm.t[:, hp * 64:(hp + 1) * 64], identf.t[:])]),
                                 reads=[negm.b, identf.b], writes=[pG[0].b])
                            S.op("vector", CP(qaug.t[64:96, 2 * hp, tt * 128:(tt + 1) * 128], pG[0].t[0:32, 0:128]),
                                 reads=[pG[0].b], writes=[qaug.bs[2 * hp]])
                            S.op("scalar", ACTF(qaug.t[64:96, 2 * hp + 1, tt * 128:(tt + 1) * 128], pG[0].t[32:64, 0:128], AF.Copy),
                                 reads=[pG[0].b], writes=[qaug.bs[2 * hp + 1]])

                wv, wb = pn.get(pi); pi += 1
                for g in range(4):
                    pu = pA[g % 2]
                    S.op("tensor", MMG([(pu.t[:, :T], wv[:, kc, g * 128:(g + 1) * 128], xT.t[:, kc, :T], kc == 0, kc == KC - 1)
                                        for kc in range(KC)]), reads=[wb] + list(xT.bs), writes=[pu.b])
                    S.op("scalar", ACTF(uT.t[:, g, 16:16 + T], pu.t[:, :T], AF.Copy), reads=[pu.b], writes=[uT.bs[g]])
                    if not sample:
                        S.op("tensor", MMG([(pG[0].t[:, :16], wv[:, kc, g * 128:(g + 1) * 128], xnTh.t[:, kc, :], kc == 0, kc == KC - 1)
                                            for kc in range(KC)]), reads=[wb] + list(xnTh.bs), writes=[pG[0].b])
                        S.op("vector", CP(uT.t[:, g, 0:16], pG[0].t[:, :16]), reads=[pG[0].b], writes=[uT.bs[g]])
                if not sample:
                    W = 16 + UT
                    for g in range(4):
                        w = (2, 4, 8, 16)[g]
                        cur = uT.t[:, g, :]
                        cur_b = uT.bs[g]
                        sh = 1
                        k = 0
                        while sh < w:
                            dst = pl[k % 2]
                            S.op("vector", TT(dst.t[:, sh:W], cur[:, sh:W], cur[:, 0:W - sh], ALU.add), reads=[cur_b], writes=[dst.b])
                            S.op("vector", CP(dst.t[:, 0:sh], cur[:, 0:sh]), reads=[cur_b], writes=[dst.b])
                            cur = dst.t[:, :]
                            cur_b = dst.b
                            sh *= 2
                            k += 1
                        if J == 0:
                            S.op("vector", TT(cur[:, 16:32], cur[:, 16:32], corr.t[:, g * 16:(g + 1) * 16], ALU.mult),
                                 reads=[cur_b, corr.b], writes=[cur_b])
                        S.op("vector", STT(dT.t[:, g, :], cur[:, 16:W], 1.0 / w, uT.t[:, g, 16:W], ALU.mult, ALU.subtract),
                             reads=[cur_b, uT.bs[g]], writes=[dT.bs[g]])
                    if J == 3:
                        S.op("tensor", TRG([(pG[1].t[:15, g * 128:(g + 1) * 128], uT.t[:, g, UT + 1:UT + 16], identf.t[:]) for g in range(4)]),
                             reads=list(uT.bs) + [identf.b], writes=[pG[1].b])
                        S.op("vector", CP(uo.t[:], pG[1].t[:15, :]), reads=[pG[1].b], writes=[uo.b])
                        S.op("sync", DMA(pool_o, uo.t[:]), reads=[uo.b], dma=True)
                else:
                    S.op("tensor", MMG([(pG[1].t[:, g * NSEQ + s:g * NSEQ + s + 1], hist.t[:, s, g * 128:(g + 1) * 128], selw.t[:, g:g + 1], True, True)
                                        for g in range(4) for s in range(NSEQ)]), reads=[hist.b, selw.b], writes=[pG[1].b])
                    for g in range(4):
                        w = (2, 4, 8, 16)[g]
                        S.op("vector", STT(dT.t[:, g, :NSEQ], uT.t[:, g, 16:16 + NSEQ], 1.0 / w - 1.0, pG[1].t[:, g * NSEQ:(g + 1) * NSEQ],
                                           ALU.mult, ALU.add), reads=[uT.bs[g], pG[1].b], writes=[dT.bs[g]])
                    S.op("sync", DMA(pools_o[:, 0:14, :], state_d[:, 1:15, :]), dma=True)
                    S.op("tensor", TRG([(pG[0].t[:NSEQ, g * 128:(g + 1) * 128], uT.t[:, g, 16:16 + NSEQ], identf.t[:]) for g in range(4)]),
                         reads=list(uT.bs) + [identf.b], writes=[pG[0].b])
                    S.op("vector", CP(u_tok.t[:], pG[0].t[:NSEQ, :]), reads=[pG[0].b], writes=[u_tok.b])
                    S.op("sync", DMA(pools_o[:, 14, :], u_tok.t[:]), reads=[u_tok.b], dma=True)
                for g in range(4):
                    pb = pA[g % 2]
                    S.op("tensor", MMG([(pb.t[:, :T], poolw.t[:, g, :], dT.t[:, g, :T], True, True)]), reads=[poolw.b, dT.bs[g]], writes=[pb.b])
                    S.op("scalar", ACTF(bT.t[:, g, :T], pb.t[:, :T], AF.Copy, scale=pscale.t[:, g:g + 1]),
                         reads=[pb.b, pscale.b], writes=[bT.bs[g]])

                if not sample:
                    slots = list(range(J + 1)) + list(range(4, 4 + 3 * (J + 1)))
                    accs = [pA[0], pA[1], pG[0], pG[1]]
                    scs = [pS[0], pS[1], pacc]
                    nsl = len(slots)
                    ldi = [0]

                    def load_kv(hg, sl):
                        kbuf = kb[ldi[0] % 3]
                        vbuf = vb[ldi[0] % 3]
                        ldi[0] += 1
                        S.op("sync", DMA(kbuf.t[0:64, :, :], kT_s.rearrange("(h d) k -> d h k", d=HD)[:, hg * 4:hg * 4 + 4, sl * UT:(sl + 1) * UT]),
                             reads=b_kT[sl], writes=[kbuf.bs[0]], dma=True)
                        S.op("sync", DMA(kbuf.t[64:96, :, :], ind_bf[sl * 32:(sl + 1) * 32, :].rearrange("r (a k) -> r a k", a=4)),
                             reads=[b_indbf], writes=[kbuf.bs[1]], dma=True)
                        S.op("sync", DMA(vbuf.t[:, :, :], v_s[sl * UT:(sl + 1) * UT, hg * 512:(hg + 1) * 512].rearrange("(t p) c -> p t c", p=128)),
                             reads=b_v[sl], writes=[vbuf.b], dma=True)
                        return kbuf, vbuf

                    for hg in range(2):
                        steps = []
                        bufs = {}
                        order = [(li, sl) for li, sl in enumerate(slots)]
                        bufs[0] = load_kv(hg, order[0][1])
                        for li, sl in order:
                            for tau in range(4):
                                for hh in range(4):
                                    steps.append((li, sl, tau, hh))
                        n = len(steps)

                        def emit_score(idx):
                            li, sl, tau, hh = steps[idx]
                            if tau == 0 and hh == 0 and li + 1 < nsl:
                                bufs[li + 1] = load_kv(hg, order[li + 1][1])
                            kbuf, vbuf = bufs[li]
                            h = hg * 4 + hh
                            diag = (sl == J)
                            psc = scs[idx % 3]
                            lst = [(psc.t[:, :], kbuf.t[0:96, hh, tau * 128:(tau + 1) * 128], qaug.t[0:96, h, :], True, not diag)]
                            rd = [kbuf.bs[0], kbuf.bs[1], qaug.bs[h]]
                            if diag:
                                lst.append((psc.t[:, :], identb.t[:], cm.t[:, tau * UT:(tau + 1) * UT], False, True))
                                rd += [identb.b, cm.b]
                            S.op("tensor", MMG(lst), reads=rd, writes=[psc.b])
                            pt_ = pT[idx % 3]
                            S.op("scalar", ACTF(pt_.t[:], psc.t[:, :], AF.Exp), reads=[psc.b], writes=[pt_.b])

                        def emit_pv(idx):
                            li, sl, tau, hh = steps[idx]
                            kbuf, vbuf = bufs[li]
                            pt_ = pT[idx % 3]
                            S.op("tensor", MMG([(accs[hh].t[:, :], vbuf.t[:, tau, hh * 128:(hh + 1) * 128], pt_.t[:],
                                                 li == 0 and tau == 0, li == nsl - 1 and tau == 3)]),
                                 reads=[vbuf.b, pt_.b], writes=[accs[hh].b])

                        LOOK = 2
                        for idx in range(n + LOOK):
                            if idx < n:
                                emit_score(idx)
                            if idx >= LOOK:
                                emit_pv(idx - LOOK)
                        for hh in range(4):
                            h = hg * 4 + hh
                            m, e_ = h // 2, h % 2
                            S.op("vector", RCP(rden.t[:], accs[hh].t[64:128, :]), reads=[accs[hh].b], writes=[rden.b])
                            S.op("vector", TT(aT.t[64 * e_:64 * e_ + 64, m, :], accs[hh].t[0:64, :], rden.t[:], ALU.mult),
                                 reads=[accs[hh].b, rden.b], writes=[aT_bs[m]])

                pi_ao = pi
                wao, wao_b = pn.get(pi_ao, oldest=pi_ao)
                wpo, wpo_b = pn.get(pi_ao + 1, oldest=pi_ao)
                for m in range(KC):
                    half = m // 4
                    gav, gab = pn.get(pi_ao + 2 + 2 * half, oldest=pi_ao)
                    gbv, gbb = pn.get(pi_ao + 3 + 2 * half, oldest=pi_ao)
                    pa_, pg_ = pA[0], pG[0]
                    S.op("tensor", MMG([(pa_.t[:, :T], wao[:, kc, m * 128:(m + 1) * 128], aT.t[:, kc, :T], kc == 0, kc == 3) for kc in range(4)]),
                         reads=[wao_b] + list(aT_bs), writes=[pa_.b])
                    S.op("tensor", MMG([(pg_.t[:, :T], gav[:, kc, (m % 4) * 128:(m % 4 + 1) * 128], xT.t[:, kc, :T], kc == 0, kc == KC - 1)
                                        for kc in range(KC)]), reads=[gab] + list(xT.bs), writes=[pg_.b])
                    S.op("scalar", ACTF(tmpf[0].t[:, :T], pg_.t[:, :T], AF.Sigmoid), reads=[pg_.b], writes=[tmpf[0].b])
                    S.op("vector", TT(tmpf[1].t[:, :T], pa_.t[:, :T], tmpf[0].t[:, :T], ALU.mult), reads=[pa_.b, tmpf[0].b], writes=[tmpf[1].b])
                    pb_, ph_ = pA[1], pG[1]
                    S.op("tensor", MMG([(pb_.t[:, :T], wpo[:, kc, m * 128:(m + 1) * 128], bT.t[:, kc, :T], kc == 0, kc == 3) for kc in range(4)]),
                         reads=[wpo_b] + list(bT.bs), writes=[pb_.b])
                    S.op("tensor", MMG([(ph_.t[:, :T], gbv[:, kc, (m % 4) * 128:(m % 4 + 1) * 128], xT.t[:, kc, :T], kc == 0, kc == KC - 1)
                                        for kc in range(KC)]), reads=[gbb] + list(xT.bs), writes=[ph_.b])
                    S.op("scalar", ACTF(tmpf[2].t[:, :T], ph_.t[:, :T], AF.Sigmoid), reads=[ph_.b], writes=[tmpf[2].b])
                    S.op("vector", TT(tmpf[3].t[:, :T], pb_.t[:, :T], tmpf[2].t[:, :T], ALU.mult), reads=[pb_.b, tmpf[2].b], writes=[tmpf[3].b])
                    S.op("vector", TT(mT.t[:, m, :T], tmpf[1].t[:, :T], tmpf[3].t[:, :T], ALU.add),
                         reads=[tmpf[1].b, tmpf[3].b], writes=[mT.bs[m]])
                pi = pi_ao + 6
                for n in range(2):
                    wv, wb = pn.get(pi); pi += 1
                    for tt in range(ntt):
                        po = pS[tt % 2]
                        S.op("tensor", MMG([(po.t[:P, :], mT.t[:, kc, tt * P:(tt + 1) * P], wv[:, kc, :], kc == 0, kc == KC - 1) for kc in range(KC)]),
                             reads=[wb] + list(mT.bs), writes=[po.b])
                        S.op("vector", TT(xres.t[:P, tt, n * 512:(n + 1) * 512], po.t[:P, :], xres.t[:P, tt, n * 512:(n + 1) * 512], ALU.add),
                             reads=[po.b, xres.bs[tt]], writes=[xres.bs[tt]])
                S.barrier()
                S.run_block()

            with ExitStack() as s2:
                sb2, ps2 = mk(s2)
                hT = sb2([128, 32, UT], BF16, nb=32, name="hT")
                pt_tok = sb2([128, 4, 256], F32, name="pt_tok")
                pt_bf = sb2([128, 4, 256], BF16, name="pt_bf")
                ppT = sb2([128, 2, UT], BF16, nb=2, name="ppT")
                yt = [sb2([128, 512], F32, name="yt") for _ in range(2)]
                _pb = [ps2.bank(), ps2.bank()]
                ptr = [_pb[0][1], _pb[1][1]]
                pH = [ps2([128, UT], F32, name="pH") for _ in range(2)]
                pD = [ps2([128, 512], F32, name="pD") for _ in range(4)]

                psrc = ps_d if sample else p_own
                for tt in range(ntt):
                    r0 = 0 if sample else J * UT + tt * 128
                    S.op("sync", DMA(pt_tok.t[:P, tt, :], psrc[r0:r0 + P, :]), writes=[pt_tok.b], dma=True)
                ln_transpose(xres, P, ntt, gmlp, xT, ptr)
                specs = [("up", 0, KC, c * 512, 512) for c in range(8)]
                specs += [("down", q * 8, 8, n * 512, 512) for n in range(2) for q in range(4)]
                specs += [("pg", 0, KC, 0, 512), ("pg", 0, KC, 512, 512), ("pp", 0, 2, 0, 1024)]
                pn = Panels(specs)
                pi = 0
                for c in range(8):
                    wv, wb = pn.get(pi); pi += 1
                    for mm in range(4):
                        ph = pH[mm % 2]
                        S.op("tensor", MMG([(ph.t[:, :T], wv[:, kc, mm * 128:(mm + 1) * 128], xT.t[:, kc, :T], kc == 0, kc == KC - 1)
                                            for kc in range(KC)]), reads=[wb] + list(xT.bs), writes=[ph.b])
                        tf = tmpf[mm % 2]
                        S.op("scalar", ACTF(tf.t[:, :T], ph.t[:, :T], AF.Relu), reads=[ph.b], writes=[tf.b])
                        S.op("vector", TT(hT.t[:, c * 4 + mm, :T], tf.t[:, :T], tf.t[:, :T], ALU.mult), reads=[tf.b], writes=[hT.bs[c * 4 + mm]])
                for n in range(2):
                    for q in range(4):
                        wv, wb = pn.get(pi); pi += 1
                        for tt in range(ntt):
                            S.op("tensor", MMG([(pD[tt].t[:P, :], hT.t[:, q * 8 + kc, tt * P:(tt + 1) * P], wv[:, kc, :],
                                                 q == 0 and kc == 0, q == 3 and kc == 7) for kc in range(8)]),
                                 reads=[wb] + hT.bs[q * 8:(q + 1) * 8], writes=[pD[tt].b])
                    for tt in range(ntt):
                        S.op("vector", TT(xres.t[:P, tt, n * 512:(n + 1) * 512], pD[tt].t[:P, :], xres.t[:P, tt, n * 512:(n + 1) * 512], ALU.add),
                             reads=[pD[tt].b, xres.bs[tt]], writes=[xres.bs[tt]])
                ln_transpose(xres, P, ntt, gple, xT, ptr)
                for tt in range(ntt):
                    S.op("vector", CP(pt_bf.t[:P, tt, :], pt_tok.t[:P, tt, :]), reads=[pt_tok.b], writes=[pt_bf.b])
                for kc in range(2):
                    S.op("tensor", TRG([(ptr[kc].t[:, tt * P:(tt + 1) * P], pt_bf.t[:P, tt, kc * 128:(kc + 1) * 128], identb.t[:P, :P])
                                        for tt in range(ntt)]), reads=[pt_bf.b, identb.b], writes=[ptr[kc].b])
                    S.op("vector", CP(ppT.t[:, kc, :T], ptr[kc].t[:, :T]), reads=[ptr[kc].b], writes=[ppT.bs[kc]])
                wg = [pn.get(pi, oldest=pi), pn.get(pi + 1, oldest=pi)]
                wpp, wpp_b = pn.get(pi + 2, oldest=pi)
                pi += 3
                k = 0
                for tt in range(ntt):
                    for n in range(2):
                        pg_, pp_ = pD[0 + (k % 2) * 2], pD[1 + (k % 2) * 2]
                        S.op("tensor", MMG([(pg_.t[:P, :], xT.t[:, kc, tt * P:(tt + 1) * P], wg[n][0][:, kc, :], kc == 0, kc == KC - 1)
                                            for kc in range(KC)]), reads=[wg[n][1]] + list(xT.bs), writes=[pg_.b])
                        S.op("tensor", MMG([(pp_.t[:P, :], ppT.t[:, kc, tt * P:(tt + 1) * P], wpp[:, kc, n * 512:(n + 1) * 512], kc == 0, kc == 1)
                                            for kc in range(2)]), reads=[wpp_b] + list(ppT.bs), writes=[pp_.b])
                        tf = tmpf[k % 2]
                        S.op("scalar", ACTF(tf.t[:P, :], pg_.t[:P, :], AF.Sigmoid), reads=[pg_.b], writes=[tf.b])
                        y = yt[k % 2]
                        S.op("vector", TT(y.t[:P, :], pp_.t[:P, :], tf.t[:P, :], ALU.mult), reads=[pp_.b, tf.b], writes=[y.b])
                        S.op("vector", TT(y.t[:P, :], y.t[:P, :], xres.t[:P, tt, n * 512:(n + 1) * 512], ALU.add),
                             reads=[y.b, xres.bs[tt]], writes=[y.b])
                        if sample:
                            dst = ys_o[:, n * 512:(n + 1) * 512]
                        else:
                            dst = y_own[J * UT + tt * 128: J * UT + (tt + 1) * 128, n * 512:(n + 1) * 512]
                        S.op("sync", DMA(dst, y.t[:P, :]), reads=[y.b], dma=True)
                        k += 1
                S.barrier()
                S.run_block()

        if _on("uS"):
            unit_pass(None)
        for J in range(4):
            if _on("u%d" % J):
                unit_pass(J)
        S.barrier()
        S.run_block()
    return nc


_NC_CACHE = {}


def _core_consts(r):
    own = [4 * J + r for J in range(4)]
    nonown = [u for u in range(16) if u % 4 != r]
    slot_units = own + nonown
    gmask = np.zeros((4, 2, 32), np.float32)
    pastind = np.zeros((4, 2, 32), np.float32)
    ownind = np.zeros((4, 2, 32), np.float32)
    for J in range(4):
        for bq in range(2):
            ob_q = 2 * own[J] + bq
            for sl in range(16):
                for be in range(2):
                    ob = 2 * slot_units[sl] + be
                    rho = 2 * sl + be
                    if ob < ob_q:
                        pastind[J, bq, rho] = 1.0
                    else:
                        gmask[J, bq, rho] = -1e30
                    if ob == ob_q:
                        ownind[J, bq, rho] = 1.0
    corr = np.ones((4, 16), np.float32)
    if r == 0:
        for g, w in enumerate((2, 4, 8, 16)):
            for t in range(16):
                corr[g, t] = w / min(w, t + 1)
    rep = lambda a: np.ascontiguousarray(np.broadcast_to(a.reshape(1, -1), (128, a.size))).astype(np.float32)
    return slot_units, rep(gmask), rep(pastind), rep(ownind), rep(corr)


def _static_consts():
    k = np.arange(128)[:, None, None]
    tau = np.arange(4)[None, :, None]
    q = np.arange(UT)[None, None, :]
    cm = np.where(128 * tau + k <= q, 0.0, NEG).astype(np.float32).reshape(128, 4 * UT)
    ind = np.zeros((NSLOT, 32, UT), np.float32)
    for sl in range(NSLOT):
        ind[sl, 2 * sl, 0:256] = 1.0
        ind[sl, 2 * sl + 1, 256:512] = 1.0
    ind = np.ascontiguousarray(np.broadcast_to(ind.reshape(NSLOT * 32, 1, UT), (NSLOT * 32, 4, UT))).reshape(NSLOT * 32, 4 * UT)
    pairm = np.zeros((128, 64), np.float32)
    pairm[np.arange(128), np.arange(128) // 2] = 1.0
    tokoff = (np.arange(128)[:, None] * 8 + (np.arange(48)[None, :] // 6)).astype(np.float32)
    delta = np.zeros((8, 48), np.float32)
    for h in range(8):
        delta[h, h * 6:(h + 1) * 6] = 1.0
    pmmask = np.zeros((4, 32), np.float32)
    for s in range(4):
        pmmask[s, s * 8:(s + 1) * 8] = 1.0
    selw = np.zeros((15, 4), np.float32)
    for g, w in enumerate((2, 4, 8, 16)):
        selw[16 - w:, g] = 1.0 / w
    return dict(cm=cm, ind=ind, pairm=pairm, tokoff=tokoff, delta=delta, pmmask=pmmask, selw=selw)


def kernel(x_prompt, x_sample, cache_k, cache_v, state_pool, page_table, p_prompt, p_sample, ln_mix, w_in,
           q_norm, k_norm, pool_w, pool_scale, w_attn_out, w_pool_out, w_out, ln_mlp, w_up, w_down, ln_ple,
           w_ple_gate, w_ple_proj):
    f = lambda a: np.ascontiguousarray(np.asarray(a, dtype=np.float32))
    x_prompt = f(x_prompt); x_sample = f(x_sample); p_prompt = f(p_prompt); p_sample = f(p_sample)
    ck = f(cache_k).reshape(-1, HD)
    cv = f(cache_v).reshape(-1, HD)
    page_table = np.ascontiguousarray(np.asarray(page_table, dtype=np.int32))
    state_pool = f(state_pool)
    if "nc" not in _NC_CACHE:
        _NC_CACHE["nc"] = build_nc()
    nc = _NC_CACHE["nc"]
    st = _static_consts()
    shared = dict(
        cache_k=ck, cache_v=cv, ln_mix=f(ln_mix)[0], w_in=f(w_in)[0], q_norm=f(q_norm)[0], k_norm=f(k_norm)[0],
        pool_w=f(pool_w)[0], pool_scale=f(pool_scale)[0], w_attn_out=f(w_attn_out)[0], w_pool_out=f(w_pool_out)[0],
        w_out=f(w_out)[0], ln_mlp=f(ln_mlp)[0], w_up=f(w_up)[0], w_down=f(w_down)[0], ln_ple=f(ln_ple)[0],
        w_ple_gate=f(w_ple_gate)[0], w_ple_proj=f(w_ple_proj)[0], **st)
    in_maps = []
    layouts = {}
    cores = list(CFG.get("cores", range(8)))
    for c in cores:
        b, r = c // 4, c % 4
        slot_units, gmask, pastind, ownind, corr = _core_consts(r)
        xb = np.concatenate([x_prompt[b, u * UT:(u + 1) * UT] for u in slot_units], axis=0)
        xhalo = np.zeros((64, D), np.float32)
        for J in range(4):
            u = slot_units[J]
            if u > 0:
                xhalo[J * 16:(J + 1) * 16] = x_prompt[b, u * UT - 16:u * UT]
        p_own = np.concatenate([p_prompt[0, b, slot_units[J] * UT:(slot_units[J] + 1) * UT] for J in range(4)], axis=0)
        pt = page_table[4 * c:4 * c + 4]
        m = dict(shared)
        m.update(xb=xb, xhalo=xhalo, p_own=np.ascontiguousarray(p_own), xs=np.ascontiguousarray(x_sample[4 * c:4 * c + 4, 0]),
                 ps=np.ascontiguousarray(p_sample[0, 4 * c:4 * c + 4, 0]), ptT=np.ascontiguousarray(pt.T), pt=np.ascontiguousarray(pt),
                 state=np.ascontiguousarray(state_pool[0, 4 * c:4 * c + 4]), gmask=gmask, pastind=pastind, ownind=ownind, corr=corr)
        in_maps.append(m)
        layouts[c] = slot_units
    res = run_bass_kernel_spmd(nc, in_maps, core_ids=list(range(len(cores))))
    outs = res.results
    y_prompt = np.zeros((2, SEQ, D), np.float32)
    k_prompt = np.zeros((1, 2, SEQ, H, HD), np.float32)
    v_prompt = np.zeros((1, 2, SEQ, H, HD), np.float32)
    pool_prompt = np.zeros((1, 2, 15, 512), np.float32)
    y_sample = np.zeros((32, 1, D), np.float32)
    k_sample = np.zeros((1, 32, 1, H, HD), np.float32)
    v_sample = np.zeros((1, 32, 1, H, HD), np.float32)
    pool_sample = np.zeros((1, 32, 15, 512), np.float32)
    for ci, c in enumerate(cores):
        b, r = c // 4, c % 4
        o = outs[ci]
        for J in range(4):
            u = layouts[c][J]
            y_prompt[b, u * UT:(u + 1) * UT] = o["y_own"][J * UT:(J + 1) * UT]
            k_prompt[0, b, u * UT:(u + 1) * UT] = o["k_own"][J * UT:(J + 1) * UT].reshape(UT, H, HD)
            v_prompt[0, b, u * UT:(u + 1) * UT] = o["v_own"][J * UT:(J + 1) * UT].reshape(UT, H, HD)
        if r == 3:
            pool_prompt[0, b] = o["pool_o"]
        y_sample[4 * c:4 * c + 4, 0] = o["ys_o"]
        k_sample[0, 4 * c:4 * c + 4, 0] = o["ks_o"].reshape(4, H, HD)
        v_sample[0, 4 * c:4 * c + 4, 0] = o["vs_o"].reshape(4, H, HD)
        pool_sample[0, 4 * c:4 * c + 4] = o["pools_o"]
    return (y_prompt, y_sample, k_prompt, v_prompt, pool_prompt, k_sample, v_sample, pool_sample)
```

```python
import numpy as np
from contextlib import ExitStack
import concourse.bass as bass
import concourse.mybir as mybir
from concourse.bass_utils import run_bass_kernel_spmd

F32 = mybir.dt.float32
BF16 = mybir.dt.bfloat16
I32 = mybir.dt.int32
ALU = mybir.AluOpType
AF = mybir.ActivationFunctionType
AX = mybir.AxisListType

D = 1024
KC = 8
H = 8
HD = 64
UT = 512
NSLOT = 16
SEQ = 8192
EPS = 1e-6
NEG = -30000.0
N_PHYS = 5120
CFG = {"n_phys": 5120, "stop": None, "nslots": 16, "pieces": True}
ORDER = ["p0", "A", "samp", "uS", "u0", "u1", "u2", "u3"]


def _on(name):
    st = CFG.get("stop")
    return True if st is None else ORDER.index(name) <= ORDER.index(st)
NSEQ = 4

ENGS = ["sync", "scalar", "vector", "gpsimd", "tensor"]
N_DMA_SLOTS = {"sync": 16, "gpsimd": 8}


class Buf:
    __slots__ = ("last_write", "readers", "excl")

    def __init__(self):
        self.last_write = None
        self.readers = {}
        self.excl = False


class Sched:
    def __init__(self, nc, stack):
        self.nc = nc
        self.q = {e: [] for e in ENGS}
        self.sem = {}
        for e in ["scalar", "vector", "gpsimd", "tensor"]:
            self.sem[e] = stack.enter_context(nc.semaphore("p_" + e))
        self.cnt = {e: 0 for e in self.sem}
        self.dsem = {}
        self.dcnt = {}
        self.dnext = {}
        for e, n in N_DMA_SLOTS.items():
            for i in range(n):
                self.dsem[(e, i)] = stack.enter_context(nc.semaphore(f"d_{e}{i}"))
                self.dcnt[(e, i)] = 0
            self.dnext[e] = 0
        self.waited = {e: {} for e in ENGS}

    def _semobj(self, key):
        return self.sem[key] if key in self.sem else self.dsem[key]

    def op(self, eng, fn, reads=(), writes=(), dma=False):
        deps = {}
        ex = [b for b in reads if b.excl]
        if ex:
            reads = [b for b in reads if not b.excl]
            writes = list(writes) + [b for b in ex if b not in writes]

        def add(h):
            if h is None:
                return
            k, v = h
            if deps.get(k, 0) < v:
                deps[k] = v

        for b in reads:
            add(b.last_write)
        for b in writes:
            add(b.last_write)
            for k, v in b.readers.items():
                add((k, v))
        if dma:
            slot = self.dnext[eng]
            self.dnext[eng] = (slot + 1) % N_DMA_SLOTS[eng]
            key = (eng, slot)
            if self.dcnt[key] > 0:
                add((key, self.dcnt[key]))
            self.dcnt[key] += 16
            h = (key, self.dcnt[key])
            inc = 16
        else:
            key = eng
            self.cnt[eng] += 1
            h = (key, self.cnt[eng])
            inc = 1
        waits = []
        w = self.waited[eng]
        for k, v in deps.items():
            if k == "tensor" and eng == "tensor":
                continue
            if w.get(k, 0) >= v:
                continue
            w[k] = v
            waits.append((self._semobj(k), v))
        semo = self._semobj(key)

        def emit(e, waits=waits, fn=fn, semo=semo, inc=inc):
            for s, v in waits:
                e.wait_ge(s, v)
            ins = fn(e)
            ins.then_inc(semo, inc)

        self.q[eng].append(emit)
        for b in writes:
            b.last_write = h
            b.readers = {}
        for b in reads:
            if b.readers.get(h[0], 0) < h[1]:
                b.readers[h[0]] = h[1]
        return h

    def barrier(self):
        targets = [(k, v) for k, v in self.cnt.items() if v > 0]
        targets += [(k, v) for k, v in self.dcnt.items() if v > 0]
        for eng in ENGS:
            w = self.waited[eng]
            waits = []
            for k, v in targets:
                if w.get(k, 0) >= v:
                    continue
                w[k] = v
                waits.append((self._semobj(k), v))

            def emit(e, waits=waits):
                for s, v in waits:
                    e.wait_ge(s, v)

            self.q[eng].append(emit)

    def run_block(self):
        nc = self.nc
        q = self.q
        with nc.Block() as block:
            @block.sync
            def _(e):
                for c in q["sync"]:
                    c(e)

            @block.scalar
            def _(e):
                for c in q["scalar"]:
                    c(e)

            @block.vector
            def _(e):
                for c in q["vector"]:
                    c(e)

            @block.gpsimd
            def _(e):
                for c in q["gpsimd"]:
                    c(e)

            @block.tensor
            def _(e):
                for c in q["tensor"]:
                    c(e)
        self.q = {e: [] for e in ENGS}


class TB:
    def __init__(self, t, nb=1):
        self.t = t
        self.bs = [Buf() for _ in range(nb)]

    @property
    def b(self):
        return self.bs[0]


def DMA(out, in_):
    return lambda e: e.dma_start(out=out, in_=in_)


def IDMA(out, in_, idx):
    return lambda e: e.indirect_dma_start(out=out, out_offset=None, in_=in_,
                                          in_offset=bass.IndirectOffsetOnAxis(ap=idx, axis=0))


def MMG(lst):
    def f(e):
        r = None
        for (ps, lhsT, rhs, st, sp) in lst:
            r = e.matmul(ps, lhsT=lhsT, rhs=rhs, start=st, stop=sp)
        return r
    return f


def TRG(lst):
    def f(e):
        r = None
        for (out, in_, ident) in lst:
            r = e.transpose(out=out, in_=in_, identity=ident)
        return r
    return f


def ACTF(out, in_, func, scale=None, accum_out=None):
    kw = {}
    if scale is not None:
        kw["scale"] = scale
    if accum_out is not None:
        kw["accum_out"] = accum_out
    return lambda e: e.activation(out=out, in_=in_, func=func, **kw)


def TT(out, in0, in1, op):
    return lambda e: e.tensor_tensor(out=out, in0=in0, in1=in1, op=op)


def TS(out, in0, s1, s2, op0, op1=None):
    if op1 is None:
        return lambda e: e.tensor_scalar(out=out, in0=in0, scalar1=s1, scalar2=None, op0=op0)
    return lambda e: e.tensor_scalar(out=out, in0=in0, scalar1=s1, scalar2=s2, op0=op0, op1=op1)


def STT(out, in0, scalar, in1, op0, op1):
    return lambda e: e.scalar_tensor_tensor(out=out, in0=in0, scalar=scalar, in1=in1, op0=op0, op1=op1)


def RED(out, in_, op=ALU.add):
    return lambda e: e.tensor_reduce(out=out, in_=in_, axis=AX.X, op=op)


def CP(out, in_):
    return lambda e: e.tensor_copy(out=out, in_=in_)


def RCP(out, in_):
    return lambda e: e.reciprocal(out=out, in_=in_)


def MAX8(out, in_):
    return lambda e: e.max(out=out, in_=in_)


def MEMSET(ap, v):
    return lambda e: e.memset(ap, v)


def build_nc():
    nc = bass.Bass("TRN2", target_bir_lowering=False)
    uid = [0]

    def din(name, shape, dt=F32):
        return nc.dram_tensor(name, list(shape), dt, kind="ExternalInput").ap()

    def dout(name, shape, dt=F32):
        return nc.dram_tensor(name, list(shape), dt, kind="ExternalOutput").ap()

    def dscr(name, shape, dt):
        return nc.dram_tensor(name, list(shape), dt).ap()

    xb = din("xb", [SEQ, D])
    xhalo = din("xhalo", [64, D])
    p_own = din("p_own", [4 * UT, 256])
    xs_d = din("xs", [NSEQ, D])
    ps_d = din("ps", [NSEQ, 256])
    ptT_d = din("ptT", [128, NSEQ], I32)
    pt_d = din("pt", [NSEQ, 128], I32)
    state_d = din("state", [NSEQ, 15, 512])
    NP_ = CFG["n_phys"]
    ck_d = din("cache_k", [NP_ * 1024, HD])
    cv_d = din("cache_v", [NP_ * 1024, HD])
    ln_mix_d = din("ln_mix", [D])
    w_in_d = din("w_in", [D, 4096])
    q_norm_d = din("q_norm", [HD])
    k_norm_d = din("k_norm", [HD])
    pool_w_d = din("pool_w", [4, 128, 128])
    pool_scale_d = din("pool_scale", [512])
    w_ao_d = din("w_attn_out", [512, D])
    w_po_d = din("w_pool_out", [512, D])
    w_out_d = din("w_out", [D, D])
    ln_mlp_d = din("ln_mlp", [D])
    w_up_d = din("w_up", [D, 4096])
    w_down_d = din("w_down", [4096, D])
    ln_ple_d = din("ln_ple", [D])
    w_pg_d = din("w_ple_gate", [D, D])
    w_pp_d = din("w_ple_proj", [256, D])
    cm_d = din("cm", [128, 4 * UT])
    ind_d = din("ind", [NSLOT * 32, 4 * UT])
    gmask_d = din("gmask", [128, 256])
    pastind_d = din("pastind", [128, 256])
    ownind_d = din("ownind", [128, 256])
    corr_d = din("corr", [128, 64])
    pair_d = din("pairm", [128, 64])
    tokoff_d = din("tokoff", [128, 48])
    delta_d = din("delta", [8, 48])
    pmmask_d = din("pmmask", [4, 32])
    selw_d = din("selw", [15, 4])
    cidx_d = din("cidx", [128, 32])

    y_own = dout("y_own", [4 * UT, D])
    k_own = dout("k_own", [4 * UT, 512])
    v_own = dout("v_own", [4 * UT, 512])
    pool_o = dout("pool_o", [15, 512])
    ys_o = dout("ys_o", [NSEQ, D])
    ks_o = dout("ks_o", [NSEQ, 512])
    vs_o = dout("vs_o", [NSEQ, 512])
    pools_o = dout("pools_o", [NSEQ, 15, 512])

    wbf = {
        "in": dscr("wbf_in", [D, 4096], BF16),
        "ao": dscr("wbf_ao", [512, D], BF16),
        "po": dscr("wbf_po", [512, D], BF16),
        "out": dscr("wbf_out", [D, D], BF16),
        "up": dscr("wbf_up", [D, 4096], BF16),
        "down": dscr("wbf_down", [4096, D], BF16),
        "pg": dscr("wbf_pg", [D, D], BF16),
        "pp": dscr("wbf_pp", [256, D], BF16),
    }
    wsrc = {"in": w_in_d, "ao": w_ao_d, "po": w_po_d, "out": w_out_d, "up": w_up_d,
            "down": w_down_d, "pg": w_pg_d, "pp": w_pp_d}
    ind_bf = dscr("ind_bf", [NSLOT * 32, 4 * UT], BF16)
    kT_s = dscr("kT_s", [H * HD, SEQ], BF16)
    v_s = dscr("v_s", [SEQ, H * 128], BF16)

    with ExitStack() as top:
        S = Sched(nc, top)

        def mk(stack):
            def sb(shape, dt=F32, nb=1, name="t"):
                uid[0] += 1
                return TB(stack.enter_context(nc.sbuf_tensor(f"{name}_{uid[0]}", list(shape), dt)), nb)

            def ps(shape, dt=F32, name="ps", nb=1):
                uid[0] += 1
                t = TB(stack.enter_context(nc.psum_tensor(f"{name}_{uid[0]}", list(shape), dt)), nb)
                for b_ in t.bs:
                    b_.excl = True
                return t

            def psb(name="ptr"):
                uid[0] += 1
                h = stack.enter_context(nc.psum_tensor(f"{name}_{uid[0]}", [128, 512], F32))
                f = TB(h)
                f.b.excl = True
                b = TB(h.bitcast(BF16))
                b.bs = f.bs
                return f, b
            ps.bank = psb
            return sb, ps

        sb, ps = mk(top)
        top.enter_context(nc.allow_non_contiguous_dma(reason="small strided parameter loads"))

        identf = sb([128, 128], F32, name="identf")
        identb = sb([128, 128], BF16, name="identb")
        bd = sb([128, 128], F32, name="bd")
        ones_f = sb([128, 128], F32, name="ones")
        gq = sb([128, 1], F32, name="gq")
        gk = sb([128, 1], F32, name="gk")
        gmix = sb([128, KC], F32, name="gmix")
        gmlp = sb([128, KC], F32, name="gmlp")
        gple = sb([128, KC], F32, name="gple")
        pscale = sb([128, 4], F32, name="pscale")
        poolw = sb([128, 4, 128], BF16, name="poolw")
        kmT = sb([128, 4, 32], F32, name="kmT")
        kmz = sb([128, H, 32], F32, name="kmz")
        gmask = sb([128, 256], F32, name="gmask")
        pastind = sb([128, 256], F32, name="pastind")
        ownind = sb([128, 256], F32, name="ownind")
        corr = sb([128, 64], F32, name="corr")
        cm = sb([128, 4 * UT], BF16, name="cm")

        S.op("gpsimd", MEMSET(identf.t[:], 0.0), writes=[identf.b])
        S.op("gpsimd", lambda e: e.affine_select(out=identf.t[:], in_=identf.t[:], pattern=[[-1, 128]],
                                                 compare_op=ALU.not_equal, fill=1.0, base=0,
                                                 channel_multiplier=1), reads=[identf.b], writes=[identf.b])
        S.op("vector", CP(identb.t[:], identf.t[:]), reads=[identf.b], writes=[identb.b])
        S.op("vector", MEMSET(bd.t[:], 0.0), writes=[bd.b])
        S.op("vector", MEMSET(bd.t[0:64, 0:64], 1.0), writes=[bd.b])
        S.op("vector", MEMSET(bd.t[64:128, 64:128], 1.0), writes=[bd.b])
        S.op("vector", MEMSET(ones_f.t[:], 1.0), writes=[ones_f.b])
        S.op("vector", MEMSET(kmz.t[:], 0.0), writes=[kmz.b])
        S.op("vector", MEMSET(kmT.t[:], 0.0), writes=[kmT.b])
        gstage = sb([KC, 5, 128], F32, name="gstage")
        S.op("vector", MEMSET(gstage.t[:], 0.0), writes=[gstage.b])
        for j, src in enumerate((ln_mix_d, ln_mlp_d, ln_ple_d)):
            S.op("sync", DMA(gstage.t[:, j, :], src.rearrange("(k p) -> k p", p=128)), reads=[gstage.b], writes=[gstage.b], dma=True)
        S.op("sync", DMA(gstage.t[0:4, 3, :], pool_scale_d.rearrange("(g p) -> g p", p=128)), reads=[gstage.b], writes=[gstage.b], dma=True)
        for hh in range(2):
            S.op("sync", DMA(gstage.t[0:1, 4, hh * 64:(hh + 1) * 64], q_norm_d.rearrange("(o d) -> o d", o=1)), reads=[gstage.b], writes=[gstage.b], dma=True)
            S.op("sync", DMA(gstage.t[1:2, 4, hh * 64:(hh + 1) * 64], k_norm_d.rearrange("(o d) -> o d", o=1)), reads=[gstage.b], writes=[gstage.b], dma=True)
        for (dst, src) in ((gmask, gmask_d), (pastind, pastind_d), (ownind, ownind_d), (corr, corr_d)):
            S.op("sync", DMA(dst.t[:], src), writes=[dst.b], dma=True)
        S.op("gpsimd", DMA(poolw.t[:], pool_w_d.rearrange("g c d -> c g d")), writes=[poolw.b], dma=True)
        S.op("gpsimd", DMA(cm.t[:], cm_d), writes=[cm.b], dma=True)

        xres = sb([128, 4, D], F32, nb=4, name="xres")
        xsb = sb([128, 4, D], BF16, nb=4, name="xsb")
        aT_s = sb([128, 4, NSEQ], BF16, name="aT_s")
        xnT = sb([128, KC, UT], BF16, nb=KC, name="xnT")
        xnTs = sb([128, KC, NSEQ], BF16, nb=KC, name="xnTs")
        junk = sb([128, D], BF16, name="junk")
        ssq4 = sb([128, 4], F32, name="ssq4")
        rstd4 = sb([128, 4], F32, name="rstd4")

        b_wbf = {k: Buf() for k in wbf}
        b_indbf = Buf()
        b_kT = [[Buf() for _ in range(4)] for _ in range(NSLOT)]
        b_v = [[Buf() for _ in range(4)] for _ in range(NSLOT)]

        def ln_transpose(xt, P, ntt, gcols, dstT, ptr):
            T = P * ntt
            S.op("vector", MEMSET(ssq4.t[:P, :], 0.0), writes=[ssq4.b])
            for tt in range(ntt):
                S.op("scalar", ACTF(junk.t[:P, :], xt.t[:P, tt, :], AF.Square, accum_out=ssq4.t[:P, tt:tt + 1]),
                     reads=[xt.bs[tt]], writes=[junk.b, ssq4.b])
            S.op("vector", TS(rstd4.t[:P, :ntt], ssq4.t[:P, :ntt], 1.0 / D, EPS, ALU.mult, ALU.add),
                 reads=[ssq4.b], writes=[rstd4.b])
            S.op("scalar", ACTF(rstd4.t[:P, :ntt], rstd4.t[:P, :ntt], AF.Sqrt), reads=[rstd4.b], writes=[rstd4.b])
            S.op("vector", RCP(rstd4.t[:P, :ntt], rstd4.t[:P, :ntt]), reads=[rstd4.b], writes=[rstd4.b])
            for tt in range(ntt):
                S.op("vector", TS(xsb.t[:P, tt, :], xt.t[:P, tt, :], rstd4.t[:P, tt:tt + 1], None, ALU.mult),
                     reads=[xt.bs[tt], rstd4.b], writes=[xsb.bs[tt]])
            for kc in range(KC):
                pt_k = ptr[kc % 2]
                S.op("tensor", TRG([(pt_k.t[:, tt * P:(tt + 1) * P], xsb.t[:P, tt, kc * 128:(kc + 1) * 128],
                                     identb.t[:P, :P]) for tt in range(ntt)]),
                     reads=xsb.bs[:ntt] + [identb.b], writes=[pt_k.b])
                if kc % 2 == 0:
                    S.op("scalar", ACTF(dstT.t[:, kc, :T], pt_k.t[:, :T], AF.Copy, scale=gcols.t[:, kc:kc + 1]),
                         reads=[pt_k.b, gcols.b], writes=[dstT.bs[kc]])
                else:
                    S.op("vector", TS(dstT.t[:, kc, :T], pt_k.t[:, :T], gcols.t[:, kc:kc + 1], None, ALU.mult),
                         reads=[pt_k.b, gcols.b], writes=[dstT.bs[kc]])

        def head_norm(pk, T, gcol, dst_ap, dst_b, sq, rk, pss):
            S.op("scalar", ACTF(sq.t[:, :T], pk.t[:, :T], AF.Square), reads=[pk.b], writes=[sq.b])
            S.op("tensor", MMG([(pss.t[:, :T], bd.t[:], sq.t[:, :T], True, True)]), reads=[bd.b, sq.b], writes=[pss.b])
            S.op("vector", TS(rk.t[:, :T], pss.t[:, :T], 1.0 / HD, EPS, ALU.mult, ALU.add), reads=[pss.b], writes=[rk.b])
            S.op("scalar", ACTF(rk.t[:, :T], rk.t[:, :T], AF.Sqrt), reads=[rk.b], writes=[rk.b])
            S.op("vector", RCP(rk.t[:, :T], rk.t[:, :T]), reads=[rk.b], writes=[rk.b])
            S.op("vector", STT(dst_ap, pk.t[:, :T], gcol.t[:, 0:1], rk.t[:, :T], ALU.mult, ALU.mult),
                 reads=[pk.b, gcol.b, rk.b], writes=[dst_b])

        with ExitStack() as pa:
            sbA, psA = mk(pa)
            wq_s = sbA([128, KC, 512], BF16, name="wq_s")
            sqA = sbA([128, UT], F32, name="sqA")
            rkA = sbA([128, UT], F32, name="rkA")
            ksum = sbA([128, NSEQ, 512], F32, nb=NSEQ, name="ksum")
            ks_tok = sbA([NSEQ, 512], F32, name="ks_tok")
            vs_tok = sbA([NSEQ, 512], F32, name="vs_tok")
            _pb = [psA.bank(), psA.bank()]
            ptr = [_pb[0][1], _pb[1][1]]
            pk = [psA([128, UT], F32, name="pk") for _ in range(2)]
            pss = psA([128, UT], F32, name="pss")
            pv = [psA([128, 512], F32, name="pv") for _ in range(2)]
            pkT = psA([128, 512], F32, name="pkT")
            S.op("tensor", TRG([(pkT.t[:, j * KC:(j + 1) * KC], gstage.t[:, j, :], identf.t[:KC, :KC]) for j in range(5)]),
                 reads=[gstage.b, identf.b], writes=[pkT.b])
            S.op("vector", CP(gmix.t[:], pkT.t[:, 0:KC]), reads=[pkT.b], writes=[gmix.b])
            S.op("vector", CP(gmlp.t[:], pkT.t[:, KC:2 * KC]), reads=[pkT.b], writes=[gmlp.b])
            S.op("vector", CP(gple.t[:], pkT.t[:, 2 * KC:3 * KC]), reads=[pkT.b], writes=[gple.b])
            S.op("vector", CP(pscale.t[:], pkT.t[:, 3 * KC:3 * KC + 4]), reads=[pkT.b], writes=[pscale.b])
            S.op("vector", CP(gq.t[:], pkT.t[:, 4 * KC:4 * KC + 1]), reads=[pkT.b], writes=[gq.b])
            S.op("vector", CP(gk.t[:], pkT.t[:, 4 * KC + 1:4 * KC + 2]), reads=[pkT.b], writes=[gk.b])

            with ExitStack() as pa1:
                sbB, _ = mk(pa1)
                wkv = sbB([128, KC, 1024], BF16, nb=KC, name="wkv")
                for kc in range(KC):
                    S.op("gpsimd", DMA(wkv.t[:, kc, :], w_in_d[kc * 128:(kc + 1) * 128, 512:1536]),
                         writes=[wkv.bs[kc]], dma=True)
                cast_jobs = [(ind_bf, ind_d, b_indbf)]
                for name in ["in", "ao", "po", "out", "up", "down", "pg", "pp"]:
                    src = wsrc[name]
                    ncols = src.shape[1]
                    step = min(ncols, 2048)
                    for c0 in range(0, ncols, step):
                        cast_jobs.append((wbf[name][:, c0:c0 + step], src[:, c0:c0 + step], b_wbf[name]))

                def emit_cast():
                    if cast_jobs:
                        o_, i_, b_ = cast_jobs.pop(0)
                        S.op("gpsimd", DMA(o_, i_), writes=[b_], dma=True)

                xresB = sbB([128, 4, D], F32, nb=4, name="xresB")
                xbufs = [xres, xresB]
                knT = sbB([128, 4, UT], F32, nb=4, name="knT")
                kbf = [sbB([128, UT], BF16, name="kbf") for _ in range(2)]
                vbf = [sbB([128, H, 128], BF16, name="vbf") for _ in range(2)]
                vf = [sbB([128, 512], F32, name="vf") for _ in range(2)]
                kout = [sbB([128, 512], F32, name="kout") for _ in range(2)]
                NKP = 4
                kp = [sbB([128, 2048], F32, name="kp") for _ in range(NKP)]
                kred = [sbB([128, 512], F32, name="kred") for _ in range(2)]
                ptT = sbB([128, NSEQ], I32, name="ptT")
                ptf = sbB([128, NSEQ], F32, name="ptf")
                pidx = sbB([128, NSEQ * 32], I32, name="pidx")
                pidxf = sbB([128, NSEQ * 32], F32, name="pidxf")
                xs_t = sbB([128, 1, D], F32, name="xs_t")
                knTs = sbB([128, 4, NSEQ], F32, nb=4, name="knTs")
                for v in vbf:
                    S.op("vector", MEMSET(v.t[:, :, 64:128], 1.0), writes=[v.b])

                S.op("sync", DMA(ptT.t[:], ptT_d), writes=[ptT.b], dma=True)
                S.op("vector", CP(ptf.t[:], ptT.t[:]), reads=[ptT.b], writes=[ptf.b])
                cidx = sbB([128, 32], F32, name="cidx")
                S.op("sync", DMA(cidx.t[:], cidx_d), writes=[cidx.b], dma=True)
                S.op("vector", TS(ptf.t[:], ptf.t[:], 32.0, None, ALU.mult), reads=[ptf.b], writes=[ptf.b])
                for s in range(NSEQ):
                    S.op("vector", TS(pidxf.t[:, s * 32:(s + 1) * 32], cidx.t[:], ptf.t[:, s:s + 1], None, ALU.add),
                         reads=[ptf.b, cidx.b], writes=[pidxf.b])
                S.op("gpsimd", CP(pidx.t[:], pidxf.t[:]), reads=[pidxf.b], writes=[pidx.b])
                for s in range(NSEQ):
                    S.op("gpsimd", MEMSET(ksum.t[:, s, :], 0.0), writes=[ksum.bs[s]])
                ck_pieces = ck_d.rearrange("(a b) d -> a (b d)", b=32)
                pieces = [(s, c) for s in range(NSEQ) for c in range(32)]
                piece_i = [0]

                g_issued = [0]
                r_done = [0]
                g_limit = [0]

                def piece_gather():
                    i = g_issued[0]
                    g_issued[0] += 1
                    s, c = pieces[i]
                    buf = kp[i % NKP]
                    col = s * 32 + c
                    S.op("gpsimd", IDMA(buf.t[:], ck_pieces, pidx.t[:, col:col + 1]), reads=[pidx.b], writes=[buf.b], dma=True)

                def piece_reduce():
                    i = r_done[0]
                    r_done[0] += 1
                    s, c = pieces[i]
                    buf = kp[i % NKP]
                    kr = kred[i % 2]
                    eng = "vector" if i % 2 == 0 else "gpsimd"
                    if eng == "vector":
                        S.op("vector", RED(kr.t[:], buf.t[:].rearrange("p (t f) -> p f t", f=512)), reads=[buf.b], writes=[kr.b])
                    else:
                        S.op("gpsimd", TT(buf.t[:, 0:1024], buf.t[:, 0:1024], buf.t[:, 1024:2048], ALU.add), reads=[buf.b], writes=[buf.b])
                        S.op("gpsimd", TT(kr.t[:], buf.t[:, 0:512], buf.t[:, 512:1024], ALU.add), reads=[buf.b], writes=[kr.b])
                    S.op("gpsimd", TT(ksum.t[:, s, :], ksum.t[:, s, :], kr.t[:], ALU.add), reads=[kr.b, ksum.bs[s]], writes=[ksum.bs[s]])

                def hook(flush=False):
                    if not (_on("A") and CFG["pieces"]):
                        return
                    while True:
                        did = False
                        if r_done[0] < g_issued[0] and (flush or g_issued[0] - r_done[0] >= NKP - 1 or g_issued[0] >= g_limit[0]):
                            piece_reduce()
                            did = True
                        if g_issued[0] < min(g_limit[0], len(pieces)) and g_issued[0] - r_done[0] < NKP:
                            piece_gather()
                            did = True
                        if not flush or not did:
                            break

                def load_x(slot, dst):
                    for tt in range(4):
                        S.op("sync", DMA(dst.t[:, tt, :], xb[slot * UT + tt * 128: slot * UT + (tt + 1) * 128, :]),
                             writes=[dst.bs[tt]], dma=True)

                def kv_stage(xT, P, ntt, slot):
                    T = P * ntt
                    sample = slot is None
                    kdst = knTs if sample else knT
                    for m in range(4):
                        pkm = pk[m % 2]
                        S.op("tensor", MMG([(pkm.t[:, :T], wkv.t[:, kc, m * 128:(m + 1) * 128], xT.t[:, kc, :T], kc == 0, kc == KC - 1)
                                            for kc in range(KC)]),
                             reads=list(wkv.bs) + list(xT.bs), writes=[pkm.b])
                        head_norm(pkm, T, gk, kdst.t[:, m, :T], kdst.bs[m], sqA, rkA, pss)
                        if not sample:
                            hook()
                            kb = kbf[m % 2]
                            S.op("gpsimd", CP(kb.t[:], knT.t[:, m, :]), reads=[knT.bs[m]], writes=[kb.b])
                            S.op("sync", DMA(kT_s[m * 128:(m + 1) * 128, slot * UT:(slot + 1) * UT], kb.t[:]),
                                 reads=[kb.b], writes=[b_kT[slot][m]], dma=True)
                            S.op("vector", RED(kmT.t[:, m, 2 * slot:2 * slot + 2], knT.t[:, m, :].rearrange("p (b k) -> p b k", k=256)),
                                 reads=[knT.bs[m]], writes=[kmT.b])
                    if sample or slot < 4:
                        for tt in range(ntt):
                            S.op("tensor", TRG([(pkT.t[:P, m * 128:(m + 1) * 128], kdst.t[:, m, tt * P:(tt + 1) * P], identf.t[:])
                                                for m in range(4)]),
                                 reads=list(kdst.bs) + [identf.b], writes=[pkT.b])
                            if sample:
                                S.op("vector", CP(ks_tok.t[:], pkT.t[:P, :]), reads=[pkT.b], writes=[ks_tok.b])
                                S.op("sync", DMA(ks_o, ks_tok.t[:]), reads=[ks_tok.b], dma=True)
                            else:
                                ko = kout[tt % 2]
                                S.op("scalar", ACTF(ko.t[:], pkT.t[:, :], AF.Copy), reads=[pkT.b], writes=[ko.b])
                                S.op("sync", DMA(k_own[slot * UT + tt * 128: slot * UT + (tt + 1) * 128, :], ko.t[:]),
                                     reads=[ko.b], dma=True)
                    for tt in range(ntt):
                        if not sample:
                            hook()
                        pvt = pv[tt % 2]
                        S.op("tensor", MMG([(pvt.t[:P, :], xT.t[:, kc, tt * P:(tt + 1) * P], wkv.t[:, kc, 512:1024], kc == 0, kc == KC - 1)
                                            for kc in range(KC)]),
                             reads=list(wkv.bs) + list(xT.bs), writes=[pvt.b])
                        if sample:
                            S.op("vector", CP(vs_tok.t[:], pvt.t[:P, :]), reads=[pvt.b], writes=[vs_tok.b])
                            S.op("sync", DMA(vs_o, vs_tok.t[:]), reads=[vs_tok.b], dma=True)
                        else:
                            vb = vbf[tt % 2]
                            S.op("scalar", ACTF(vb.t[:, :, 0:64], pvt.t[:, :].rearrange("p (h d) -> p h d", d=HD), AF.Copy),
                                 reads=[pvt.b], writes=[vb.b])
                            S.op("sync", DMA(v_s[slot * UT + tt * 128: slot * UT + (tt + 1) * 128, :], vb.t[:].rearrange("p h c -> p (h c)")),
                                 reads=[vb.b], writes=[b_v[slot][tt]], dma=True)
                            if slot < 4:
                                vv = vf[tt % 2]
                                S.op("vector", CP(vv.t[:], pvt.t[:, :]), reads=[pvt.b], writes=[vv.b])
                                S.op("sync", DMA(v_own[slot * UT + tt * 128: slot * UT + (tt + 1) * 128, :], vv.t[:]),
                                     reads=[vv.b], dma=True)

                NSL = CFG["nslots"] if _on("A") else 0
                if NSL:
                    load_x(0, xbufs[0])
                for slot in range(NSL):
                    if slot + 1 < NSL:
                        load_x(slot + 1, xbufs[(slot + 1) % 2])
                    g_limit[0] = 8 * (slot + 1) + 2
                    hook()
                    if slot >= 1:
                        emit_cast()
                    ln_transpose(xbufs[slot % 2], 128, 4, gmix, xnT, ptr)
                    hook()
                    kv_stage(xnT, 128, 4, slot)
                g_limit[0] = len(pieces)
                hook(flush=True)
                while cast_jobs:
                    emit_cast()
                if _on("A"):
                    S.op("sync", DMA(xs_t.t[:NSEQ, 0, :], xs_d), writes=[xs_t.b], dma=True)
                    ln_transpose(xs_t, NSEQ, 1, gmix, xnTs, ptr)
                    kv_stage(xnTs, NSEQ, 1, None)
                for m in range(4):
                    S.op("vector", CP(kmz.t[0:64, 2 * m, :], kmT.t[0:64, m, :]), reads=[kmT.b], writes=[kmz.b])
                    S.op("vector", CP(kmz.t[64:128, 2 * m + 1, :], kmT.t[64:128, m, :]), reads=[kmT.b], writes=[kmz.b])
                S.barrier()
                S.run_block()

            with ExitStack() as sa:
                sbS, psS = mk(sa)
                qnTs = sbS([128, 4, NSEQ], F32, name="qnTs")
                q_tok = sbS([NSEQ, 512], F32, name="q_tok")
                qrep = sbS([128, 512], F32, name="qrep")
                esel = sbS([NSEQ, NSEQ, 128], F32, name="esel")
                prod = sbS([128, 512], F32, name="prod")
                pgs = sbS([128, H], F32, name="pgs")
                pairm = sbS([128, 64], F32, name="pairm")
                gates = sbS([H, NSEQ, 64], F32, name="gates")
                mx8 = sbS([H, NSEQ, 8], F32, name="mx8")
                pt8i = sbS([H, NSEQ * 128], I32, name="pt8i")
                pt8 = sbS([H, NSEQ * 128], F32, name="pt8")
                oh = sbS([H, 64], F32, name="oh")
                ohp = sbS([H, 64], F32, name="ohp")
                Pm = sbS([H, NSEQ, 6], F32, name="Pm")
                Dm = sbS([H, 48], F32, name="Dm")
                delta = sbS([H, 48], F32, name="delta")
                tokoff = sbS([128, 48], F32, name="tokoff")
                gidxf = sbS([128, NSEQ, 48], F32, name="gidxf")
                gidx = sbS([128, NSEQ, 48], I32, name="gidx")
                KG = [sbS([128, 48, HD], F32, nb=48, name="KG") for _ in range(2)]
                VG = [sbS([128, 48, HD], F32, nb=48, name="VG") for _ in range(2)]
                sprod = sbS([128, 48, HD], F32, name="sprod")
                sc = sbS([128, 48], F32, name="sc")
                pexp = sbS([128, 48], F32, name="pexp")
                den8 = sbS([1, H], F32, name="den8")
                own_p = sbS([NSEQ, 512], F32, name="own_p")
                own_s = sbS([NSEQ, H], F32, name="own_s")
                own_e = sbS([NSEQ, H], F32, name="own_e")
                pmmask = sbS([NSEQ, 32], F32, name="pmmask")
                PM = sbS([NSEQ, NSEQ, H], F32, name="PM")
                a_row = sbS([1, 512], F32, name="a_row")
                rden = sbS([1, H], F32, name="rden")

                pq = pk[0]
                pqT = pkT
                pgt = pv[0]
                pidxp = pv[1]
                pacc = pk[1]
                pden = pss
                paT = pkT

                if _on("samp"):
                    S.op("sync", DMA(wq_s.t[:], wbf["in"].rearrange("(k p) n -> p k n", p=128)[:, :, 0:512]),
                         reads=[b_wbf["in"]], writes=[wq_s.b], dma=True)
                    for m in range(4):
                        S.op("tensor", MMG([(pq.t[:, :NSEQ], wq_s.t[:, kc, m * 128:(m + 1) * 128], xnTs.t[:, kc, :], kc == 0, kc == KC - 1)
                                            for kc in range(KC)]), reads=[wq_s.b] + list(xnTs.bs), writes=[pq.b])
                        head_norm(pq, NSEQ, gq, qnTs.t[:, m, :], qnTs.b, sqA, rkA, pss)
                    S.op("tensor", TRG([(pqT.t[:NSEQ, m * 128:(m + 1) * 128], qnTs.t[:, m, :], identf.t[:]) for m in range(4)]),
                         reads=[qnTs.b, identf.b], writes=[pqT.b])
                    S.op("vector", CP(q_tok.t[:], pqT.t[:NSEQ, :]), reads=[pqT.b], writes=[q_tok.b])
                    S.op("sync", DMA(pairm.t[:], pair_d), writes=[pairm.b], dma=True)
                    S.op("sync", DMA(delta.t[:], delta_d), writes=[delta.b], dma=True)
                    S.op("sync", DMA(tokoff.t[:], tokoff_d), writes=[tokoff.b], dma=True)
                    S.op("sync", DMA(pmmask.t[:], pmmask_d), writes=[pmmask.b], dma=True)
                    S.op("sync", DMA(pt8i.t[:], pt_d.rearrange("s p -> (s p)").partition_broadcast(H)), writes=[pt8i.b], dma=True)
                    S.op("vector", CP(pt8.t[:], pt8i.t[:]), reads=[pt8i.b], writes=[pt8.b])
                    for s in range(NSEQ):
                        S.op("vector", TS(esel.t[:, s, :], ones_f.t[:NSEQ, :], pmmask.t[:, s * H:s * H + 1], None, ALU.mult),
                             reads=[ones_f.b, pmmask.b], writes=[esel.b])
                    for s in range(NSEQ):
                        S.op("tensor", MMG([(pgt.t[:, :], esel.t[:, s, :], q_tok.t[:, :], True, True)]),
                             reads=[esel.b, q_tok.b], writes=[pgt.b])
                        S.op("vector", TT(prod.t[:], ksum.t[:, s, :], pgt.t[:, :], ALU.mult), reads=[ksum.bs[s], pgt.b], writes=[prod.b])
                        S.op("vector", RED(pgs.t[:], prod.t[:].rearrange("p (h d) -> p h d", d=HD)), reads=[prod.b], writes=[pgs.b])
                        S.op("tensor", MMG([(pidxp.t[:H, :64], pgs.t[:], pairm.t[:], True, True)]), reads=[pgs.b, pairm.b], writes=[pidxp.b])
                        S.op("vector", CP(gates.t[:, s, :], pidxp.t[:H, :64]), reads=[pidxp.b], writes=[gates.b])
                        S.op("vector", MAX8(mx8.t[:, s, :], gates.t[:, s, :]), reads=[gates.b], writes=[mx8.b])
                        for i in range(3):
                            S.op("vector", TS(oh.t[:], gates.t[:, s, :], mx8.t[:, s, i:i + 1], None, ALU.is_equal),
                                 reads=[gates.b, mx8.b], writes=[oh.b])
                            for e_ in range(2):
                                ptv = pt8.t[:, s * 128:(s + 1) * 128].rearrange("p (j e) -> p e j", e=2)[:, e_, :]
                                S.op("vector", TT(ohp.t[:], oh.t[:], ptv, ALU.mult), reads=[oh.b, pt8.b], writes=[ohp.b])
                                S.op("vector", RED(Pm.t[:, s, 2 * i + e_:2 * i + e_ + 1], ohp.t[:]), reads=[ohp.b], writes=[Pm.b])
                        S.op("vector", TT(Dm.t[:].rearrange("p (h c) -> p h c", c=6), delta.t[:].rearrange("p (h c) -> p h c", c=6),
                                          Pm.t[:, s, :].unsqueeze(1).to_broadcast([H, H, 6]), ALU.mult),
                             reads=[delta.b, Pm.b], writes=[Dm.b])
                        S.op("tensor", MMG([(pidxp.t[:, 64:112], ones_f.t[:H, :], Dm.t[:], True, True)]), reads=[ones_f.b, Dm.b], writes=[pidxp.b])
                        S.op("vector", STT(gidxf.t[:, s, :], pidxp.t[:, 64:112], 1024.0, tokoff.t[:], ALU.mult, ALU.add),
                             reads=[pidxp.b, tokoff.b], writes=[gidxf.b])
                    S.op("vector", CP(gidx.t[:], gidxf.t[:]), reads=[gidxf.b], writes=[gidx.b])
                    S.op("vector", TT(own_p.t[:], q_tok.t[:], ks_tok.t[:], ALU.mult), reads=[q_tok.b, ks_tok.b], writes=[own_p.b])
                    S.op("vector", RED(own_s.t[:], own_p.t[:].rearrange("p (h d) -> p h d", d=HD)), reads=[own_p.b], writes=[own_s.b])
                    S.op("scalar", ACTF(own_e.t[:], own_s.t[:], AF.Exp, scale=HD ** -0.5), reads=[own_s.b], writes=[own_e.b])
                    S.op("vector", TT(PM.t[:], pmmask.t[:].rearrange("p (s h) -> p s h", h=H),
                                      own_e.t[:].unsqueeze(1).to_broadcast([NSEQ, NSEQ, H]), ALU.mult),
                         reads=[pmmask.b, own_e.b], writes=[PM.b])
                    for s in range(NSEQ):
                        kg = KG[s % 2]
                        vg = VG[s % 2]
                        for c in range(48):
                            S.op("gpsimd", IDMA(kg.t[:, c, :], ck_d, gidx.t[:, s, c:c + 1]), reads=[gidx.b], writes=[kg.bs[c]], dma=True)
                        for c in range(48):
                            S.op("gpsimd", IDMA(vg.t[:, c, :], cv_d, gidx.t[:, s, c:c + 1]), reads=[gidx.b], writes=[vg.bs[c]], dma=True)
                        S.op("tensor", MMG([(pgt.t[:, :], esel.t[:, s, :], q_tok.t[:, :], True, True)]),
                             reads=[esel.b, q_tok.b], writes=[pgt.b])
                        S.op("vector", CP(qrep.t[:], pgt.t[:, :]), reads=[pgt.b], writes=[qrep.b])
                        S.op("vector", TT(sprod.t[:].rearrange("p (h c) d -> p h c d", c=6), kg.t[:].rearrange("p (h c) d -> p h c d", c=6),
                                          qrep.t[:].rearrange("p (h d) -> p h d", d=HD).unsqueeze(2).to_broadcast([128, H, 6, HD]), ALU.mult),
                             reads=list(kg.bs) + [qrep.b], writes=[sprod.b])
                        S.op("vector", RED(sc.t[:], sprod.t[:]), reads=[sprod.b], writes=[sc.b])
                        S.op("scalar", ACTF(pexp.t[:], sc.t[:], AF.Exp, scale=HD ** -0.5), reads=[sc.b], writes=[pexp.b])
                        S.op("tensor", MMG([(pden.t[:1, :48], ones_f.t[:, 0:1], pexp.t[:], True, True)]), reads=[ones_f.b, pexp.b], writes=[pden.b])
                        S.op("vector", RED(den8.t[:], pden.t[:1, :48].rearrange("p (h c) -> p h c", c=6)), reads=[pden.b], writes=[den8.b])
                        S.op("tensor", MMG([(pden.t[:1, 64:64 + H], ones_f.t[:NSEQ, 0:1], PM.t[:, s, :], True, True)]),
                             reads=[ones_f.b, PM.b], writes=[pden.b])
                        S.op("vector", TT(den8.t[:], den8.t[:], pden.t[:1, 64:64 + H], ALU.add), reads=[pden.b, den8.b], writes=[den8.b])
                        S.op("vector", RCP(rden.t[:], den8.t[:]), reads=[den8.b], writes=[rden.b])
                        lst = []
                        for h in range(H):
                            for c6 in range(6):
                                c = h * 6 + c6
                                lst.append((pacc.t[:1, h * HD:(h + 1) * HD], pexp.t[:, c:c + 1], vg.t[:, c, :], c6 == 0, False))
                            lst.append((pacc.t[:1, h * HD:(h + 1) * HD], PM.t[:, s, h:h + 1], vs_tok.t[:, h * HD:(h + 1) * HD], False, True))
                        S.op("tensor", MMG(lst), reads=[pexp.b, PM.b, vs_tok.b] + list(vg.bs), writes=[pacc.b])
                        S.op("vector", TT(a_row.t[:].rearrange("p (h d) -> p h d", d=HD), pacc.t[:1, :].rearrange("p (h d) -> p h d", d=HD),
                                          rden.t[:].unsqueeze(2).to_broadcast([1, H, HD]), ALU.mult),
                             reads=[pacc.b, rden.b], writes=[a_row.b])
                        S.op("tensor", TRG([(paT.t[:, m:m + 1], a_row.t[:1, m * 128:(m + 1) * 128], identf.t[:1, :1]) for m in range(4)]),
                             reads=[a_row.b, identf.b], writes=[paT.b])
                        S.op("vector", CP(aT_s.t[:, :, s], paT.t[:, 0:4]), reads=[paT.b], writes=[aT_s.b])
                S.barrier()
                S.run_block()

        NRING = 6
        ring = [sb([128, 4096], BF16, name="ring") for _ in range(NRING)]
        ring_i = [0]
        tmpf = [sb([128, UT], F32, name="tmpf") for _ in range(4)]

        class Panels:
            def __init__(self, specs, ahead=3):
                self.specs = specs
                self.loaded = {}
                self.next = 0
                self.ahead = ahead

            def _load(self, i):
                name, k0, nk, c0, ncols = self.specs[i]
                r = ring[ring_i[0] % NRING]
                ring_i[0] += 1
                view = r.t[:, 0:nk * ncols].rearrange("p (k c) -> p k c", c=ncols)
                src = wbf[name].rearrange("(k p) n -> p k n", p=128)[:, k0:k0 + nk, c0:c0 + ncols]
                S.op("sync", DMA(view, src), reads=[b_wbf[name]], writes=[r.b], dma=True)
                self.loaded[i] = (view, r.b)

            def get(self, i, oldest=None):
                if oldest is None:
                    oldest = i
                while self.next <= min(i + self.ahead, len(self.specs) - 1) and self.next < oldest + NRING:
                    self._load(self.next)
                    self.next += 1
                assert i in self.loaded and i >= self.next - NRING, (i, self.next)
                return self.loaded[i]

        def unit_pass(J):
            sample = J is None
            P = NSEQ if sample else 128
            ntt = 1 if sample else 4
            T = P * ntt
            xT = xnTs if sample else xnT

            with ExitStack() as s1:
                sb1, ps1 = mk(s1)
                uT = sb1([128, 4, 16 + UT], F32, nb=4, name="uT")
                pl = [sb1([128, 16 + UT], F32, name="pl") for _ in range(2)]
                dT = sb1([128, 4, UT], BF16, nb=4, name="dT")
                bT = sb1([128, 4, UT], BF16, nb=4, name="bT")
                mT = sb1([128, KC, UT], BF16, nb=KC, name="mT")
                _pb = [ps1.bank(), ps1.bank()]
                ptr = [_pb[0][1], _pb[1][1]]
                pA = [ps1([128, UT], F32, name="pA") for _ in range(2)]
                pG = [ps1([128, UT], F32, name="pG") for _ in range(2)]
                pS = [ps1([128, UT], F32, name="pS") for _ in range(2)]
                pacc = _pb[0][0]
                if sample:
                    aT = aT_s
                    aT_bs = [aT_s.b] * 4
                    hist = sb1([15, NSEQ, 512], F32, name="hist")
                    selw = sb1([15, 4], F32, name="selw")
                    u_tok = sb1([NSEQ, 512], F32, name="u_tok")
                else:
                    aTt = sb1([128, 4, UT], BF16, nb=4, name="aT")
                    aT = aTt
                    aT_bs = aTt.bs
                    qnT = sb1([128, 4, UT], F32, nb=4, name="qnT")
                    qaug = sb1([96, H, UT], BF16, nb=H, name="qaug")
                    g1 = sb1([128, 256], F32, name="g1")
                    mx8 = sb1([128, 8], F32, name="mx8")
                    selt = sb1([128, 32], F32, name="selt")
                    negm = sb1([128, 256], F32, name="negm")
                    kb = [sb1([96, 4, UT], BF16, nb=2, name="kb") for _ in range(3)]
                    vb = [sb1([128, 4, 512], BF16, name="vb") for _ in range(3)]
                    pT = [sb1([128, UT], BF16, name="pT") for _ in range(3)]
                    rden = sb1([64, UT], F32, name="rden")
                    xh = sb1([16, 1, D], F32, name="xh")
                    xnTh = sb1([128, KC, 16], BF16, nb=KC, name="xnTh")
                    uo = sb1([15, 512], F32, name="uo")

                if not sample:
                    for tt in range(4):
                        S.op("sync", DMA(xres.t[:, tt, :], xb[J * UT + tt * 128: J * UT + (tt + 1) * 128, :]),
                             writes=[xres.bs[tt]], dma=True)
                    ln_transpose(xres, 128, 4, gmix, xnT, ptr)
                    S.op("sync", DMA(xh.t[:, 0, :], xhalo[J * 16:(J + 1) * 16, :]), writes=[xh.b], dma=True)
                    ln_transpose(xh, 16, 1, gmix, xnTh, ptr)
                else:
                    S.op("sync", DMA(xres.t[:NSEQ, 0, :], xs_d), writes=[xres.bs[0]], dma=True)
                    S.op("sync", DMA(hist.t[:], state_d.rearrange("s r c -> r s c")), writes=[hist.b], dma=True)
                    S.op("sync", DMA(selw.t[:], selw_d), writes=[selw.b], dma=True)

                specs = []
                if not sample:
                    specs.append(("in", 0, KC, 0, 512))
                specs.append(("in", 0, KC, 1536, 512))
                specs += [("ao", 0, 4, 0, 1024), ("po", 0, 4, 0, 1024)]
                specs += [("in", 0, KC, 2048, 512), ("in", 0, KC, 3072, 512), ("in", 0, KC, 2560, 512), ("in", 0, KC, 3584, 512)]
                specs += [("out", 0, KC, 0, 512), ("out", 0, KC, 512, 512)]
                pn = Panels(specs)
                pi = 0

                if not sample:
                    wv, wb = pn.get(pi); pi += 1
                    for m in range(4):
                        pkm = pA[m % 2]
                        S.op("tensor", MMG([(pkm.t[:, :T], wv[:, kc, m * 128:(m + 1) * 128], xT.t[:, kc, :T], kc == 0, kc == KC - 1)
                                            for kc in range(KC)]), reads=[wb] + list(xT.bs), writes=[pkm.b])
                        head_norm(pkm, T, gq, qnT.t[:, m, :], qnT.bs[m], tmpf[0], tmpf[1], pG[0])
                        S.op("scalar", ACTF(qaug.t[0:64, 2 * m, :], qnT.t[0:64, m, :], AF.Copy, scale=HD ** -0.5),
                             reads=[qnT.bs[m]], writes=[qaug.bs[2 * m]])
                        S.op("vector", TS(qaug.t[0:64, 2 * m + 1, :], qnT.t[64:128, m, :], HD ** -0.5, None, ALU.mult),
                             reads=[qnT.bs[m]], writes=[qaug.bs[2 * m + 1]])
                    for tt in range(4):
                        bq = tt // 2
                        gcol = (J * 2 + bq) * 32
                        S.op("tensor", MMG([(pG[1].t[:, h * 32:(h + 1) * 32], qnT.t[:, h // 2, tt * 128:(tt + 1) * 128], kmz.t[:, h, :], True, True)
                                            for h in range(H)]), reads=list(qnT.bs) + [kmz.b], writes=[pG[1].b])
                        S.op("vector", TT(g1.t[:].rearrange("p (h j) -> p h j", j=32), pG[1].t[:, 0:256].rearrange("p (h j) -> p h j", j=32),
                                          gmask.t[:, gcol:gcol + 32].unsqueeze(1).to_broadcast([128, H, 32]), ALU.add),
                             reads=[pG[1].b, gmask.b], writes=[g1.b])
                        for h in range(H):
                            S.op("vector", MAX8(mx8.t[:], g1.t[:, h * 32:(h + 1) * 32]), reads=[g1.b], writes=[mx8.b])
                            S.op("vector", STT(selt.t[:], g1.t[:, h * 32:(h + 1) * 32], mx8.t[:, 2:3], pastind.t[:, gcol:gcol + 32],
                                               ALU.is_ge, ALU.mult), reads=[g1.b, mx8.b, pastind.b], writes=[selt.b])
                            S.op("vector", TT(selt.t[:], selt.t[:], ownind.t[:, gcol:gcol + 32], ALU.max),
                                 reads=[selt.b, ownind.b], writes=[selt.b])
                            S.op("vector", TS(negm.t[:, h * 32:(h + 1) * 32], selt.t[:], -1.0, -NEG, ALU.add, ALU.mult),
                                 reads=[selt.b], writes=[negm.b])
                        for hp in range(4):
                            S.op("tensor", TRG([(pG[0].t[:64, 0:128], negm.t[:, hp * 64:(hp + 1) * 64], identf.t[:])]),
                                 reads=[negm.b, identf.b], writes=[pG[0].b])
                            S.op("vector", CP(qaug.t[64:96, 2 * hp, tt * 128:(tt + 1) * 128], pG[0].t[0:32, 0:128]),
                                 reads=[pG[0].b], writes=[qaug.bs[2 * hp]])
                            S.op("scalar", ACTF(qaug.t[64:96, 2 * hp + 1, tt * 128:(tt + 1) * 128], pG[0].t[32:64, 0:128], AF.Copy),
                                 reads=[pG[0].b], writes=[qaug.bs[2 * hp + 1]])

                wv, wb = pn.get(pi); pi += 1
                for g in range(4):
                    pu = pA[g % 2]
                    S.op("tensor", MMG([(pu.t[:, :T], wv[:, kc, g * 128:(g + 1) * 128], xT.t[:, kc, :T], kc == 0, kc == KC - 1)
                                        for kc in range(KC)]), reads=[wb] + list(xT.bs), writes=[pu.b])
                    S.op("scalar", ACTF(uT.t[:, g, 16:16 + T], pu.t[:, :T], AF.Copy), reads=[pu.b], writes=[uT.bs[g]])
                    if not sample:
                        S.op("tensor", MMG([(pG[0].t[:, :16], wv[:, kc, g * 128:(g + 1) * 128], xnTh.t[:, kc, :], kc == 0, kc == KC - 1)
                                            for kc in range(KC)]), reads=[wb] + list(xnTh.bs), writes=[pG[0].b])
                        S.op("vector", CP(uT.t[:, g, 0:16], pG[0].t[:, :16]), reads=[pG[0].b], writes=[uT.bs[g]])
                if not sample:
                    W = 16 + UT
                    for g in range(4):
                        w = (2, 4, 8, 16)[g]
                        cur = uT.t[:, g, :]
                        cur_b = uT.bs[g]
                        sh = 1
                        k = 0
                        while sh < w:
                            dst = pl[k % 2]
                            S.op("vector", TT(dst.t[:, sh:W], cur[:, sh:W], cur[:, 0:W - sh], ALU.add), reads=[cur_b], writes=[dst.b])
                            S.op("vector", CP(dst.t[:, 0:sh], cur[:, 0:sh]), reads=[cur_b], writes=[dst.b])
                            cur = dst.t[:, :]
                            cur_b = dst.b
                            sh *= 2
                            k += 1
                        if J == 0:
                            S.op("vector", TT(cur[:, 16:32], cur[:, 16:32], corr.t[:, g * 16:(g + 1) * 16], ALU.mult),
                                 reads=[cur_b, corr.b], writes=[cur_b])
                        S.op("vector", STT(dT.t[:, g, :], cur[:, 16:W], 1.0 / w, uT.t[:, g, 16:W], ALU.mult, ALU.subtract),
                             reads=[cur_b, uT.bs[g]], writes=[dT.bs[g]])
                    if J == 3:
                        S.op("tensor", TRG([(pG[1].t[:15, g * 128:(g + 1) * 128], uT.t[:, g, UT + 1:UT + 16], identf.t[:]) for g in range(4)]),
                             reads=list(uT.bs) + [identf.b], writes=[pG[1].b])
                        S.op("vector", CP(uo.t[:], pG[1].t[:15, :]), reads=[pG[1].b], writes=[uo.b])
                        S.op("sync", DMA(pool_o, uo.t[:]), reads=[uo.b], dma=True)
                else:
                    S.op("tensor", MMG([(pG[1].t[:, g * NSEQ + s:g * NSEQ + s + 1], hist.t[:, s, g * 128:(g + 1) * 128], selw.t[:, g:g + 1], True, True)
                                        for g in range(4) for s in range(NSEQ)]), reads=[hist.b, selw.b], writes=[pG[1].b])
                    for g in range(4):
                        w = (2, 4, 8, 16)[g]
                        S.op("vector", STT(dT.t[:, g, :NSEQ], uT.t[:, g, 16:16 + NSEQ], 1.0 / w - 1.0, pG[1].t[:, g * NSEQ:(g + 1) * NSEQ],
                                           ALU.mult, ALU.add), reads=[uT.bs[g], pG[1].b], writes=[dT.bs[g]])
                    S.op("sync", DMA(pools_o[:, 0:14, :], state_d[:, 1:15, :]), dma=True)
                    S.op("tensor", TRG([(pG[0].t[:NSEQ, g * 128:(g + 1) * 128], uT.t[:, g, 16:16 + NSEQ], identf.t[:]) for g in range(4)]),
                         reads=list(uT.bs) + [identf.b], writes=[pG[0].b])
                    S.op("vector", CP(u_tok.t[:], pG[0].t[:NSEQ, :]), reads=[pG[0].b], writes=[u_tok.b])
                    S.op("sync", DMA(pools_o[:, 14, :], u_tok.t[:]), reads=[u_tok.b], dma=True)
                for g in range(4):
                    pb = pA[g % 2]
                    S.op("tensor", MMG([(pb.t[:, :T], poolw.t[:, g, :], dT.t[:, g, :T], True, True)]), reads=[poolw.b, dT.bs[g]], writes=[pb.b])
                    S.op("scalar", ACTF(bT.t[:, g, :T], pb.t[:, :T], AF.Copy, scale=pscale.t[:, g:g + 1]),
                         reads=[pb.b, pscale.b], writes=[bT.bs[g]])

                if not sample:
                    slots = list(range(J + 1)) + list(range(4, 4 + 3 * (J + 1)))
                    accs = [pA[0], pA[1], pG[0], pG[1]]
                    scs = [pS[0], pS[1], pacc]
                    nsl = len(slots)
                    ldi = [0]

                    def load_kv(hg, sl):
                        kbuf = kb[ldi[0] % 3]
                        vbuf = vb[ldi[0] % 3]
                        ldi[0] += 1
                        S.op("sync", DMA(kbuf.t[0:64, :, :], kT_s.rearrange("(h d) k -> d h k", d=HD)[:, hg * 4:hg * 4 + 4, sl * UT:(sl + 1) * UT]),
                             reads=b_kT[sl], writes=[kbuf.bs[0]], dma=True)
                        S.op("sync", DMA(kbuf.t[64:96, :, :], ind_bf[sl * 32:(sl + 1) * 32, :].rearrange("r (a k) -> r a k", a=4)),
                             reads=[b_indbf], writes=[kbuf.bs[1]], dma=True)
                        S.op("sync", DMA(vbuf.t[:, :, :], v_s[sl * UT:(sl + 1) * UT, hg * 512:(hg + 1) * 512].rearrange("(t p) c -> p t c", p=128)),
                             reads=b_v[sl], writes=[vbuf.b], dma=True)
                        return kbuf, vbuf

                    for hg in range(2):
                        steps = []
                        bufs = {}
                        order = [(li, sl) for li, sl in enumerate(slots)]
                        bufs[0] = load_kv(hg, order[0][1])
                        for li, sl in order:
                            for tau in range(4):
                                for hh in range(4):
                                    steps.append((li, sl, tau, hh))
                        n = len(steps)

                        def emit_score(idx):
                            li, sl, tau, hh = steps[idx]
                            if tau == 0 and hh == 0 and li + 1 < nsl:
                                bufs[li + 1] = load_kv(hg, order[li + 1][1])
                            kbuf, vbuf = bufs[li]
                            h = hg * 4 + hh
                            diag = (sl == J)
                            psc = scs[idx % 3]
                            lst = [(psc.t[:, :], kbuf.t[0:96, hh, tau * 128:(tau + 1) * 128], qaug.t[0:96, h, :], True, not diag)]
                            rd = [kbuf.bs[0], kbuf.bs[1], qaug.bs[h]]
                            if diag:
                                lst.append((psc.t[:, :], identb.t[:], cm.t[:, tau * UT:(tau + 1) * UT], False, True))
                                rd += [identb.b, cm.b]
                            S.op("tensor", MMG(lst), reads=rd, writes=[psc.b])
                            pt_ = pT[idx % 3]
                            S.op("scalar", ACTF(pt_.t[:], psc.t[:, :], AF.Exp), reads=[psc.b], writes=[pt_.b])

                        def emit_pv(idx):
                            li, sl, tau, hh = steps[idx]
                            kbuf, vbuf = bufs[li]
                            pt_ = pT[idx % 3]
                            S.op("tensor", MMG([(accs[hh].t[:, :], vbuf.t[:, tau, hh * 128:(hh + 1) * 128], pt_.t[:],
                                                 li == 0 and tau == 0, li == nsl - 1 and tau == 3)]),
                                 reads=[vbuf.b, pt_.b], writes=[accs[hh].b])

                        LOOK = 2
                        for idx in range(n + LOOK):
                            if idx < n:
                                emit_score(idx)
                            if idx >= LOOK:
                                emit_pv(idx - LOOK)
                        for hh in range(4):
                            h = hg * 4 + hh
                            m, e_ = h // 2, h % 2
                            S.op("vector", RCP(rden.t[:], accs[hh].t[64:128, :]), reads=[accs[hh].b], writes=[rden.b])
                            S.op("vector", TT(aT.t[64 * e_:64 * e_ + 64, m, :], accs[hh].t[0:64, :], rden.t[:], ALU.mult),
                                 reads=[accs[hh].b, rden.b], writes=[aT_bs[m]])

                pi_ao = pi
                wao, wao_b = pn.get(pi_ao, oldest=pi_ao)
                wpo, wpo_b = pn.get(pi_ao + 1, oldest=pi_ao)
                for m in range(KC):
                    half = m // 4
                    gav, gab = pn.get(pi_ao + 2 + 2 * half, oldest=pi_ao)
                    gbv, gbb = pn.get(pi_ao + 3 + 2 * half, oldest=pi_ao)
                    pa_, pg_ = pA[0], pG[0]
                    S.op("tensor", MMG([(pa_.t[:, :T], wao[:, kc, m * 128:(m + 1) * 128], aT.t[:, kc, :T], kc == 0, kc == 3) for kc in range(4)]),
                         reads=[wao_b] + list(aT_bs), writes=[pa_.b])
                    S.op("tensor", MMG([(pg_.t[:, :T], gav[:, kc, (m % 4) * 128:(m % 4 + 1) * 128], xT.t[:, kc, :T], kc == 0, kc == KC - 1)
                                        for kc in range(KC)]), reads=[gab] + list(xT.bs), writes=[pg_.b])
                    S.op("scalar", ACTF(tmpf[0].t[:, :T], pg_.t[:, :T], AF.Sigmoid), reads=[pg_.b], writes=[tmpf[0].b])
                    S.op("vector", TT(tmpf[1].t[:, :T], pa_.t[:, :T], tmpf[0].t[:, :T], ALU.mult), reads=[pa_.b, tmpf[0].b], writes=[tmpf[1].b])
                    pb_, ph_ = pA[1], pG[1]
                    S.op("tensor", MMG([(pb_.t[:, :T], wpo[:, kc, m * 128:(m + 1) * 128], bT.t[:, kc, :T], kc == 0, kc == 3) for kc in range(4)]),
                         reads=[wpo_b] + list(bT.bs), writes=[pb_.b])
                    S.op("tensor", MMG([(ph_.t[:, :T], gbv[:, kc, (m % 4) * 128:(m % 4 + 1) * 128], xT.t[:, kc, :T], kc == 0, kc == KC - 1)
                                        for kc in range(KC)]), reads=[gbb] + list(xT.bs), writes=[ph_.b])
                    S.op("scalar", ACTF(tmpf[2].t[:, :T], ph_.t[:, :T], AF.Sigmoid), reads=[ph_.b], writes=[tmpf[2].b])
                    S.op("vector", TT(tmpf[3].t[:, :T], pb_.t[:, :T], tmpf[2].t[:, :T], ALU.mult), reads=[pb_.b, tmpf[2].b], writes=[tmpf[3].b])
                    S.op("vector", TT(mT.t[:, m, :T], tmpf[1].t[:, :T], tmpf[3].t[:, :T], ALU.add),
                         reads=[tmpf[1].b, tmpf[3].b], writes=[mT.bs[m]])
                pi = pi_ao + 6
                for n in range(2):
                    wv, wb = pn.get(pi); pi += 1
                    for tt in range(ntt):
                        po = pS[tt % 2]
                        S.op("tensor", MMG([(po.t[:P, :], mT.t[:, kc, tt * P:(tt + 1) * P], wv[:, kc, :], kc == 0, kc == KC - 1) for kc in range(KC)]),
                             reads=[wb] + list(mT.bs), writes=[po.b])
                        S.op("vector", TT(xres.t[:P, tt, n * 512:(n + 1) * 512], po.t[:P, :], xres.t[:P, tt, n * 512:(n + 1) * 512], ALU.add),
                             reads=[po.b, xres.bs[tt]], writes=[xres.bs[tt]])
                S.barrier()
                S.run_block()

            with ExitStack() as s2:
                sb2, ps2 = mk(s2)
                hT = sb2([128, 32, UT], BF16, nb=32, name="hT")
                pt_tok = sb2([128, 4, 256], F32, name="pt_tok")
                pt_bf = sb2([128, 4, 256], BF16, name="pt_bf")
                ppT = sb2([128, 2, UT], BF16, nb=2, name="ppT")
                yt = [sb2([128, 512], F32, name="yt") for _ in range(2)]
                _pb = [ps2.bank(), ps2.bank()]
                ptr = [_pb[0][1], _pb[1][1]]
                pH = [ps2([128, UT], F32, name="pH") for _ in range(2)]
                pD = [ps2([128, 512], F32, name="pD") for _ in range(4)]

                psrc = ps_d if sample else p_own
                for tt in range(ntt):
                    r0 = 0 if sample else J * UT + tt * 128
                    S.op("sync", DMA(pt_tok.t[:P, tt, :], psrc[r0:r0 + P, :]), writes=[pt_tok.b], dma=True)
                ln_transpose(xres, P, ntt, gmlp, xT, ptr)
                specs = [("up", 0, KC, c * 512, 512) for c in range(8)]
                specs += [("down", q * 8, 8, n * 512, 512) for n in range(2) for q in range(4)]
                specs += [("pg", 0, KC, 0, 512), ("pg", 0, KC, 512, 512), ("pp", 0, 2, 0, 1024)]
                pn = Panels(specs)
                pi = 0
                for c in range(8):
                    wv, wb = pn.get(pi); pi += 1
                    for mm in range(4):
                        ph = pH[mm % 2]
                        S.op("tensor", MMG([(ph.t[:, :T], wv[:, kc, mm * 128:(mm + 1) * 128], xT.t[:, kc, :T], kc == 0, kc == KC - 1)
                                            for kc in range(KC)]), reads=[wb] + list(xT.bs), writes=[ph.b])
                        tf = tmpf[mm % 2]
                        S.op("scalar", ACTF(tf.t[:, :T], ph.t[:, :T], AF.Relu), reads=[ph.b], writes=[tf.b])
                        S.op("vector", TT(hT.t[:, c * 4 + mm, :T], tf.t[:, :T], tf.t[:, :T], ALU.mult), reads=[tf.b], writes=[hT.bs[c * 4 + mm]])
                for n in range(2):
                    for q in range(4):
                        wv, wb = pn.get(pi); pi += 1
                        for tt in range(ntt):
                            S.op("tensor", MMG([(pD[tt].t[:P, :], hT.t[:, q * 8 + kc, tt * P:(tt + 1) * P], wv[:, kc, :],
                                                 q == 0 and kc == 0, q == 3 and kc == 7) for kc in range(8)]),
                                 reads=[wb] + hT.bs[q * 8:(q + 1) * 8], writes=[pD[tt].b])
                    for tt in range(ntt):
                        S.op("vector", TT(xres.t[:P, tt, n * 512:(n + 1) * 512], pD[tt].t[:P, :], xres.t[:P, tt, n * 512:(n + 1) * 512], ALU.add),
                             reads=[pD[tt].b, xres.bs[tt]], writes=[xres.bs[tt]])
                ln_transpose(xres, P, ntt, gple, xT, ptr)
                for tt in range(ntt):
                    S.op("vector", CP(pt_bf.t[:P, tt, :], pt_tok.t[:P, tt, :]), reads=[pt_tok.b], writes=[pt_bf.b])
                for kc in range(2):
                    S.op("tensor", TRG([(ptr[kc].t[:, tt * P:(tt + 1) * P], pt_bf.t[:P, tt, kc * 128:(kc + 1) * 128], identb.t[:P, :P])
                                        for tt in range(ntt)]), reads=[pt_bf.b, identb.b], writes=[ptr[kc].b])
                    S.op("vector", CP(ppT.t[:, kc, :T], ptr[kc].t[:, :T]), reads=[ptr[kc].b], writes=[ppT.bs[kc]])
                wg = [pn.get(pi, oldest=pi), pn.get(pi + 1, oldest=pi)]
                wpp, wpp_b = pn.get(pi + 2, oldest=pi)
                pi += 3
                k = 0
                for tt in range(ntt):
                    for n in range(2):
                        pg_, pp_ = pD[0 + (k % 2) * 2], pD[1 + (k % 2) * 2]
                        S.op("tensor", MMG([(pg_.t[:P, :], xT.t[:, kc, tt * P:(tt + 1) * P], wg[n][0][:, kc, :], kc == 0, kc == KC - 1)
                                            for kc in range(KC)]), reads=[wg[n][1]] + list(xT.bs), writes=[pg_.b])
                        S.op("tensor", MMG([(pp_.t[:P, :], ppT.t[:, kc, tt * P:(tt + 1) * P], wpp[:, kc, n * 512:(n + 1) * 512], kc == 0, kc == 1)
                                            for kc in range(2)]), reads=[wpp_b] + list(ppT.bs), writes=[pp_.b])
                        tf = tmpf[k % 2]
                        S.op("scalar", ACTF(tf.t[:P, :], pg_.t[:P, :], AF.Sigmoid), reads=[pg_.b], writes=[tf.b])
                        y = yt[k % 2]
                        S.op("vector", TT(y.t[:P, :], pp_.t[:P, :], tf.t[:P, :], ALU.mult), reads=[pp_.b, tf.b], writes=[y.b])
                        S.op("vector", TT(y.t[:P, :], y.t[:P, :], xres.t[:P, tt, n * 512:(n + 1) * 512], ALU.add),
                             reads=[y.b, xres.bs[tt]], writes=[y.b])
                        if sample:
                            dst = ys_o[:, n * 512:(n + 1) * 512]
                        else:
                            dst = y_own[J * UT + tt * 128: J * UT + (tt + 1) * 128, n * 512:(n + 1) * 512]
                        S.op("sync", DMA(dst, y.t[:P, :]), reads=[y.b], dma=True)
                        k += 1
                S.barrier()
                S.run_block()

        if _on("uS"):
            unit_pass(None)
        for J in range(4):
            if _on("u%d" % J):
                unit_pass(J)
        S.barrier()
        S.run_block()
    return nc


_NC_CACHE = {}


def _core_consts(r):
    own = [4 * J + r for J in range(4)]
    nonown = [u for u in range(16) if u % 4 != r]
    slot_units = own + nonown
    gmask = np.zeros((4, 2, 32), np.float32)
    pastind = np.zeros((4, 2, 32), np.float32)
    ownind = np.zeros((4, 2, 32), np.float32)
    for J in range(4):
        for bq in range(2):
            ob_q = 2 * own[J] + bq
            for sl in range(16):
                for be in range(2):
                    ob = 2 * slot_units[sl] + be
                    rho = 2 * sl + be
                    if ob < ob_q:
                        pastind[J, bq, rho] = 1.0
                    else:
                        gmask[J, bq, rho] = -1e30
                    if ob == ob_q:
                        ownind[J, bq, rho] = 1.0
    corr = np.ones((4, 16), np.float32)
    if r == 0:
        for g, w in enumerate((2, 4, 8, 16)):
            for t in range(16):
                corr[g, t] = w / min(w, t + 1)
    rep = lambda a: np.ascontiguousarray(np.broadcast_to(a.reshape(1, -1), (128, a.size))).astype(np.float32)
    return slot_units, rep(gmask), rep(pastind), rep(ownind), rep(corr)


def _static_consts():
    k = np.arange(128)[:, None, None]
    tau = np.arange(4)[None, :, None]
    q = np.arange(UT)[None, None, :]
    cm = np.where(128 * tau + k <= q, 0.0, NEG).astype(np.float32).reshape(128, 4 * UT)
    ind = np.zeros((NSLOT, 32, UT), np.float32)
    for sl in range(NSLOT):
        ind[sl, 2 * sl, 0:256] = 1.0
        ind[sl, 2 * sl + 1, 256:512] = 1.0
    ind = np.ascontiguousarray(np.broadcast_to(ind.reshape(NSLOT * 32, 1, UT), (NSLOT * 32, 4, UT))).reshape(NSLOT * 32, 4 * UT)
    pairm = np.zeros((128, 64), np.float32)
    pairm[np.arange(128), np.arange(128) // 2] = 1.0
    tokoff = (np.arange(128)[:, None] * 8 + (np.arange(48)[None, :] // 6)).astype(np.float32)
    delta = np.zeros((8, 48), np.float32)
    for h in range(8):
        delta[h, h * 6:(h + 1) * 6] = 1.0
    pmmask = np.zeros((4, 32), np.float32)
    for s in range(4):
        pmmask[s, s * 8:(s + 1) * 8] = 1.0
    selw = np.zeros((15, 4), np.float32)
    for g, w in enumerate((2, 4, 8, 16)):
        selw[16 - w:, g] = 1.0 / w
    cidx = np.ascontiguousarray(np.broadcast_to(np.arange(32, dtype=np.float32)[None, :], (128, 32)))
    return dict(cm=cm, ind=ind, pairm=pairm, tokoff=tokoff, delta=delta, pmmask=pmmask, selw=selw, cidx=cidx)


def kernel(x_prompt, x_sample, cache_k, cache_v, state_pool, page_table, p_prompt, p_sample, ln_mix, w_in,
           q_norm, k_norm, pool_w, pool_scale, w_attn_out, w_pool_out, w_out, ln_mlp, w_up, w_down, ln_ple,
           w_ple_gate, w_ple_proj):
    f = lambda a: np.ascontiguousarray(np.asarray(a, dtype=np.float32))
    x_prompt = f(x_prompt); x_sample = f(x_sample); p_prompt = f(p_prompt); p_sample = f(p_sample)
    ck = f(cache_k).reshape(-1, HD)
    cv = f(cache_v).reshape(-1, HD)
    page_table = np.ascontiguousarray(np.asarray(page_table, dtype=np.int32))
    state_pool = f(state_pool)
    if "nc" not in _NC_CACHE:
        _NC_CACHE["nc"] = build_nc()
    nc = _NC_CACHE["nc"]
    st = _static_consts()
    shared = dict(
        cache_k=ck, cache_v=cv, ln_mix=f(ln_mix)[0], w_in=f(w_in)[0], q_norm=f(q_norm)[0], k_norm=f(k_norm)[0],
        pool_w=f(pool_w)[0], pool_scale=f(pool_scale)[0], w_attn_out=f(w_attn_out)[0], w_pool_out=f(w_pool_out)[0],
        w_out=f(w_out)[0], ln_mlp=f(ln_mlp)[0], w_up=f(w_up)[0], w_down=f(w_down)[0], ln_ple=f(ln_ple)[0],
        w_ple_gate=f(w_ple_gate)[0], w_ple_proj=f(w_ple_proj)[0], **st)
    in_maps = []
    layouts = {}
    cores = list(CFG.get("cores", range(8)))
    for c in cores:
        b, r = c // 4, c % 4
        slot_units, gmask, pastind, ownind, corr = _core_consts(r)
        xb = np.concatenate([x_prompt[b, u * UT:(u + 1) * UT] for u in slot_units], axis=0)
        xhalo = np.zeros((64, D), np.float32)
        for J in range(4):
            u = slot_units[J]
            if u > 0:
                xhalo[J * 16:(J + 1) * 16] = x_prompt[b, u * UT - 16:u * UT]
        p_own = np.concatenate([p_prompt[0, b, slot_units[J] * UT:(slot_units[J] + 1) * UT] for J in range(4)], axis=0)
        pt = page_table[4 * c:4 * c + 4]
        m = dict(shared)
        m.update(xb=xb, xhalo=xhalo, p_own=np.ascontiguousarray(p_own), xs=np.ascontiguousarray(x_sample[4 * c:4 * c + 4, 0]),
                 ps=np.ascontiguousarray(p_sample[0, 4 * c:4 * c + 4, 0]), ptT=np.ascontiguousarray(pt.T), pt=np.ascontiguousarray(pt),
                 state=np.ascontiguousarray(state_pool[0, 4 * c:4 * c + 4]), gmask=gmask, pastind=pastind, ownind=ownind, corr=corr)
        in_maps.append(m)
        layouts[c] = slot_units
    res = run_bass_kernel_spmd(nc, in_maps, core_ids=list(range(len(cores))))
    outs = res.results
    y_prompt = np.zeros((2, SEQ, D), np.float32)
    k_prompt = np.zeros((1, 2, SEQ, H, HD), np.float32)
    v_prompt = np.zeros((1, 2, SEQ, H, HD), np.float32)
    pool_prompt = np.zeros((1, 2, 15, 512), np.float32)
    y_sample = np.zeros((32, 1, D), np.float32)
    k_sample = np.zeros((1, 32, 1, H, HD), np.float32)
    v_sample = np.zeros((1, 32, 1, H, HD), np.float32)
    pool_sample = np.zeros((1, 32, 15, 512), np.float32)
    for ci, c in enumerate(cores):
        b, r = c // 4, c % 4
        o = outs[ci]
        for J in range(4):
            u = layouts[c][J]
            y_prompt[b, u * UT:(u + 1) * UT] = o["y_own"][J * UT:(J + 1) * UT]
            k_prompt[0, b, u * UT:(u + 1) * UT] = o["k_own"][J * UT:(J + 1) * UT].reshape(UT, H, HD)
            v_prompt[0, b, u * UT:(u + 1) * UT] = o["v_own"][J * UT:(J + 1) * UT].reshape(UT, H, HD)
        if r == 3:
            pool_prompt[0, b] = o["pool_o"]
        y_sample[4 * c:4 * c + 4, 0] = o["ys_o"]
        k_sample[0, 4 * c:4 * c + 4, 0] = o["ks_o"].reshape(4, H, HD)
        v_sample[0, 4 * c:4 * c + 4, 0] = o["vs_o"].reshape(4, H, HD)
        pool_sample[0, 4 * c:4 * c + 4] = o["pools_o"]
    return (y_prompt, y_sample, k_prompt, v_prompt, pool_prompt, k_sample, v_sample, pool_sample)
```

```python
import numpy as np
from contextlib import ExitStack
import concourse.bass as bass
import concourse.mybir as mybir
from concourse.bass_utils import run_bass_kernel_spmd

F32 = mybir.dt.float32
BF16 = mybir.dt.bfloat16
I32 = mybir.dt.int32
ALU = mybir.AluOpType
AF = mybir.ActivationFunctionType
AX = mybir.AxisListType

D = 1024
KC = 8
H = 8
HD = 64
UT = 512
NSLOT = 16
SEQ = 8192
EPS = 1e-6
NEG = -30000.0
N_PHYS = 5120
CFG = {"n_phys": 5120, "stop": None, "nslots": 16, "pieces": True}
ORDER = ["p0", "A", "samp", "uS", "u0", "u1", "u2", "u3"]


def _on(name):
    st = CFG.get("stop")
    return True if st is None else ORDER.index(name) <= ORDER.index(st)
NSEQ = 4

ENGS = ["sync", "scalar", "vector", "gpsimd", "tensor"]
N_DMA_SLOTS = {"sync": 16, "gpsimd": 8}


class Buf:
    __slots__ = ("last_write", "readers", "excl")

    def __init__(self):
        self.last_write = None
        self.readers = {}
        self.excl = False


class Sched:
    def __init__(self, nc, stack):
        self.nc = nc
        self.q = {e: [] for e in ENGS}
        self.sem = {}
        for e in ["scalar", "vector", "gpsimd", "tensor"]:
            self.sem[e] = stack.enter_context(nc.semaphore("p_" + e))
        self.cnt = {e: 0 for e in self.sem}
        self.dsem = {}
        self.dcnt = {}
        self.dnext = {}
        for e, n in N_DMA_SLOTS.items():
            for i in range(n):
                self.dsem[(e, i)] = stack.enter_context(nc.semaphore(f"d_{e}{i}"))
                self.dcnt[(e, i)] = 0
            self.dnext[e] = 0
        self.waited = {e: {} for e in ENGS}

    def _semobj(self, key):
        return self.sem[key] if key in self.sem else self.dsem[key]

    def op(self, eng, fn, reads=(), writes=(), dma=False):
        deps = {}
        ex = [b for b in reads if b.excl]
        if ex:
            reads = [b for b in reads if not b.excl]
            writes = list(writes) + [b for b in ex if b not in writes]

        def add(h):
            if h is None:
                return
            k, v = h
            if deps.get(k, 0) < v:
                deps[k] = v

        for b in reads:
            add(b.last_write)
        for b in writes:
            add(b.last_write)
            for k, v in b.readers.items():
                add((k, v))
        if dma:
            slot = self.dnext[eng]
            self.dnext[eng] = (slot + 1) % N_DMA_SLOTS[eng]
            key = (eng, slot)
            if self.dcnt[key] > 0:
                add((key, self.dcnt[key]))
            self.dcnt[key] += 16
            h = (key, self.dcnt[key])
            inc = 16
        else:
            key = eng
            self.cnt[eng] += 1
            h = (key, self.cnt[eng])
            inc = 1
        waits = []
        w = self.waited[eng]
        for k, v in deps.items():
            if k == "tensor" and eng == "tensor":
                continue
            if w.get(k, 0) >= v:
                continue
            w[k] = v
            waits.append((self._semobj(k), v))
        semo = self._semobj(key)

        def emit(e, waits=waits, fn=fn, semo=semo, inc=inc):
            for s, v in waits:
                e.wait_ge(s, v)
            ins = fn(e)
            ins.then_inc(semo, inc)

        self.q[eng].append(emit)
        for b in writes:
            b.last_write = h
            b.readers = {}
        for b in reads:
            if b.readers.get(h[0], 0) < h[1]:
                b.readers[h[0]] = h[1]
        return h

    def barrier(self):
        targets = [(k, v) for k, v in self.cnt.items() if v > 0]
        targets += [(k, v) for k, v in self.dcnt.items() if v > 0]
        for eng in ENGS:
            w = self.waited[eng]
            waits = []
            for k, v in targets:
                if w.get(k, 0) >= v:
                    continue
                w[k] = v
                waits.append((self._semobj(k), v))

            def emit(e, waits=waits):
                for s, v in waits:
                    e.wait_ge(s, v)

            self.q[eng].append(emit)

    def run_block(self):
        nc = self.nc
        q = self.q
        with nc.Block() as block:
            @block.sync
            def _(e):
                for c in q["sync"]:
                    c(e)

            @block.scalar
            def _(e):
                for c in q["scalar"]:
                    c(e)

            @block.vector
            def _(e):
                for c in q["vector"]:
                    c(e)

            @block.gpsimd
            def _(e):
                for c in q["gpsimd"]:
                    c(e)

            @block.tensor
            def _(e):
                for c in q["tensor"]:
                    c(e)
        self.q = {e: [] for e in ENGS}


class TB:
    def __init__(self, t, nb=1):
        self.t = t
        self.bs = [Buf() for _ in range(nb)]

    @property
    def b(self):
        return self.bs[0]


def DMA(out, in_):
    return lambda e: e.dma_start(out=out, in_=in_)


def IDMA(out, in_, idx):
    return lambda e: e.indirect_dma_start(out=out, out_offset=None, in_=in_,
                                          in_offset=bass.IndirectOffsetOnAxis(ap=idx, axis=0))


def MMG(lst):
    def f(e):
        r = None
        for (ps, lhsT, rhs, st, sp) in lst:
            r = e.matmul(ps, lhsT=lhsT, rhs=rhs, start=st, stop=sp)
        return r
    return f


def TRG(lst):
    def f(e):
        r = None
        for (out, in_, ident) in lst:
            r = e.transpose(out=out, in_=in_, identity=ident)
        return r
    return f


def ACTF(out, in_, func, scale=None, accum_out=None):
    kw = {}
    if scale is not None:
        kw["scale"] = scale
    if accum_out is not None:
        kw["accum_out"] = accum_out
    return lambda e: e.activation(out=out, in_=in_, func=func, **kw)


def TT(out, in0, in1, op):
    return lambda e: e.tensor_tensor(out=out, in0=in0, in1=in1, op=op)


def TS(out, in0, s1, s2, op0, op1=None):
    if op1 is None:
        return lambda e: e.tensor_scalar(out=out, in0=in0, scalar1=s1, scalar2=None, op0=op0)
    return lambda e: e.tensor_scalar(out=out, in0=in0, scalar1=s1, scalar2=s2, op0=op0, op1=op1)


def STT(out, in0, scalar, in1, op0, op1):
    return lambda e: e.scalar_tensor_tensor(out=out, in0=in0, scalar=scalar, in1=in1, op0=op0, op1=op1)


def RED(out, in_, op=ALU.add):
    return lambda e: e.tensor_reduce(out=out, in_=in_, axis=AX.X, op=op)


def CP(out, in_):
    return lambda e: e.tensor_copy(out=out, in_=in_)


def RCP(out, in_):
    return lambda e: e.reciprocal(out=out, in_=in_)


def MAX8(out, in_):
    return lambda e: e.max(out=out, in_=in_)


def MEMSET(ap, v):
    return lambda e: e.memset(ap, v)


def build_nc():
    nc = bass.Bass("TRN2", target_bir_lowering=False)
    uid = [0]

    def din(name, shape, dt=F32):
        return nc.dram_tensor(name, list(shape), dt, kind="ExternalInput").ap()

    def dout(name, shape, dt=F32):
        return nc.dram_tensor(name, list(shape), dt, kind="ExternalOutput").ap()

    def dscr(name, shape, dt):
        return nc.dram_tensor(name, list(shape), dt).ap()

    xb = din("xb", [SEQ, D])
    xhalo = din("xhalo", [64, D])
    p_own = din("p_own", [4 * UT, 256])
    xs_d = din("xs", [NSEQ, D])
    ps_d = din("ps", [NSEQ, 256])
    ptT_d = din("ptT", [128, NSEQ], I32)
    pt_d = din("pt", [NSEQ, 128], I32)
    state_d = din("state", [NSEQ, 15, 512])
    NP_ = CFG["n_phys"]
    ck_d = din("cache_k", [NP_ * 1024, HD])
    cv_d = din("cache_v", [NP_ * 1024, HD])
    ln_mix_d = din("ln_mix", [D])
    w_in_d = din("w_in", [D, 4096])
    q_norm_d = din("q_norm", [HD])
    k_norm_d = din("k_norm", [HD])
    pool_w_d = din("pool_w", [4, 128, 128])
    pool_scale_d = din("pool_scale", [512])
    w_ao_d = din("w_attn_out", [512, D])
    w_po_d = din("w_pool_out", [512, D])
    w_out_d = din("w_out", [D, D])
    ln_mlp_d = din("ln_mlp", [D])
    w_up_d = din("w_up", [D, 4096])
    w_down_d = din("w_down", [4096, D])
    ln_ple_d = din("ln_ple", [D])
    w_pg_d = din("w_ple_gate", [D, D])
    w_pp_d = din("w_ple_proj", [256, D])
    cm_d = din("cm", [128, 4 * UT])
    ind_d = din("ind", [NSLOT * 32, 4 * UT])
    gmask_d = din("gmask", [128, 256])
    pastind_d = din("pastind", [128, 256])
    ownind_d = din("ownind", [128, 256])
    corr_d = din("corr", [128, 64])
    pair_d = din("pairm", [128, 64])
    tokoff_d = din("tokoff", [128, 48])
    delta_d = din("delta", [8, 48])
    pmmask_d = din("pmmask", [4, 32])
    selw_d = din("selw", [15, 4])
    cidx_d = din("cidx", [128, 32])

    y_own = dout("y_own", [4 * UT, D])
    k_own = dout("k_own", [4 * UT, 512])
    v_own = dout("v_own", [4 * UT, 512])
    pool_o = dout("pool_o", [15, 512])
    ys_o = dout("ys_o", [NSEQ, D])
    ks_o = dout("ks_o", [NSEQ, 512])
    vs_o = dout("vs_o", [NSEQ, 512])
    pools_o = dout("pools_o", [NSEQ, 15, 512])

    wbf = {
        "in": dscr("wbf_in", [D, 4096], BF16),
        "ao": dscr("wbf_ao", [512, D], BF16),
        "po": dscr("wbf_po", [512, D], BF16),
        "out": dscr("wbf_out", [D, D], BF16),
        "up": dscr("wbf_up", [D, 4096], BF16),
        "down": dscr("wbf_down", [4096, D], BF16),
        "pg": dscr("wbf_pg", [D, D], BF16),
        "pp": dscr("wbf_pp", [256, D], BF16),
    }
    wsrc = {"in": w_in_d, "ao": w_ao_d, "po": w_po_d, "out": w_out_d, "up": w_up_d,
            "down": w_down_d, "pg": w_pg_d, "pp": w_pp_d}
    ind_bf = dscr("ind_bf", [NSLOT * 32, 4 * UT], BF16)
    kT_s = dscr("kT_s", [H * HD, SEQ], BF16)
    v_s = dscr("v_s", [SEQ, H * 128], BF16)

    with ExitStack() as top:
        S = Sched(nc, top)

        def mk(stack):
            def sb(shape, dt=F32, nb=1, name="t"):
                uid[0] += 1
                return TB(stack.enter_context(nc.sbuf_tensor(f"{name}_{uid[0]}", list(shape), dt)), nb)

            def ps(shape, dt=F32, name="ps", nb=1):
                uid[0] += 1
                t = TB(stack.enter_context(nc.psum_tensor(f"{name}_{uid[0]}", list(shape), dt)), nb)
                for b_ in t.bs:
                    b_.excl = True
                return t

            def psb(name="ptr"):
                uid[0] += 1
                h = stack.enter_context(nc.psum_tensor(f"{name}_{uid[0]}", [128, 512], F32))
                f = TB(h)
                f.b.excl = True
                b = TB(h.bitcast(BF16))
                b.bs = f.bs
                return f, b
            ps.bank = psb
            return sb, ps

        sb, ps = mk(top)
        top.enter_context(nc.allow_non_contiguous_dma(reason="small strided parameter loads"))

        identf = sb([128, 128], F32, name="identf")
        identb = sb([128, 128], BF16, name="identb")
        bd = sb([128, 128], F32, name="bd")
        ones_f = sb([128, 128], F32, name="ones")
        gq = sb([128, 1], F32, name="gq")
        gk = sb([128, 1], F32, name="gk")
        gmix = sb([128, KC], F32, name="gmix")
        gmlp = sb([128, KC], F32, name="gmlp")
        gple = sb([128, KC], F32, name="gple")
        pscale = sb([128, 4], F32, name="pscale")
        poolw = sb([128, 4, 128], BF16, name="poolw")
        kmT = sb([128, 4, 32], F32, name="kmT")
        kmz = sb([128, H, 32], F32, name="kmz")
        gmask = sb([128, 256], F32, name="gmask")
        pastind = sb([128, 256], F32, name="pastind")
        ownind = sb([128, 256], F32, name="ownind")
        corr = sb([128, 64], F32, name="corr")
        cm = sb([128, 4 * UT], BF16, name="cm")

        S.op("gpsimd", MEMSET(identf.t[:], 0.0), writes=[identf.b])
        S.op("gpsimd", lambda e: e.affine_select(out=identf.t[:], in_=identf.t[:], pattern=[[-1, 128]],
                                                 compare_op=ALU.not_equal, fill=1.0, base=0,
                                                 channel_multiplier=1), reads=[identf.b], writes=[identf.b])
        S.op("vector", CP(identb.t[:], identf.t[:]), reads=[identf.b], writes=[identb.b])
        S.op("vector", MEMSET(bd.t[:], 0.0), writes=[bd.b])
        S.op("vector", MEMSET(bd.t[0:64, 0:64], 1.0), writes=[bd.b])
        S.op("vector", MEMSET(bd.t[64:128, 64:128], 1.0), writes=[bd.b])
        S.op("vector", MEMSET(ones_f.t[:], 1.0), writes=[ones_f.b])
        S.op("vector", MEMSET(kmz.t[:], 0.0), writes=[kmz.b])
        S.op("vector", MEMSET(kmT.t[:], 0.0), writes=[kmT.b])
        gstage = sb([KC, 5, 128], F32, name="gstage")
        S.op("vector", MEMSET(gstage.t[:], 0.0), writes=[gstage.b])
        for j, src in enumerate((ln_mix_d, ln_mlp_d, ln_ple_d)):
            S.op("sync", DMA(gstage.t[:, j, :], src.rearrange("(k p) -> k p", p=128)), reads=[gstage.b], writes=[gstage.b], dma=True)
        S.op("sync", DMA(gstage.t[0:4, 3, :], pool_scale_d.rearrange("(g p) -> g p", p=128)), reads=[gstage.b], writes=[gstage.b], dma=True)
        for hh in range(2):
            S.op("sync", DMA(gstage.t[0:1, 4, hh * 64:(hh + 1) * 64], q_norm_d.rearrange("(o d) -> o d", o=1)), reads=[gstage.b], writes=[gstage.b], dma=True)
            S.op("sync", DMA(gstage.t[1:2, 4, hh * 64:(hh + 1) * 64], k_norm_d.rearrange("(o d) -> o d", o=1)), reads=[gstage.b], writes=[gstage.b], dma=True)
        for (dst, src) in ((gmask, gmask_d), (pastind, pastind_d), (ownind, ownind_d), (corr, corr_d)):
            S.op("sync", DMA(dst.t[:], src), writes=[dst.b], dma=True)
        S.op("gpsimd", DMA(poolw.t[:], pool_w_d.rearrange("g c d -> c g d")), writes=[poolw.b], dma=True)
        S.op("gpsimd", DMA(cm.t[:], cm_d), writes=[cm.b], dma=True)

        xres = sb([128, 4, D], F32, nb=4, name="xres")
        xsb = sb([128, 4, D], BF16, nb=4, name="xsb")
        aT_s = sb([128, 4, NSEQ], BF16, name="aT_s")
        xnT = sb([128, KC, UT], BF16, nb=KC, name="xnT")
        xnTs = sb([128, KC, NSEQ], BF16, nb=KC, name="xnTs")
        junk = sb([128, D], BF16, name="junk")
        ssq4 = sb([128, 4], F32, name="ssq4")
        rstd4 = sb([128, 4], F32, name="rstd4")

        b_wbf = {k: Buf() for k in wbf}
        b_indbf = Buf()
        b_kT = [[Buf() for _ in range(4)] for _ in range(NSLOT)]
        b_v = [[Buf() for _ in range(4)] for _ in range(NSLOT)]

        def ln_transpose(xt, P, ntt, gcols, dstT, ptr):
            T = P * ntt
            S.op("vector", MEMSET(ssq4.t[:P, :], 0.0), writes=[ssq4.b])
            for tt in range(ntt):
                S.op("scalar", ACTF(junk.t[:P, :], xt.t[:P, tt, :], AF.Square, accum_out=ssq4.t[:P, tt:tt + 1]),
                     reads=[xt.bs[tt]], writes=[junk.b, ssq4.b])
            S.op("vector", TS(rstd4.t[:P, :ntt], ssq4.t[:P, :ntt], 1.0 / D, EPS, ALU.mult, ALU.add),
                 reads=[ssq4.b], writes=[rstd4.b])
            S.op("scalar", ACTF(rstd4.t[:P, :ntt], rstd4.t[:P, :ntt], AF.Ln), reads=[rstd4.b], writes=[rstd4.b])
            S.op("scalar", ACTF(rstd4.t[:P, :ntt], rstd4.t[:P, :ntt], AF.Exp, scale=-0.5), reads=[rstd4.b], writes=[rstd4.b])
            for tt in range(ntt):
                S.op("vector", TS(xsb.t[:P, tt, :], xt.t[:P, tt, :], rstd4.t[:P, tt:tt + 1], None, ALU.mult),
                     reads=[xt.bs[tt], rstd4.b], writes=[xsb.bs[tt]])
            for kc in range(KC):
                pt_k = ptr[kc % 2]
                S.op("tensor", TRG([(pt_k.t[:, tt * P:(tt + 1) * P], xsb.t[:P, tt, kc * 128:(kc + 1) * 128],
                                     identb.t[:P, :P]) for tt in range(ntt)]),
                     reads=xsb.bs[:ntt] + [identb.b], writes=[pt_k.b])
                if kc % 2 == 0:
                    S.op("scalar", ACTF(dstT.t[:, kc, :T], pt_k.t[:, :T], AF.Copy, scale=gcols.t[:, kc:kc + 1]),
                         reads=[pt_k.b, gcols.b], writes=[dstT.bs[kc]])
                else:
                    S.op("vector", TS(dstT.t[:, kc, :T], pt_k.t[:, :T], gcols.t[:, kc:kc + 1], None, ALU.mult),
                         reads=[pt_k.b, gcols.b], writes=[dstT.bs[kc]])

        def head_norm(pk, T, gcol, dst_ap, dst_b, sq, rk, pss):
            S.op("scalar", ACTF(sq.t[:, :T], pk.t[:, :T], AF.Square), reads=[pk.b], writes=[sq.b])
            S.op("tensor", MMG([(pss.t[:, :T], bd.t[:], sq.t[:, :T], True, True)]), reads=[bd.b, sq.b], writes=[pss.b])
            S.op("vector", TS(rk.t[:, :T], pss.t[:, :T], 1.0 / HD, EPS, ALU.mult, ALU.add), reads=[pss.b], writes=[rk.b])
            S.op("scalar", ACTF(rk.t[:, :T], rk.t[:, :T], AF.Ln), reads=[rk.b], writes=[rk.b])
            S.op("scalar", ACTF(rk.t[:, :T], rk.t[:, :T], AF.Exp, scale=-0.5), reads=[rk.b], writes=[rk.b])
            S.op("vector", STT(dst_ap, pk.t[:, :T], gcol.t[:, 0:1], rk.t[:, :T], ALU.mult, ALU.mult),
                 reads=[pk.b, gcol.b, rk.b], writes=[dst_b])

        with ExitStack() as pa:
            sbA, psA = mk(pa)
            wq_s = sbA([128, KC, 512], BF16, name="wq_s")
            sqA = sbA([128, UT], F32, name="sqA")
            rkA = sbA([128, UT], F32, name="rkA")
            ksum = sbA([128, NSEQ, 512], F32, nb=NSEQ, name="ksum")
            ks_tok = sbA([NSEQ, 512], F32, name="ks_tok")
            vs_tok = sbA([NSEQ, 512], F32, name="vs_tok")
            _pb = [psA.bank(), psA.bank()]
            ptr = [_pb[0][1], _pb[1][1]]
            pk = [psA([128, UT], F32, name="pk") for _ in range(2)]
            pss = psA([128, UT], F32, name="pss")
            pv = [psA([128, 512], F32, name="pv") for _ in range(2)]
            pkT = psA([128, 512], F32, name="pkT")
            S.op("tensor", TRG([(pkT.t[:, j * KC:(j + 1) * KC], gstage.t[:, j, :], identf.t[:KC, :KC]) for j in range(5)]),
                 reads=[gstage.b, identf.b], writes=[pkT.b])
            S.op("vector", CP(gmix.t[:], pkT.t[:, 0:KC]), reads=[pkT.b], writes=[gmix.b])
            S.op("vector", CP(gmlp.t[:], pkT.t[:, KC:2 * KC]), reads=[pkT.b], writes=[gmlp.b])
            S.op("vector", CP(gple.t[:], pkT.t[:, 2 * KC:3 * KC]), reads=[pkT.b], writes=[gple.b])
            S.op("vector", CP(pscale.t[:], pkT.t[:, 3 * KC:3 * KC + 4]), reads=[pkT.b], writes=[pscale.b])
            S.op("vector", CP(gq.t[:], pkT.t[:, 4 * KC:4 * KC + 1]), reads=[pkT.b], writes=[gq.b])
            S.op("vector", CP(gk.t[:], pkT.t[:, 4 * KC + 1:4 * KC + 2]), reads=[pkT.b], writes=[gk.b])

            with ExitStack() as pa1:
                sbB, _ = mk(pa1)
                wkv = sbB([128, KC, 1024], BF16, nb=KC, name="wkv")
                for kc in range(KC):
                    S.op("gpsimd", DMA(wkv.t[:, kc, :], w_in_d[kc * 128:(kc + 1) * 128, 512:1536]),
                         writes=[wkv.bs[kc]], dma=True)
                cast_jobs = [(ind_bf, ind_d, b_indbf)]
                for name in ["in", "ao", "po", "out", "up", "down", "pg", "pp"]:
                    src = wsrc[name]
                    ncols = src.shape[1]
                    step = min(ncols, 2048)
                    for c0 in range(0, ncols, step):
                        cast_jobs.append((wbf[name][:, c0:c0 + step], src[:, c0:c0 + step], b_wbf[name]))

                def emit_cast():
                    if cast_jobs:
                        o_, i_, b_ = cast_jobs.pop(0)
                        S.op("gpsimd", DMA(o_, i_), writes=[b_], dma=True)

                xresB = sbB([128, 4, D], F32, nb=4, name="xresB")
                xbufs = [xres, xresB]
                knT = sbB([128, 4, UT], F32, nb=4, name="knT")
                kbf = [sbB([128, UT], BF16, name="kbf") for _ in range(2)]
                vbf = [sbB([128, H, 128], BF16, name="vbf") for _ in range(2)]
                vf = [sbB([128, 512], F32, name="vf") for _ in range(2)]
                kout = [sbB([128, 512], F32, name="kout") for _ in range(2)]
                NKP = 4
                kp = [sbB([128, 2048], F32, name="kp") for _ in range(NKP)]
                kred = [sbB([128, 512], F32, name="kred") for _ in range(2)]
                ptT = sbB([128, NSEQ], I32, name="ptT")
                ptf = sbB([128, NSEQ], F32, name="ptf")
                pidx = sbB([128, NSEQ * 32], I32, name="pidx")
                pidxf = sbB([128, NSEQ * 32], F32, name="pidxf")
                xs_t = sbB([128, 1, D], F32, name="xs_t")
                knTs = sbB([128, 4, NSEQ], F32, nb=4, name="knTs")
                for v in vbf:
                    S.op("vector", MEMSET(v.t[:, :, 64:128], 1.0), writes=[v.b])

                S.op("sync", DMA(ptT.t[:], ptT_d), writes=[ptT.b], dma=True)
                S.op("vector", CP(ptf.t[:], ptT.t[:]), reads=[ptT.b], writes=[ptf.b])
                cidx = sbB([128, 32], F32, name="cidx")
                S.op("sync", DMA(cidx.t[:], cidx_d), writes=[cidx.b], dma=True)
                S.op("vector", TS(ptf.t[:], ptf.t[:], 32.0, None, ALU.mult), reads=[ptf.b], writes=[ptf.b])
                for s in range(NSEQ):
                    S.op("vector", TS(pidxf.t[:, s * 32:(s + 1) * 32], cidx.t[:], ptf.t[:, s:s + 1], None, ALU.add),
                         reads=[ptf.b, cidx.b], writes=[pidxf.b])
                S.op("gpsimd", CP(pidx.t[:], pidxf.t[:]), reads=[pidxf.b], writes=[pidx.b])
                for s in range(NSEQ):
                    S.op("gpsimd", MEMSET(ksum.t[:, s, :], 0.0), writes=[ksum.bs[s]])
                ck_pieces = ck_d.rearrange("(a b) d -> a (b d)", b=32)
                pieces = [(s, c) for s in range(NSEQ) for c in range(32)]
                piece_i = [0]

                g_issued = [0]
                r_done = [0]
                g_limit = [0]

                def piece_gather():
                    i = g_issued[0]
                    g_issued[0] += 1
                    s, c = pieces[i]
                    buf = kp[i % NKP]
                    col = s * 32 + c
                    S.op("gpsimd", IDMA(buf.t[:], ck_pieces, pidx.t[:, col:col + 1]), reads=[pidx.b], writes=[buf.b], dma=True)

                def piece_reduce():
                    i = r_done[0]
                    r_done[0] += 1
                    s, c = pieces[i]
                    buf = kp[i % NKP]
                    kr = kred[i % 2]
                    eng = "vector" if i % 2 == 0 else "gpsimd"
                    if eng == "vector":
                        S.op("vector", RED(kr.t[:], buf.t[:].rearrange("p (t f) -> p f t", f=512)), reads=[buf.b], writes=[kr.b])
                    else:
                        S.op("gpsimd", TT(buf.t[:, 0:1024], buf.t[:, 0:1024], buf.t[:, 1024:2048], ALU.add), reads=[buf.b], writes=[buf.b])
                        S.op("gpsimd", TT(kr.t[:], buf.t[:, 0:512], buf.t[:, 512:1024], ALU.add), reads=[buf.b], writes=[kr.b])
                    S.op("gpsimd", TT(ksum.t[:, s, :], ksum.t[:, s, :], kr.t[:], ALU.add), reads=[kr.b, ksum.bs[s]], writes=[ksum.bs[s]])

                def hook(flush=False):
                    if not (_on("A") and CFG["pieces"]):
                        return
                    while True:
                        did = False
                        if r_done[0] < g_issued[0] and (flush or g_issued[0] - r_done[0] >= NKP - 1 or g_issued[0] >= g_limit[0]):
                            piece_reduce()
                            did = True
                        if g_issued[0] < min(g_limit[0], len(pieces)) and g_issued[0] - r_done[0] < NKP:
                            piece_gather()
                            did = True
                        if not flush or not did:
                            break

                def load_x(slot, dst):
                    for tt in range(4):
                        S.op("sync", DMA(dst.t[:, tt, :], xb[slot * UT + tt * 128: slot * UT + (tt + 1) * 128, :]),
                             writes=[dst.bs[tt]], dma=True)

                def kv_stage(xT, P, ntt, slot):
                    T = P * ntt
                    sample = slot is None
                    kdst = knTs if sample else knT
                    for m in range(4):
                        pkm = pk[m % 2]
                        S.op("tensor", MMG([(pkm.t[:, :T], wkv.t[:, kc, m * 128:(m + 1) * 128], xT.t[:, kc, :T], kc == 0, kc == KC - 1)
                                            for kc in range(KC)]),
                             reads=list(wkv.bs) + list(xT.bs), writes=[pkm.b])
                        head_norm(pkm, T, gk, kdst.t[:, m, :T], kdst.bs[m], sqA, rkA, pss)
                        if not sample:
                            hook()
                            kb = kbf[m % 2]
                            S.op("gpsimd", CP(kb.t[:], knT.t[:, m, :]), reads=[knT.bs[m]], writes=[kb.b])
                            S.op("sync", DMA(kT_s[m * 128:(m + 1) * 128, slot * UT:(slot + 1) * UT], kb.t[:]),
                                 reads=[kb.b], writes=[b_kT[slot][m]], dma=True)
                            S.op("vector", RED(kmT.t[:, m, 2 * slot:2 * slot + 2], knT.t[:, m, :].rearrange("p (b k) -> p b k", k=256)),
                                 reads=[knT.bs[m]], writes=[kmT.b])
                    if sample or slot < 4:
                        for tt in range(ntt):
                            S.op("tensor", TRG([(pkT.t[:P, m * 128:(m + 1) * 128], kdst.t[:, m, tt * P:(tt + 1) * P], identf.t[:])
                                                for m in range(4)]),
                                 reads=list(kdst.bs) + [identf.b], writes=[pkT.b])
                            if sample:
                                S.op("vector", CP(ks_tok.t[:], pkT.t[:P, :]), reads=[pkT.b], writes=[ks_tok.b])
                                S.op("sync", DMA(ks_o, ks_tok.t[:]), reads=[ks_tok.b], dma=True)
                            else:
                                ko = kout[tt % 2]
                                S.op("scalar", ACTF(ko.t[:], pkT.t[:, :], AF.Copy), reads=[pkT.b], writes=[ko.b])
                                S.op("sync", DMA(k_own[slot * UT + tt * 128: slot * UT + (tt + 1) * 128, :], ko.t[:]),
                                     reads=[ko.b], dma=True)
                    for tt in range(ntt):
                        if not sample:
                            hook()
                        pvt = pv[tt % 2]
                        S.op("tensor", MMG([(pvt.t[:P, :], xT.t[:, kc, tt * P:(tt + 1) * P], wkv.t[:, kc, 512:1024], kc == 0, kc == KC - 1)
                                            for kc in range(KC)]),
                             reads=list(wkv.bs) + list(xT.bs), writes=[pvt.b])
                        if sample:
                            S.op("vector", CP(vs_tok.t[:], pvt.t[:P, :]), reads=[pvt.b], writes=[vs_tok.b])
                            S.op("sync", DMA(vs_o, vs_tok.t[:]), reads=[vs_tok.b], dma=True)
                        else:
                            vb = vbf[tt % 2]
                            S.op("scalar", ACTF(vb.t[:, :, 0:64], pvt.t[:, :].rearrange("p (h d) -> p h d", d=HD), AF.Copy),
                                 reads=[pvt.b], writes=[vb.b])
                            S.op("sync", DMA(v_s[slot * UT + tt * 128: slot * UT + (tt + 1) * 128, :], vb.t[:].rearrange("p h c -> p (h c)")),
                                 reads=[vb.b], writes=[b_v[slot][tt]], dma=True)
                            if slot < 4:
                                vv = vf[tt % 2]
                                S.op("vector", CP(vv.t[:], pvt.t[:, :]), reads=[pvt.b], writes=[vv.b])
                                S.op("sync", DMA(v_own[slot * UT + tt * 128: slot * UT + (tt + 1) * 128, :], vv.t[:]),
                                     reads=[vv.b], dma=True)

                NSL = CFG["nslots"] if _on("A") else 0
                if NSL:
                    load_x(0, xbufs[0])
                for slot in range(NSL):
                    if slot + 1 < NSL:
                        load_x(slot + 1, xbufs[(slot + 1) % 2])
                    g_limit[0] = 8 * (slot + 1) + 2
                    hook()
                    if slot >= 1:
                        emit_cast()
                    ln_transpose(xbufs[slot % 2], 128, 4, gmix, xnT, ptr)
                    hook()
                    kv_stage(xnT, 128, 4, slot)
                g_limit[0] = len(pieces)
                hook(flush=True)
                while cast_jobs:
                    emit_cast()
                if _on("A"):
                    S.op("sync", DMA(xs_t.t[:NSEQ, 0, :], xs_d), writes=[xs_t.b], dma=True)
                    ln_transpose(xs_t, NSEQ, 1, gmix, xnTs, ptr)
                    kv_stage(xnTs, NSEQ, 1, None)
                for m in range(4):
                    S.op("vector", CP(kmz.t[0:64, 2 * m, :], kmT.t[0:64, m, :]), reads=[kmT.b], writes=[kmz.b])
                    S.op("vector", CP(kmz.t[64:128, 2 * m + 1, :], kmT.t[64:128, m, :]), reads=[kmT.b], writes=[kmz.b])
                S.barrier()
                S.run_block()

            with ExitStack() as sa:
                sbS, psS = mk(sa)
                qnTs = sbS([128, 4, NSEQ], F32, name="qnTs")
                q_tok = sbS([NSEQ, 512], F32, name="q_tok")
                qrep = sbS([128, 512], F32, name="qrep")
                esel = sbS([NSEQ, NSEQ, 128], F32, name="esel")
                prod = sbS([128, 512], F32, name="prod")
                pgs = sbS([128, H], F32, name="pgs")
                pairm = sbS([128, 64], F32, name="pairm")
                gates = sbS([H, NSEQ, 64], F32, name="gates")
                mx8 = sbS([H, NSEQ, 8], F32, name="mx8")
                pt8i = sbS([H, NSEQ * 128], I32, name="pt8i")
                pt8 = sbS([H, NSEQ * 128], F32, name="pt8")
                oh = sbS([H, 64], F32, name="oh")
                ohp = sbS([H, 64], F32, name="ohp")
                Pm = sbS([H, NSEQ, 6], F32, name="Pm")
                Dm = sbS([H, 48], F32, name="Dm")
                delta = sbS([H, 48], F32, name="delta")
                tokoff = sbS([128, 48], F32, name="tokoff")
                gidxf = sbS([128, NSEQ, 48], F32, name="gidxf")
                gidx = sbS([128, NSEQ, 48], I32, name="gidx")
                KG = [sbS([128, 48, HD], F32, nb=48, name="KG") for _ in range(2)]
                VG = [sbS([128, 48, HD], F32, nb=48, name="VG") for _ in range(2)]
                sprod = sbS([128, 48, HD], F32, name="sprod")
                sc = sbS([128, 48], F32, name="sc")
                pexp = sbS([128, 48], F32, name="pexp")
                den8 = sbS([1, H], F32, name="den8")
                own_p = sbS([NSEQ, 512], F32, name="own_p")
                own_s = sbS([NSEQ, H], F32, name="own_s")
                own_e = sbS([NSEQ, H], F32, name="own_e")
                pmmask = sbS([NSEQ, 32], F32, name="pmmask")
                PM = sbS([NSEQ, NSEQ, H], F32, name="PM")
                a_row = sbS([1, 512], F32, name="a_row")
                rden = sbS([1, H], F32, name="rden")

                pq = pk[0]
                pqT = pkT
                pgt = pv[0]
                pidxp = pv[1]
                pacc = pk[1]
                pden = pss
                paT = pkT

                if _on("samp"):
                    S.op("sync", DMA(wq_s.t[:], wbf["in"].rearrange("(k p) n -> p k n", p=128)[:, :, 0:512]),
                         reads=[b_wbf["in"]], writes=[wq_s.b], dma=True)
                    for m in range(4):
                        S.op("tensor", MMG([(pq.t[:, :NSEQ], wq_s.t[:, kc, m * 128:(m + 1) * 128], xnTs.t[:, kc, :], kc == 0, kc == KC - 1)
                                            for kc in range(KC)]), reads=[wq_s.b] + list(xnTs.bs), writes=[pq.b])
                        head_norm(pq, NSEQ, gq, qnTs.t[:, m, :], qnTs.b, sqA, rkA, pss)
                    S.op("tensor", TRG([(pqT.t[:NSEQ, m * 128:(m + 1) * 128], qnTs.t[:, m, :], identf.t[:]) for m in range(4)]),
                         reads=[qnTs.b, identf.b], writes=[pqT.b])
                    S.op("vector", CP(q_tok.t[:], pqT.t[:NSEQ, :]), reads=[pqT.b], writes=[q_tok.b])
                    S.op("sync", DMA(pairm.t[:], pair_d), writes=[pairm.b], dma=True)
                    S.op("sync", DMA(delta.t[:], delta_d), writes=[delta.b], dma=True)
                    S.op("sync", DMA(tokoff.t[:], tokoff_d), writes=[tokoff.b], dma=True)
                    S.op("sync", DMA(pmmask.t[:], pmmask_d), writes=[pmmask.b], dma=True)
                    S.op("sync", DMA(pt8i.t[:], pt_d.rearrange("s p -> (s p)").partition_broadcast(H)), writes=[pt8i.b], dma=True)
                    S.op("vector", CP(pt8.t[:], pt8i.t[:]), reads=[pt8i.b], writes=[pt8.b])
                    for s in range(NSEQ):
                        S.op("vector", TS(esel.t[:, s, :], ones_f.t[:NSEQ, :], pmmask.t[:, s * H:s * H + 1], None, ALU.mult),
                             reads=[ones_f.b, pmmask.b], writes=[esel.b])
                    for s in range(NSEQ):
                        S.op("tensor", MMG([(pgt.t[:, :], esel.t[:, s, :], q_tok.t[:, :], True, True)]),
                             reads=[esel.b, q_tok.b], writes=[pgt.b])
                        S.op("vector", TT(prod.t[:], ksum.t[:, s, :], pgt.t[:, :], ALU.mult), reads=[ksum.bs[s], pgt.b], writes=[prod.b])
                        S.op("vector", RED(pgs.t[:], prod.t[:].rearrange("p (h d) -> p h d", d=HD)), reads=[prod.b], writes=[pgs.b])
                        S.op("tensor", MMG([(pidxp.t[:H, :64], pgs.t[:], pairm.t[:], True, True)]), reads=[pgs.b, pairm.b], writes=[pidxp.b])
                        S.op("vector", CP(gates.t[:, s, :], pidxp.t[:H, :64]), reads=[pidxp.b], writes=[gates.b])
                        S.op("vector", MAX8(mx8.t[:, s, :], gates.t[:, s, :]), reads=[gates.b], writes=[mx8.b])
                        for i in range(3):
                            S.op("vector", TS(oh.t[:], gates.t[:, s, :], mx8.t[:, s, i:i + 1], None, ALU.is_equal),
                                 reads=[gates.b, mx8.b], writes=[oh.b])
                            for e_ in range(2):
                                ptv = pt8.t[:, s * 128:(s + 1) * 128].rearrange("p (j e) -> p e j", e=2)[:, e_, :]
                                S.op("vector", TT(ohp.t[:], oh.t[:], ptv, ALU.mult), reads=[oh.b, pt8.b], writes=[ohp.b])
                                S.op("vector", RED(Pm.t[:, s, 2 * i + e_:2 * i + e_ + 1], ohp.t[:]), reads=[ohp.b], writes=[Pm.b])
                        S.op("vector", TT(Dm.t[:].rearrange("p (h c) -> p h c", c=6), delta.t[:].rearrange("p (h c) -> p h c", c=6),
                                          Pm.t[:, s, :].unsqueeze(1).to_broadcast([H, H, 6]), ALU.mult),
                             reads=[delta.b, Pm.b], writes=[Dm.b])
                        S.op("tensor", MMG([(pidxp.t[:, 64:112], ones_f.t[:H, :], Dm.t[:], True, True)]), reads=[ones_f.b, Dm.b], writes=[pidxp.b])
                        S.op("vector", STT(gidxf.t[:, s, :], pidxp.t[:, 64:112], 1024.0, tokoff.t[:], ALU.mult, ALU.add),
                             reads=[pidxp.b, tokoff.b], writes=[gidxf.b])
                    S.op("vector", CP(gidx.t[:], gidxf.t[:]), reads=[gidxf.b], writes=[gidx.b])
                    S.op("vector", TT(own_p.t[:], q_tok.t[:], ks_tok.t[:], ALU.mult), reads=[q_tok.b, ks_tok.b], writes=[own_p.b])
                    S.op("vector", RED(own_s.t[:], own_p.t[:].rearrange("p (h d) -> p h d", d=HD)), reads=[own_p.b], writes=[own_s.b])
                    S.op("scalar", ACTF(own_e.t[:], own_s.t[:], AF.Exp, scale=HD ** -0.5), reads=[own_s.b], writes=[own_e.b])
                    S.op("vector", TT(PM.t[:], pmmask.t[:].rearrange("p (s h) -> p s h", h=H),
                                      own_e.t[:].unsqueeze(1).to_broadcast([NSEQ, NSEQ, H]), ALU.mult),
                         reads=[pmmask.b, own_e.b], writes=[PM.b])
                    for s in range(NSEQ):
                        kg = KG[s % 2]
                        vg = VG[s % 2]
                        for c in range(48):
                            S.op("gpsimd", IDMA(kg.t[:, c, :], ck_d, gidx.t[:, s, c:c + 1]), reads=[gidx.b], writes=[kg.bs[c]], dma=True)
                        for c in range(48):
                            S.op("gpsimd", IDMA(vg.t[:, c, :], cv_d, gidx.t[:, s, c:c + 1]), reads=[gidx.b], writes=[vg.bs[c]], dma=True)
                        S.op("tensor", MMG([(pgt.t[:, :], esel.t[:, s, :], q_tok.t[:, :], True, True)]),
                             reads=[esel.b, q_tok.b], writes=[pgt.b])
                        S.op("vector", CP(qrep.t[:], pgt.t[:, :]), reads=[pgt.b], writes=[qrep.b])
                        S.op("vector", TT(sprod.t[:].rearrange("p (h c) d -> p h c d", c=6), kg.t[:].rearrange("p (h c) d -> p h c d", c=6),
                                          qrep.t[:].rearrange("p (h d) -> p h d", d=HD).unsqueeze(2).to_broadcast([128, H, 6, HD]), ALU.mult),
                             reads=list(kg.bs) + [qrep.b], writes=[sprod.b])
                        S.op("vector", RED(sc.t[:], sprod.t[:]), reads=[sprod.b], writes=[sc.b])
                        S.op("scalar", ACTF(pexp.t[:], sc.t[:], AF.Exp, scale=HD ** -0.5), reads=[sc.b], writes=[pexp.b])
                        S.op("tensor", MMG([(pden.t[:1, :48], ones_f.t[:, 0:1], pexp.t[:], True, True)]), reads=[ones_f.b, pexp.b], writes=[pden.b])
                        S.op("vector", RED(den8.t[:], pden.t[:1, :48].rearrange("p (h c) -> p h c", c=6)), reads=[pden.b], writes=[den8.b])
                        S.op("tensor", MMG([(pden.t[:1, 64:64 + H], ones_f.t[:NSEQ, 0:1], PM.t[:, s, :], True, True)]),
                             reads=[ones_f.b, PM.b], writes=[pden.b])
                        S.op("vector", TT(den8.t[:], den8.t[:], pden.t[:1, 64:64 + H], ALU.add), reads=[pden.b, den8.b], writes=[den8.b])
                        S.op("vector", RCP(rden.t[:], den8.t[:]), reads=[den8.b], writes=[rden.b])
                        lst = []
                        for h in range(H):
                            for c6 in range(6):
                                c = h * 6 + c6
                                lst.append((pacc.t[:1, h * HD:(h + 1) * HD], pexp.t[:, c:c + 1], vg.t[:, c, :], c6 == 0, False))
                            lst.append((pacc.t[:1, h * HD:(h + 1) * HD], PM.t[:, s, h:h + 1], vs_tok.t[:, h * HD:(h + 1) * HD], False, True))
                        S.op("tensor", MMG(lst), reads=[pexp.b, PM.b, vs_tok.b] + list(vg.bs), writes=[pacc.b])
                        S.op("vector", TT(a_row.t[:].rearrange("p (h d) -> p h d", d=HD), pacc.t[:1, :].rearrange("p (h d) -> p h d", d=HD),
                                          rden.t[:].unsqueeze(2).to_broadcast([1, H, HD]), ALU.mult),
                             reads=[pacc.b, rden.b], writes=[a_row.b])
                        S.op("tensor", TRG([(paT.t[:, m:m + 1], a_row.t[:1, m * 128:(m + 1) * 128], identf.t[:1, :1]) for m in range(4)]),
                             reads=[a_row.b, identf.b], writes=[paT.b])
                        S.op("vector", CP(aT_s.t[:, :, s], paT.t[:, 0:4]), reads=[paT.b], writes=[aT_s.b])
                S.barrier()
                S.run_block()

        NRING = 6
        ring = [sb([128, 4096], BF16, name="ring") for _ in range(NRING)]
        ring_i = [0]
        tmpf = [sb([128, UT], F32, name="tmpf") for _ in range(4)]

        class Panels:
            def __init__(self, specs, ahead=3):
                self.specs = specs
                self.loaded = {}
                self.next = 0
                self.ahead = ahead

            def _load(self, i):
                name, k0, nk, c0, ncols = self.specs[i]
                r = ring[ring_i[0] % NRING]
                ring_i[0] += 1
                view = r.t[:, 0:nk * ncols].rearrange("p (k c) -> p k c", c=ncols)
                src = wbf[name].rearrange("(k p) n -> p k n", p=128)[:, k0:k0 + nk, c0:c0 + ncols]
                S.op("sync", DMA(view, src), reads=[b_wbf[name]], writes=[r.b], dma=True)
                self.loaded[i] = (view, r.b)

            def get(self, i, oldest=None):
                if oldest is None:
                    oldest = i
                while self.next <= min(i + self.ahead, len(self.specs) - 1) and self.next < oldest + NRING:
                    self._load(self.next)
                    self.next += 1
                assert i in self.loaded and i >= self.next - NRING, (i, self.next)
                return self.loaded[i]

        def unit_pass(J):
            sample = J is None
            P = NSEQ if sample else 128
            ntt = 1 if sample else 4
            T = P * ntt
            xT = xnTs if sample else xnT

            with ExitStack() as s1:
                sb1, ps1 = mk(s1)
                uT = sb1([128, 4, 16 + UT], F32, nb=4, name="uT")
                pl = [sb1([128, 16 + UT], F32, name="pl") for _ in range(2)]
                dT = sb1([128, 4, UT], BF16, nb=4, name="dT")
                bT = sb1([128, 4, UT], BF16, nb=4, name="bT")
                mT = sb1([128, KC, UT], BF16, nb=KC, name="mT")
                _pb = [ps1.bank(), ps1.bank()]
                ptr = [_pb[0][1], _pb[1][1]]
                pA = [ps1([128, UT], F32, name="pA") for _ in range(2)]
                pG = [ps1([128, UT], F32, name="pG") for _ in range(2)]
                pS = [ps1([128, UT], F32, name="pS") for _ in range(2)]
                pacc = _pb[0][0]
                if sample:
                    aT = aT_s
                    aT_bs = [aT_s.b] * 4
                    hist = sb1([15, NSEQ, 512], F32, name="hist")
                    selw = sb1([15, 4], F32, name="selw")
                    u_tok = sb1([NSEQ, 512], F32, name="u_tok")
                else:
                    aTt = sb1([128, 4, UT], BF16, nb=4, name="aT")
                    aT = aTt
                    aT_bs = aTt.bs
                    qnT = sb1([128, 4, UT], F32, nb=4, name="qnT")
                    qaug = sb1([96, H, UT], BF16, nb=H, name="qaug")
                    g1 = sb1([128, 256], F32, name="g1")
                    mx8 = sb1([128, 8], F32, name="mx8")
                    selt = sb1([128, 32], F32, name="selt")
                    negm = sb1([128, 256], F32, name="negm")
                    kb = [sb1([96, 4, UT], BF16, nb=2, name="kb") for _ in range(3)]
                    vb = [sb1([128, 4, 512], BF16, name="vb") for _ in range(3)]
                    pT = [sb1([128, UT], BF16, name="pT") for _ in range(3)]
                    rden = sb1([64, UT], F32, name="rden")
                    xh = sb1([16, 1, D], F32, name="xh")
                    xnTh = sb1([128, KC, 16], BF16, nb=KC, name="xnTh")
                    uo = sb1([15, 512], F32, name="uo")

                if not sample:
                    for tt in range(4):
                        S.op("sync", DMA(xres.t[:, tt, :], xb[J * UT + tt * 128: J * UT + (tt + 1) * 128, :]),
                             writes=[xres.bs[tt]], dma=True)
                    ln_transpose(xres, 128, 4, gmix, xnT, ptr)
                    S.op("sync", DMA(xh.t[:, 0, :], xhalo[J * 16:(J + 1) * 16, :]), writes=[xh.b], dma=True)
                    ln_transpose(xh, 16, 1, gmix, xnTh, ptr)
                else:
                    S.op("sync", DMA(xres.t[:NSEQ, 0, :], xs_d), writes=[xres.bs[0]], dma=True)
                    S.op("sync", DMA(hist.t[:], state_d.rearrange("s r c -> r s c")), writes=[hist.b], dma=True)
                    S.op("sync", DMA(selw.t[:], selw_d), writes=[selw.b], dma=True)

                specs = []
                if not sample:
                    specs.append(("in", 0, KC, 0, 512))
                specs.append(("in", 0, KC, 1536, 512))
                specs += [("ao", 0, 4, 0, 1024), ("po", 0, 4, 0, 1024)]
                specs += [("in", 0, KC, 2048, 512), ("in", 0, KC, 3072, 512), ("in", 0, KC, 2560, 512), ("in", 0, KC, 3584, 512)]
                specs += [("out", 0, KC, 0, 512), ("out", 0, KC, 512, 512)]
                pn = Panels(specs)
                pi = 0

                if not sample:
                    wv, wb = pn.get(pi); pi += 1
                    for m in range(4):
                        pkm = pA[m % 2]
                        S.op("tensor", MMG([(pkm.t[:, :T], wv[:, kc, m * 128:(m + 1) * 128], xT.t[:, kc, :T], kc == 0, kc == KC - 1)
                                            for kc in range(KC)]), reads=[wb] + list(xT.bs), writes=[pkm.b])
                        head_norm(pkm, T, gq, qnT.t[:, m, :], qnT.bs[m], tmpf[0], tmpf[1], pG[0])
                        S.op("scalar", ACTF(qaug.t[0:64, 2 * m, :], qnT.t[0:64, m, :], AF.Copy, scale=HD ** -0.5),
                             reads=[qnT.bs[m]], writes=[qaug.bs[2 * m]])
                        S.op("vector", TS(qaug.t[0:64, 2 * m + 1, :], qnT.t[64:128, m, :], HD ** -0.5, None, ALU.mult),
                             reads=[qnT.bs[m]], writes=[qaug.bs[2 * m + 1]])
                    for tt in range(4):
                        bq = tt // 2
                        gcol = (J * 2 + bq) * 32
                        S.op("tensor", MMG([(pG[1].t[:, h * 32:(h + 1) * 32], qnT.t[:, h // 2, tt * 128:(tt + 1) * 128], kmz.t[:, h, :], True, True)
                                            for h in range(H)]), reads=list(qnT.bs) + [kmz.b], writes=[pG[1].b])
                        S.op("vector", TT(g1.t[:].rearrange("p (h j) -> p h j", j=32), pG[1].t[:, 0:256].rearrange("p (h j) -> p h j", j=32),
                                          gmask.t[:, gcol:gcol + 32].unsqueeze(1).to_broadcast([128, H, 32]), ALU.add),
                             reads=[pG[1].b, gmask.b], writes=[g1.b])
                        for h in range(H):
                            S.op("vector", MAX8(mx8.t[:], g1.t[:, h * 32:(h + 1) * 32]), reads=[g1.b], writes=[mx8.b])
                            S.op("vector", STT(selt.t[:], g1.t[:, h * 32:(h + 1) * 32], mx8.t[:, 2:3], pastind.t[:, gcol:gcol + 32],
                                               ALU.is_ge, ALU.mult), reads=[g1.b, mx8.b, pastind.b], writes=[selt.b])
                            S.op("vector", TT(selt.t[:], selt.t[:], ownind.t[:, gcol:gcol + 32], ALU.max),
                                 reads=[selt.b, ownind.b], writes=[selt.b])
                            S.op("vector", TS(negm.t[:, h * 32:(h + 1) * 32], selt.t[:], -1.0, -NEG, ALU.add, ALU.mult),
                                 reads=[selt.b], writes=[negm.b])
                        for hp in range(4):
                            S.op("tensor", TRG([(pG[0].t[:64, 0:128], negm.t[:, hp * 64:(hp + 1) * 64], identf.t[:])]),
                                 reads=[negm.b, identf.b], writes=[pG[0].b])
                            S.op("vector", CP(qaug.t[64:96, 2 * hp, tt * 128:(tt + 1) * 128], pG[0].t[0:32, 0:128]),
                                 reads=[pG[0].b], writes=[qaug.bs[2 * hp]])
                            S.op("scalar", ACTF(qaug.t[64:96, 2 * hp + 1, tt * 128:(tt + 1) * 128], pG[0].t[32:64, 0:128], AF.Copy),
                                 reads=[pG[0].b], writes=[qaug.bs[2 * hp + 1]])

                wv, wb = pn.get(pi); pi += 1
                for g in range(4):
                    pu = pA[g % 2]
                    S.op("tensor", MMG([(pu.t[:, :T], wv[:, kc, g * 128:(g + 1) * 128], xT.t[:, kc, :T], kc == 0, kc == KC - 1)
                                        for kc in range(KC)]), reads=[wb] + list(xT.bs), writes=[pu.b])
                    S.op("scalar", ACTF(uT.t[:, g, 16:16 + T], pu.t[:, :T], AF.Copy), reads=[pu.b], writes=[uT.bs[g]])
                    if not sample:
                        S.op("tensor", MMG([(pG[0].t[:, :16], wv[:, kc, g * 128:(g + 1) * 128], xnTh.t[:, kc, :], kc == 0, kc == KC - 1)
                                            for kc in range(KC)]), reads=[wb] + list(xnTh.bs), writes=[pG[0].b])
                        S.op("vector", CP(uT.t[:, g, 0:16], pG[0].t[:, :16]), reads=[pG[0].b], writes=[uT.bs[g]])
                if not sample:
                    W = 16 + UT
                    for g in range(4):
                        w = (2, 4, 8, 16)[g]
                        cur = uT.t[:, g, :]
                        cur_b = uT.bs[g]
                        sh = 1
                        k = 0
                        while sh < w:
                            dst = pl[k % 2]
                            S.op("vector", TT(dst.t[:, sh:W], cur[:, sh:W], cur[:, 0:W - sh], ALU.add), reads=[cur_b], writes=[dst.b])
                            S.op("vector", CP(dst.t[:, 0:sh], cur[:, 0:sh]), reads=[cur_b], writes=[dst.b])
                            cur = dst.t[:, :]
                            cur_b = dst.b
                            sh *= 2
                            k += 1
                        if J == 0:
                            S.op("vector", TT(cur[:, 16:32], cur[:, 16:32], corr.t[:, g * 16:(g + 1) * 16], ALU.mult),
                                 reads=[cur_b, corr.b], writes=[cur_b])
                        S.op("vector", STT(dT.t[:, g, :], cur[:, 16:W], 1.0 / w, uT.t[:, g, 16:W], ALU.mult, ALU.subtract),
                             reads=[cur_b, uT.bs[g]], writes=[dT.bs[g]])
                    if J == 3:
                        S.op("tensor", TRG([(pG[1].t[:15, g * 128:(g + 1) * 128], uT.t[:, g, UT + 1:UT + 16], identf.t[:]) for g in range(4)]),
                             reads=list(uT.bs) + [identf.b], writes=[pG[1].b])
                        S.op("vector", CP(uo.t[:], pG[1].t[:15, :]), reads=[pG[1].b], writes=[uo.b])
                        S.op("sync", DMA(pool_o, uo.t[:]), reads=[uo.b], dma=True)
                else:
                    S.op("tensor", MMG([(pG[1].t[:, g * NSEQ + s:g * NSEQ + s + 1], hist.t[:, s, g * 128:(g + 1) * 128], selw.t[:, g:g + 1], True, True)
                                        for g in range(4) for s in range(NSEQ)]), reads=[hist.b, selw.b], writes=[pG[1].b])
                    for g in range(4):
                        w = (2, 4, 8, 16)[g]
                        S.op("vector", STT(dT.t[:, g, :NSEQ], uT.t[:, g, 16:16 + NSEQ], 1.0 / w - 1.0, pG[1].t[:, g * NSEQ:(g + 1) * NSEQ],
                                           ALU.mult, ALU.add), reads=[uT.bs[g], pG[1].b], writes=[dT.bs[g]])
                    S.op("sync", DMA(pools_o[:, 0:14, :], state_d[:, 1:15, :]), dma=True)
                    S.op("tensor", TRG([(pG[0].t[:NSEQ, g * 128:(g + 1) * 128], uT.t[:, g, 16:16 + NSEQ], identf.t[:]) for g in range(4)]),
                         reads=list(uT.bs) + [identf.b], writes=[pG[0].b])
                    S.op("vector", CP(u_tok.t[:], pG[0].t[:NSEQ, :]), reads=[pG[0].b], writes=[u_tok.b])
                    S.op("sync", DMA(pools_o[:, 14, :], u_tok.t[:]), reads=[u_tok.b], dma=True)
                for g in range(4):
                    pb = pA[g % 2]
                    S.op("tensor", MMG([(pb.t[:, :T], poolw.t[:, g, :], dT.t[:, g, :T], True, True)]), reads=[poolw.b, dT.bs[g]], writes=[pb.b])
                    S.op("scalar", ACTF(bT.t[:, g, :T], pb.t[:, :T], AF.Copy, scale=pscale.t[:, g:g + 1]),
                         reads=[pb.b, pscale.b], writes=[bT.bs[g]])

                if not sample:
                    slots = list(range(J + 1)) + list(range(4, 4 + 3 * (J + 1)))
                    accs = [pA[0], pA[1], pG[0], pG[1]]
                    scs = [pS[0], pS[1], pacc]
                    nsl = len(slots)
                    ldi = [0]

                    def load_kv(hg, sl):
                        kbuf = kb[ldi[0] % 3]
                        vbuf = vb[ldi[0] % 3]
                        ldi[0] += 1
                        S.op("sync", DMA(kbuf.t[0:64, :, :], kT_s.rearrange("(h d) k -> d h k", d=HD)[:, hg * 4:hg * 4 + 4, sl * UT:(sl + 1) * UT]),
                             reads=b_kT[sl], writes=[kbuf.bs[0]], dma=True)
                        S.op("sync", DMA(kbuf.t[64:96, :, :], ind_bf[sl * 32:(sl + 1) * 32, :].rearrange("r (a k) -> r a k", a=4)),
                             reads=[b_indbf], writes=[kbuf.bs[1]], dma=True)
                        S.op("sync", DMA(vbuf.t[:, :, :], v_s[sl * UT:(sl + 1) * UT, hg * 512:(hg + 1) * 512].rearrange("(t p) c -> p t c", p=128)),
                             reads=b_v[sl], writes=[vbuf.b], dma=True)
                        return kbuf, vbuf

                    for hg in range(2):
                        steps = []
                        bufs = {}
                        order = [(li, sl) for li, sl in enumerate(slots)]
                        bufs[0] = load_kv(hg, order[0][1])
                        for li, sl in order:
                            for tau in range(4):
                                for hh in range(4):
                                    steps.append((li, sl, tau, hh))
                        n = len(steps)

                        def emit_score(idx):
                            li, sl, tau, hh = steps[idx]
                            if tau == 0 and hh == 0 and li + 1 < nsl:
                                bufs[li + 1] = load_kv(hg, order[li + 1][1])
                            kbuf, vbuf = bufs[li]
                            h = hg * 4 + hh
                            diag = (sl == J)
                            psc = scs[idx % 3]
                            lst = [(psc.t[:, :], kbuf.t[0:96, hh, tau * 128:(tau + 1) * 128], qaug.t[0:96, h, :], True, not diag)]
                            rd = [kbuf.bs[0], kbuf.bs[1], qaug.bs[h]]
                            if diag:
                                lst.append((psc.t[:, :], identb.t[:], cm.t[:, tau * UT:(tau + 1) * UT], False, True))
                                rd += [identb.b, cm.b]
                            S.op("tensor", MMG(lst), reads=rd, writes=[psc.b])
                            pt_ = pT[idx % 3]
                            S.op("scalar", ACTF(pt_.t[:], psc.t[:, :], AF.Exp), reads=[psc.b], writes=[pt_.b])

                        def emit_pv(idx):
                            li, sl, tau, hh = steps[idx]
                            kbuf, vbuf = bufs[li]
                            pt_ = pT[idx % 3]
                            S.op("tensor", MMG([(accs[hh].t[:, :], vbuf.t[:, tau, hh * 128:(hh + 1) * 128], pt_.t[:],
                                                 li == 0 and tau == 0, li == nsl - 1 and tau == 3)]),
                                 reads=[vbuf.b, pt_.b], writes=[accs[hh].b])

                        LOOK = 2
                        for idx in range(n + LOOK):
                            if idx < n:
                                emit_score(idx)
                            if idx >= LOOK:
                                emit_pv(idx - LOOK)
                        for hh in range(4):
                            h = hg * 4 + hh
                            m, e_ = h // 2, h % 2
                            S.op("vector", RCP(rden.t[:], accs[hh].t[64:128, :]), reads=[accs[hh].b], writes=[rden.b])
                            S.op("vector", TT(aT.t[64 * e_:64 * e_ + 64, m, :], accs[hh].t[0:64, :], rden.t[:], ALU.mult),
                                 reads=[accs[hh].b, rden.b], writes=[aT_bs[m]])

                pi_ao = pi
                wao, wao_b = pn.get(pi_ao, oldest=pi_ao)
                wpo, wpo_b = pn.get(pi_ao + 1, oldest=pi_ao)
                for m in range(KC):
                    half = m // 4
                    gav, gab = pn.get(pi_ao + 2 + 2 * half, oldest=pi_ao)
                    gbv, gbb = pn.get(pi_ao + 3 + 2 * half, oldest=pi_ao)
                    pa_, pg_ = pA[0], pG[0]
                    S.op("tensor", MMG([(pa_.t[:, :T], wao[:, kc, m * 128:(m + 1) * 128], aT.t[:, kc, :T], kc == 0, kc == 3) for kc in range(4)]),
                         reads=[wao_b] + list(aT_bs), writes=[pa_.b])
                    S.op("tensor", MMG([(pg_.t[:, :T], gav[:, kc, (m % 4) * 128:(m % 4 + 1) * 128], xT.t[:, kc, :T], kc == 0, kc == KC - 1)
                                        for kc in range(KC)]), reads=[gab] + list(xT.bs), writes=[pg_.b])
                    S.op("scalar", ACTF(tmpf[0].t[:, :T], pg_.t[:, :T], AF.Sigmoid), reads=[pg_.b], writes=[tmpf[0].b])
                    S.op("vector", TT(tmpf[1].t[:, :T], pa_.t[:, :T], tmpf[0].t[:, :T], ALU.mult), reads=[pa_.b, tmpf[0].b], writes=[tmpf[1].b])
                    pb_, ph_ = pA[1], pG[1]
                    S.op("tensor", MMG([(pb_.t[:, :T], wpo[:, kc, m * 128:(m + 1) * 128], bT.t[:, kc, :T], kc == 0, kc == 3) for kc in range(4)]),
                         reads=[wpo_b] + list(bT.bs), writes=[pb_.b])
                    S.op("tensor", MMG([(ph_.t[:, :T], gbv[:, kc, (m % 4) * 128:(m % 4 + 1) * 128], xT.t[:, kc, :T], kc == 0, kc == KC - 1)
                                        for kc in range(KC)]), reads=[gbb] + list(xT.bs), writes=[ph_.b])
                    S.op("scalar", ACTF(tmpf[2].t[:, :T], ph_.t[:, :T], AF.Sigmoid), reads=[ph_.b], writes=[tmpf[2].b])
                    S.op("vector", TT(tmpf[3].t[:, :T], pb_.t[:, :T], tmpf[2].t[:, :T], ALU.mult), reads=[pb_.b, tmpf[2].b], writes=[tmpf[3].b])
                    S.op("vector", TT(mT.t[:, m, :T], tmpf[1].t[:, :T], tmpf[3].t[:, :T], ALU.add),
                         reads=[tmpf[1].b, tmpf[3].b], writes=[mT.bs[m]])
                pi = pi_ao + 6
                for n in range(2):
                    wv, wb = pn.get(pi); pi += 1
                    for tt in range(ntt):
                        po = pS[tt % 2]
                        S.op("tensor", MMG([(po.t[:P, :], mT.t[:, kc, tt * P:(tt + 1) * P], wv[:, kc, :], kc == 0, kc == KC - 1) for kc in range(KC)]),
                             reads=[wb] + list(mT.bs), writes=[po.b])
                        S.op("vector", TT(xres.t[:P, tt, n * 512:(n + 1) * 512], po.t[:P, :], xres.t[:P, tt, n * 512:(n + 1) * 512], ALU.add),
                             reads=[po.b, xres.bs[tt]], writes=[xres.bs[tt]])
                S.barrier()
                S.run_block()

            with ExitStack() as s2:
                sb2, ps2 = mk(s2)
                hT = sb2([128, 32, UT], BF16, nb=32, name="hT")
                pt_tok = sb2([128, 4, 256], F32, name="pt_tok")
                pt_bf = sb2([128, 4, 256], BF16, name="pt_bf")
                ppT = sb2([128, 2, UT], BF16, nb=2, name="ppT")
                yt = [sb2([128, 512], F32, name="yt") for _ in range(2)]
                _pb = [ps2.bank(), ps2.bank()]
                ptr = [_pb[0][1], _pb[1][1]]
                pH = [ps2([128, UT], F32, name="pH") for _ in range(2)]
                pD = [ps2([128, 512], F32, name="pD") for _ in range(4)]

                psrc = ps_d if sample else p_own
                for tt in range(ntt):
                    r0 = 0 if sample else J * UT + tt * 128
                    S.op("sync", DMA(pt_tok.t[:P, tt, :], psrc[r0:r0 + P, :]), writes=[pt_tok.b], dma=True)
                ln_transpose(xres, P, ntt, gmlp, xT, ptr)
                specs = [("up", 0, KC, c * 512, 512) for c in range(8)]
                specs += [("down", q * 8, 8, n * 512, 512) for n in range(2) for q in range(4)]
                specs += [("pg", 0, KC, 0, 512), ("pg", 0, KC, 512, 512), ("pp", 0, 2, 0, 1024)]
                pn = Panels(specs)
                pi = 0
                for c in range(8):
                    wv, wb = pn.get(pi); pi += 1
                    for mm in range(4):
                        ph = pH[mm % 2]
                        S.op("tensor", MMG([(ph.t[:, :T], wv[:, kc, mm * 128:(mm + 1) * 128], xT.t[:, kc, :T], kc == 0, kc == KC - 1)
                                            for kc in range(KC)]), reads=[wb] + list(xT.bs), writes=[ph.b])
                        tf = tmpf[mm % 2]
                        S.op("scalar", ACTF(tf.t[:, :T], ph.t[:, :T], AF.Relu), reads=[ph.b], writes=[tf.b])
                        S.op("vector", TT(hT.t[:, c * 4 + mm, :T], tf.t[:, :T], tf.t[:, :T], ALU.mult), reads=[tf.b], writes=[hT.bs[c * 4 + mm]])
                for n in range(2):
                    for q in range(4):
                        wv, wb = pn.get(pi); pi += 1
                        for tt in range(ntt):
                            S.op("tensor", MMG([(pD[tt].t[:P, :], hT.t[:, q * 8 + kc, tt * P:(tt + 1) * P], wv[:, kc, :],
                                                 q == 0 and kc == 0, q == 3 and kc == 7) for kc in range(8)]),
                                 reads=[wb] + hT.bs[q * 8:(q + 1) * 8], writes=[pD[tt].b])
                    for tt in range(ntt):
                        S.op("vector", TT(xres.t[:P, tt, n * 512:(n + 1) * 512], pD[tt].t[:P, :], xres.t[:P, tt, n * 512:(n + 1) * 512], ALU.add),
                             reads=[pD[tt].b, xres.bs[tt]], writes=[xres.bs[tt]])
                ln_transpose(xres, P, ntt, gple, xT, ptr)
                for tt in range(ntt):
                    S.op("vector", CP(pt_bf.t[:P, tt, :], pt_tok.t[:P, tt, :]), reads=[pt_tok.b], writes=[pt_bf.b])
                for kc in range(2):
                    S.op("tensor", TRG([(ptr[kc].t[:, tt * P:(tt + 1) * P], pt_bf.t[:P, tt, kc * 128:(kc + 1) * 128], identb.t[:P, :P])
                                        for tt in range(ntt)]), reads=[pt_bf.b, identb.b], writes=[ptr[kc].b])
                    S.op("vector", CP(ppT.t[:, kc, :T], ptr[kc].t[:, :T]), reads=[ptr[kc].b], writes=[ppT.bs[kc]])
                wg = [pn.get(pi, oldest=pi), pn.get(pi + 1, oldest=pi)]
                wpp, wpp_b = pn.get(pi + 2, oldest=pi)
                pi += 3
                k = 0
                for tt in range(ntt):
                    for n in range(2):
                        pg_, pp_ = pD[0 + (k % 2) * 2], pD[1 + (k % 2) * 2]
                        S.op("tensor", MMG([(pg_.t[:P, :], xT.t[:, kc, tt * P:(tt + 1) * P], wg[n][0][:, kc, :], kc == 0, kc == KC - 1)
                                            for kc in range(KC)]), reads=[wg[n][1]] + list(xT.bs), writes=[pg_.b])
                        S.op("tensor", MMG([(pp_.t[:P, :], ppT.t[:, kc, tt * P:(tt + 1) * P], wpp[:, kc, n * 512:(n + 1) * 512], kc == 0, kc == 1)
                                            for kc in range(2)]), reads=[wpp_b] + list(ppT.bs), writes=[pp_.b])
                        tf = tmpf[k % 2]
                        S.op("scalar", ACTF(tf.t[:P, :], pg_.t[:P, :], AF.Sigmoid), reads=[pg_.b], writes=[tf.b])
                        y = yt[k % 2]
                        S.op("vector", TT(y.t[:P, :], pp_.t[:P, :], tf.t[:P, :], ALU.mult), reads=[pp_.b, tf.b], writes=[y.b])
                        S.op("vector", TT(y.t[:P, :], y.t[:P, :], xres.t[:P, tt, n * 512:(n + 1) * 512], ALU.add),
                             reads=[y.b, xres.bs[tt]], writes=[y.b])
                        if sample:
                            dst = ys_o[:, n * 512:(n + 1) * 512]
                        else:
                            dst = y_own[J * UT + tt * 128: J * UT + (tt + 1) * 128, n * 512:(n + 1) * 512]
                        S.op("sync", DMA(dst, y.t[:P, :]), reads=[y.b], dma=True)
                        k += 1
                S.barrier()
                S.run_block()

        if _on("uS"):
            unit_pass(None)
        for J in range(4):
            if _on("u%d" % J):
                unit_pass(J)
        S.barrier()
        S.run_block()
    return nc


_NC_CACHE = {}


def _core_consts(r):
    own = [4 * J + r for J in range(4)]
    nonown = [u for u in range(16) if u % 4 != r]
    slot_units = own + nonown
    gmask = np.zeros((4, 2, 32), np.float32)
    pastind = np.zeros((4, 2, 32), np.float32)
    ownind = np.zeros((4, 2, 32), np.float32)
    for J in range(4):
        for bq in range(2):
            ob_q = 2 * own[J] + bq
            for sl in range(16):
                for be in range(2):
                    ob = 2 * slot_units[sl] + be
                    rho = 2 * sl + be
                    if ob < ob_q:
                        pastind[J, bq, rho] = 1.0
                    else:
                        gmask[J, bq, rho] = -1e30
                    if ob == ob_q:
                        ownind[J, bq, rho] = 1.0
    corr = np.ones((4, 16), np.float32)
    if r == 0:
        for g, w in enumerate((2, 4, 8, 16)):
            for t in range(16):
                corr[g, t] = w / min(w, t + 1)
    rep = lambda a: np.ascontiguousarray(np.broadcast_to(a.reshape(1, -1), (128, a.size))).astype(np.float32)
    return slot_units, rep(gmask), rep(pastind), rep(ownind), rep(corr)


def _static_consts():
    k = np.arange(128)[:, None, None]
    tau = np.arange(4)[None, :, None]
    q = np.arange(UT)[None, None, :]
    cm = np.where(128 * tau + k <= q, 0.0, NEG).astype(np.float32).reshape(128, 4 * UT)
    ind = np.zeros((NSLOT, 32, UT), np.float32)
    for sl in range(NSLOT):
        ind[sl, 2 * sl, 0:256] = 1.0
        ind[sl, 2 * sl + 1, 256:512] = 1.0
    ind = np.ascontiguousarray(np.broadcast_to(ind.reshape(NSLOT * 32, 1, UT), (NSLOT * 32, 4, UT))).reshape(NSLOT * 32, 4 * UT)
    pairm = np.zeros((128, 64), np.float32)
    pairm[np.arange(128), np.arange(128) // 2] = 1.0
    tokoff = (np.arange(128)[:, None] * 8 + (np.arange(48)[None, :] // 6)).astype(np.float32)
    delta = np.zeros((8, 48), np.float32)
    for h in range(8):
        delta[h, h * 6:(h + 1) * 6] = 1.0
    pmmask = np.zeros((4, 32), np.float32)
    for s in range(4):
        pmmask[s, s * 8:(s + 1) * 8] = 1.0
    selw = np.zeros((15, 4), np.float32)
    for g, w in enumerate((2, 4, 8, 16)):
        selw[16 - w:, g] = 1.0 / w
    cidx = np.ascontiguousarray(np.broadcast_to(np.arange(32, dtype=np.float32)[None, :], (128, 32)))
    return dict(cm=cm, ind=ind, pairm=pairm, tokoff=tokoff, delta=delta, pmmask=pmmask, selw=selw, cidx=cidx)


def kernel(x_prompt, x_sample, cache_k, cache_v, state_pool, page_table, p_prompt, p_sample, ln_mix, w_in,
           q_norm, k_norm, pool_w, pool_scale, w_attn_out, w_pool_out, w_out, ln_mlp, w_up, w_down, ln_ple,
           w_ple_gate, w_ple_proj):
    f = lambda a: np.ascontiguousarray(np.asarray(a, dtype=np.float32))
    x_prompt = f(x_prompt); x_sample = f(x_sample); p_prompt = f(p_prompt); p_sample = f(p_sample)
    ck = f(cache_k).reshape(-1, HD)
    cv = f(cache_v).reshape(-1, HD)
    page_table = np.ascontiguousarray(np.asarray(page_table, dtype=np.int32))
    state_pool = f(state_pool)
    if "nc" not in _NC_CACHE:
        _NC_CACHE["nc"] = build_nc()
    nc = _NC_CACHE["nc"]
    st = _static_consts()
    shared = dict(
        cache_k=ck, cache_v=cv, ln_mix=f(ln_mix)[0], w_in=f(w_in)[0], q_norm=f(q_norm)[0], k_norm=f(k_norm)[0],
        pool_w=f(pool_w)[0], pool_scale=f(pool_scale)[0], w_attn_out=f(w_attn_out)[0], w_pool_out=f(w_pool_out)[0],
        w_out=f(w_out)[0], ln_mlp=f(ln_mlp)[0], w_up=f(w_up)[0], w_down=f(w_down)[0], ln_ple=f(ln_ple)[0],
        w_ple_gate=f(w_ple_gate)[0], w_ple_proj=f(w_ple_proj)[0], **st)
    in_maps = []
    layouts = {}
    cores = list(CFG.get("cores", range(8)))
    for c in cores:
        b, r = c // 4, c % 4
        slot_units, gmask, pastind, ownind, corr = _core_consts(r)
        xb = np.concatenate([x_prompt[b, u * UT:(u + 1) * UT] for u in slot_units], axis=0)
        xhalo = np.zeros((64, D), np.float32)
        for J in range(4):
            u = slot_units[J]
            if u > 0:
                xhalo[J * 16:(J + 1) * 16] = x_prompt[b, u * UT - 16:u * UT]
        p_own = np.concatenate([p_prompt[0, b, slot_units[J] * UT:(slot_units[J] + 1) * UT] for J in range(4)], axis=0)
        pt = page_table[4 * c:4 * c + 4]
        m = dict(shared)
        m.update(xb=xb, xhalo=xhalo, p_own=np.ascontiguousarray(p_own), xs=np.ascontiguousarray(x_sample[4 * c:4 * c + 4, 0]),
                 ps=np.ascontiguousarray(p_sample[0, 4 * c:4 * c + 4, 0]), ptT=np.ascontiguousarray(pt.T), pt=np.ascontiguousarray(pt),
                 state=np.ascontiguousarray(state_pool[0, 4 * c:4 * c + 4]), gmask=gmask, pastind=pastind, ownind=ownind, corr=corr)
        in_maps.append(m)
        layouts[c] = slot_units
    res = run_bass_kernel_spmd(nc, in_maps, core_ids=list(range(len(cores))))
    outs = res.results
    y_prompt = np.zeros((2, SEQ, D), np.float32)
    k_prompt = np.zeros((1, 2, SEQ, H, HD), np.float32)
    v_prompt = np.zeros((1, 2, SEQ, H, HD), np.float32)
    pool_prompt = np.zeros((1, 2, 15, 512), np.float32)
    y_sample = np.zeros((32, 1, D), np.float32)
    k_sample = np.zeros((1, 32, 1, H, HD), np.float32)
    v_sample = np.zeros((1, 32, 1, H, HD), np.float32)
    pool_sample = np.zeros((1, 32, 15, 512), np.float32)
    for ci, c in enumerate(cores):
        b, r = c // 4, c % 4
        o = outs[ci]
        for J in range(4):
            u = layouts[c][J]
            y_prompt[b, u * UT:(u + 1) * UT] = o["y_own"][J * UT:(J + 1) * UT]
            k_prompt[0, b, u * UT:(u + 1) * UT] = o["k_own"][J * UT:(J + 1) * UT].reshape(UT, H, HD)
            v_prompt[0, b, u * UT:(u + 1) * UT] = o["v_own"][J * UT:(J + 1) * UT].reshape(UT, H, HD)
        if r == 3:
            pool_prompt[0, b] = o["pool_o"]
        y_sample[4 * c:4 * c + 4, 0] = o["ys_o"]
        k_sample[0, 4 * c:4 * c + 4, 0] = o["ks_o"].reshape(4, H, HD)
        v_sample[0, 4 * c:4 * c + 4, 0] = o["vs_o"].reshape(4, H, HD)
        pool_sample[0, 4 * c:4 * c + 4] = o["pools_o"]
    return (y_prompt, y_sample, k_prompt, v_prompt, pool_prompt, k_sample, v_sample, pool_sample)
```
